# Optimizing a Trainium2 kernel written in Bass

```python
import jax, jax.numpy as jnp
from jax import lax
import numpy as np

D_MODEL = 1024
BATCH = 16
SEQ = 256
DEPTH = 4
DEC_BATCH = 8
DEC_SEQ = 1024
PAST_LEN = 512

GRID_W = 64
N_MIXERS = 2
N_RWKV_LAYERS = (DEPTH + 1) // 2
N_ATTN_LAYERS = DEPTH // 2
ALPHA = (2 * DEPTH) ** 0.25
BETA = (8 * DEPTH) ** -0.25
LN_EPS = 1e-5
RWKV_HEAD = 64
RWKV_HEADS = D_MODEL // RWKV_HEAD
DECAY_LORA = 64
ICLR_LORA = 64
GATE_LORA = 128
GN_EPS = RWKV_HEAD * 1e-5
HEAD_DIM = 64
N_HEADS = D_MODEL // HEAD_DIM
KV_HEADS = 4
GROUP = N_HEADS // KV_HEADS
Q_WIDTH = N_HEADS * HEAD_DIM
KV_WIDTH = KV_HEADS * HEAD_DIM
Q_BLOCK = 128
ROPE_THETA = 10000.0
ROPE_FREQS = HEAD_DIM // 4
ATTN_SCALE = HEAD_DIM ** -0.5
RMS_EPS = 1e-6
PEER_HEADS = 8
N_KEYS = 128
N_EXPERTS = N_KEYS * N_KEYS
PEER_QUERY = 256
PEER_HALF = PEER_QUERY // 2
PEER_TOPK = 16
PEER_BLOCK = 128

kernel_name = 'hybrid_rwkv7_gqa_peer_diffusion_step'


def residual_post_norm(x, branch, g, b):
    z = ALPHA * x.astype(jnp.float32) + branch.astype(jnp.float32)
    mu = jnp.mean(z, -1, keepdims=True)
    var = jnp.mean(jnp.square(z - mu), -1, keepdims=True)
    return ((z - mu) * lax.rsqrt(var + LN_EPS) * g + b).astype(x.dtype)


def rms_norm(x, g):
    xf = x.astype(jnp.float32)
    return (xf * lax.rsqrt(jnp.mean(xf * xf, -1, keepdims=True) + RMS_EPS) * g).astype(x.dtype)


def adaln_params(cond, w, b):
    return (jax.nn.silu(cond) @ w + b).reshape(cond.shape[0], 6, D_MODEL)


def modulate(x, shift, scale):
    return x * (1 + scale[:, None]) + shift[:, None]


def centred_shift(x):
    prev = jnp.pad(x[:, :-1], ((0, 0), (1, 0), (0, 0)))
    nxt = jnp.pad(x[:, 1:], ((0, 0), (0, 1), (0, 0)))
    return 0.5 * (prev + nxt)


def wkv_scan(r, w, k, v, kk, a, s0, reverse):
    def step(S, inp):
        r_t, w_t, k_t, v_t, kk_t, a_t = inp
        s_kk = jnp.einsum('bhij,bhj->bhi', S, kk_t)
        S = (S * w_t[:, :, None, :] - s_kk[..., None] * (kk_t * a_t)[:, :, None, :]
             + v_t[..., None] * k_t[:, :, None, :])
        return S, jnp.einsum('bhij,bhj->bhi', S, r_t)
    seq = tuple(jnp.swapaxes(t, 0, 1) for t in (r, w, k, v, kk, a))
    s_fin, y = lax.scan(step, s0, seq, reverse=reverse)
    return jnp.swapaxes(y, 0, 1), s_fin


def rwkv_time_mix(h, s0, mu, wrkv, wo, w0, w1, w2, a0, a1, a2, g1, g2, k_k, k_a, r_k, lnx_g, lnx_b):
    B, T, D = h.shape
    f32 = jnp.float32
    heads = lambda t: t.reshape(t.shape[:-1] + (RWKV_HEADS, RWKV_HEAD))
    xx = centred_shift(h) - h
    xr, xw, xk, xv, xa, xg = (h + xx * mu[m] for m in range(6))
    r = xr @ wrkv[0]
    k = xk @ wrkv[1]
    v = xv @ wrkv[2]
    wl = w0[:, None, None, :] + jnp.einsum('zbtl,zld->zbtd', jnp.tanh(jnp.einsum('btd,zdl->zbtl', xw, w1)), w2)
    decay = jnp.exp(-jnp.exp(-jax.nn.softplus(-wl.astype(f32)) - 0.5))
    a = jax.nn.sigmoid((a0[:, None, None, :] + jnp.einsum('zbtl,zld->zbtd', jnp.einsum('btd,zdl->zbtl', xa, a1), a2)).astype(f32))
    g = jax.nn.sigmoid(xg @ g1) @ g2
    kk = heads((k * k_k).astype(f32))
    kk = kk / jnp.maximum(jnp.sqrt(jnp.sum(kk * kk, -1, keepdims=True)), 1e-12)
    kd = k.astype(f32)[None] * (1 + (a - 1) * k_a.astype(f32))
    rf, vf = heads(r.astype(f32)), heads(v.astype(f32))
    s0 = s0.astype(f32)
    y_f, s_f = wkv_scan(rf, heads(decay[0]), heads(kd[0]), vf, kk, heads(a[0]), s0[:, 0], False)
    y_b, s_b = wkv_scan(rf, heads(decay[1]), heads(kd[1]), vf, kk, heads(a[1]), s0[:, 1], True)
    y = y_f + y_b
    ym = jnp.mean(y, -1, keepdims=True)
    yv = jnp.mean(jnp.square(y - ym), -1, keepdims=True)
    yn = ((y - ym) * lax.rsqrt(yv + GN_EPS)).reshape(B, T, D) * lnx_g + lnx_b
    bonus = jnp.sum(rf[None] * heads(kd) * r_k, axis=(0, -1))[..., None] * vf
    out = ((yn + bonus.reshape(B, T, D)).astype(h.dtype) * g) @ wo
    return out, jnp.stack([s_f, s_b], axis=1)


def axial_rope_angles(n):
    rows = n // GRID_W
    row = jnp.repeat(jnp.arange(rows), GRID_W).astype(jnp.float32)
    col = (jnp.arange(rows * GRID_W) % GRID_W).astype(jnp.float32)
    freqs = ROPE_THETA ** (-jnp.arange(ROPE_FREQS, dtype=jnp.float32) / ROPE_FREQS)
    ang = jnp.stack([row[:, None] * freqs, col[:, None] * freqs], axis=1)
    return jnp.cos(ang), jnp.sin(ang)


def apply_axial_rope(x, cos, sin):
    xs = x.astype(jnp.float32).reshape(x.shape[:-1] + (2, 2, ROPE_FREQS))
    bshape = (x.shape[1],) + (1,) * (x.ndim - 3) + (2, ROPE_FREQS)
    cos, sin = cos.reshape(bshape), sin.reshape(bshape)
    x1, x2 = xs[..., 0, :], xs[..., 1, :]
    out = jnp.stack([x1 * cos - x2 * sin, x2 * cos + x1 * sin], axis=-2)
    return out.reshape(x.shape).astype(x.dtype)


def attn_project(h, wqkv, qn, kn):
    B, n, _ = h.shape
    qkv = h @ wqkv
    q = rms_norm(qkv[..., :Q_WIDTH].reshape(B, n, KV_HEADS, GROUP, HEAD_DIM), qn)
    k = rms_norm(qkv[..., Q_WIDTH:Q_WIDTH + KV_WIDTH].reshape(B, n, KV_HEADS, HEAD_DIM), kn)
    v = qkv[..., Q_WIDTH + KV_WIDTH:].reshape(B, n, KV_HEADS, HEAD_DIM)
    return q, k, v


def block_attention(q, k, v):
    B, T = q.shape[:2]
    qb = jnp.moveaxis(q.reshape((B, T // Q_BLOCK, Q_BLOCK) + q.shape[2:]), 1, 0)
    def one_block(qi):
        s = jnp.einsum('bqkgd,bskd->bkgqs', qi, k).astype(jnp.float32) * ATTN_SCALE
        p = jax.nn.softmax(s, axis=-1).astype(v.dtype)
        return jnp.einsum('bkgqs,bskd->bqkgd', p, v)
    o = lax.map(one_block, qb)
    return jnp.moveaxis(o, 0, 1).reshape(q.shape)


def attention_context(h, wqkv, wo, qn, kn):
    q, k, v = attn_project(h, wqkv, qn, kn)
    o = block_attention(q, k, v)
    return o.reshape(h.shape[0], h.shape[1], Q_WIDTH) @ wo, k, v


def attention_latent(h, ctx_k, ctx_v, wqkv, wo, qn, kn):
    q, k, v = attn_project(h, wqkv, qn, kn)
    cos, sin = axial_rope_angles(h.shape[1])
    q = apply_axial_rope(q, cos, sin)
    k = apply_axial_rope(k, cos, sin)
    keys = jnp.concatenate([ctx_k.astype(k.dtype), k], axis=1)
    vals = jnp.concatenate([ctx_v.astype(v.dtype), v], axis=1)
    o = block_attention(q, keys, vals)
    return o.reshape(h.shape[0], h.shape[1], Q_WIDTH) @ wo


def peer_ffn(h, wq, sub_keys, u, v):
    B, n, D = h.shape
    x = h.reshape(B * n, D)
    q = (x @ wq).reshape(B * n, PEER_HEADS, 2, PEER_HALF)
    s = jnp.einsum('thzd,zkd->thzk', q, sub_keys).astype(jnp.float32)
    sv, si = lax.top_k(s, PEER_TOPK)
    cand = (sv[:, :, 0, :, None] + sv[:, :, 1, None, :]).reshape(B * n, PEER_HEADS, PEER_TOPK * PEER_TOPK)
    cv, ci = lax.top_k(cand, PEER_TOPK)
    i1 = jnp.take_along_axis(si[:, :, 0], ci // PEER_TOPK, axis=-1)
    i2 = jnp.take_along_axis(si[:, :, 1], ci % PEER_TOPK, axis=-1)
    idx = (i1 * N_KEYS + i2).reshape(-1, PEER_BLOCK, PEER_HEADS * PEER_TOPK)
    gate = jax.nn.softmax(cv, axis=-1).astype(x.dtype).reshape(-1, PEER_BLOCK, PEER_HEADS * PEER_TOPK)
    xb = x.reshape(-1, PEER_BLOCK, D)
    def experts(args):
        xt, it, gt = args
        act = jax.nn.gelu(jnp.einsum('tkd,td->tk', u[it], xt), approximate=False)
        return jnp.einsum('tk,tkd->td', gt * act, v[it])
    out = lax.map(experts, (xb, idx, gate))
    return out.reshape(B, n, D)


def setup_inputs(seed: int = 0) -> dict:
    key = jax.random.key(seed)
    ks = iter(jax.random.split(key, 40))
    nrm = lambda shape, scale: jax.random.normal(next(ks), shape, jnp.float32) * scale
    D = D_MODEL
    NR, NA = N_RWKV_LAYERS, N_ATTN_LAYERS
    return {
        'x_prompt': nrm((BATCH, SEQ, D), 1.0),
        'x_sample': nrm((DEC_BATCH, DEC_SEQ, D), 1.0),
        'c': nrm((DEC_BATCH, D), 1.0),
        'state_rwkv': nrm((DEC_BATCH, NR, 2, RWKV_HEADS, RWKV_HEAD, RWKV_HEAD), 1.0),
        'cache_k': nrm((DEC_BATCH, NA, PAST_LEN, KV_HEADS, HEAD_DIM), 1.0),
        'cache_v': nrm((DEC_BATCH, NA, PAST_LEN, KV_HEADS, HEAD_DIM), 1.0),
        'c_ctx': nrm((D,), 1.0),
        'ada_w': nrm((DEPTH, D, 6 * D), 0.5 * D ** -0.5),
        'ada_b': nrm((DEPTH, 6 * D), 0.02),
        'ln_g': 1.0 + nrm((DEPTH, 2, D), 0.02),
        'ln_b': nrm((DEPTH, 2, D), 0.02),
        'rwkv_mu': jax.random.uniform(next(ks), (NR, 6, D), jnp.float32),
        'rwkv_wrkv': nrm((NR, 3, D, D), D ** -0.5),
        'rwkv_wo': nrm((NR, D, D), BETA * D ** -0.5),
        'rwkv_w0': jax.random.uniform(next(ks), (NR, 2, D), jnp.float32, -6.0, -1.0),
        'rwkv_w1': nrm((NR, 2, D, DECAY_LORA), 0.1 * D ** -0.5),
        'rwkv_w2': nrm((NR, 2, DECAY_LORA, D), 0.1 * DECAY_LORA ** -0.5),
        'rwkv_a0': nrm((NR, 2, D), 0.1),
        'rwkv_a1': nrm((NR, 2, D, ICLR_LORA), 0.1 * D ** -0.5),
        'rwkv_a2': nrm((NR, 2, ICLR_LORA, D), 0.1 * ICLR_LORA ** -0.5),
        'rwkv_g1': nrm((NR, D, GATE_LORA), D ** -0.5),
        'rwkv_g2': nrm((NR, GATE_LORA, D), GATE_LORA ** -0.5),
        'rwkv_kk': 0.85 + nrm((NR, D), 0.05),
        'rwkv_ka': 1.0 + nrm((NR, D), 0.05),
        'rwkv_rk': nrm((NR, RWKV_HEADS, RWKV_HEAD), 0.1),
        'rwkv_lnx_g': 1.0 + nrm((NR, D), 0.02),
        'rwkv_lnx_b': nrm((NR, D), 0.02),
        'attn_wqkv': nrm((NA, D, Q_WIDTH + 2 * KV_WIDTH), D ** -0.5),
        'attn_wo': nrm((NA, Q_WIDTH, D), BETA * Q_WIDTH ** -0.5),
        'attn_qn': 1.0 + nrm((NA, HEAD_DIM), 0.02),
        'attn_kn': 1.0 + nrm((NA, HEAD_DIM), 0.02),
        'peer_wq': nrm((DEPTH, D, PEER_HEADS * PEER_QUERY), D ** -0.5),
        'peer_keys': nrm((DEPTH, 2, N_KEYS, PEER_HALF), PEER_HALF ** -0.5),
        'peer_u': nrm((DEPTH, N_EXPERTS, D), D ** -0.5),
        'peer_v': nrm((DEPTH, N_EXPERTS, D), BETA),
    }


def reference(x_prompt, x_sample, c, state_rwkv, cache_k, cache_v, c_ctx, ada_w, ada_b, ln_g, ln_b,
              rwkv_mu, rwkv_wrkv, rwkv_wo, rwkv_w0, rwkv_w1, rwkv_w2, rwkv_a0, rwkv_a1, rwkv_a2,
              rwkv_g1, rwkv_g2, rwkv_kk, rwkv_ka, rwkv_rk, rwkv_lnx_g, rwkv_lnx_b,
              attn_wqkv, attn_wo, attn_qn, attn_kn, peer_wq, peer_keys, peer_u, peer_v):
    xp, xs = x_prompt, x_sample
    states, keys_out, vals_out = [], [], []
    for i in range(DEPTH):
        j = i // N_MIXERS
        mod_p = adaln_params(c_ctx[None], ada_w[i], ada_b[i])
        mod_s = adaln_params(c, ada_w[i], ada_b[i])
        hp = modulate(xp, mod_p[:, 0], mod_p[:, 1])
        hs = modulate(xs, mod_s[:, 0], mod_s[:, 1])
        if i % N_MIXERS == 0:
            rw = (rwkv_mu[j], rwkv_wrkv[j], rwkv_wo[j], rwkv_w0[j], rwkv_w1[j], rwkv_w2[j],
                  rwkv_a0[j], rwkv_a1[j], rwkv_a2[j], rwkv_g1[j], rwkv_g2[j], rwkv_kk[j], rwkv_ka[j],
                  rwkv_rk[j], rwkv_lnx_g[j], rwkv_lnx_b[j])
            s_zero = jnp.zeros((xp.shape[0], 2, RWKV_HEADS, RWKV_HEAD, RWKV_HEAD), jnp.float32)
            op, s_ctx = rwkv_time_mix(hp, s_zero, *rw)
            os_, _ = rwkv_time_mix(hs, state_rwkv[:, j], *rw)
            states.append(s_ctx)
        else:
            op, kp, vp = attention_context(hp, attn_wqkv[j], attn_wo[j], attn_qn[j], attn_kn[j])
            os_ = attention_latent(hs, cache_k[:, j], cache_v[:, j], attn_wqkv[j], attn_wo[j], attn_qn[j], attn_kn[j])
            keys_out.append(kp)
            vals_out.append(vp)
        xp = residual_post_norm(xp, mod_p[:, 2][:, None] * op, ln_g[i, 0], ln_b[i, 0])
        xs = residual_post_norm(xs, mod_s[:, 2][:, None] * os_, ln_g[i, 0], ln_b[i, 0])
        hp = modulate(xp, mod_p[:, 3], mod_p[:, 4])
        hs = modulate(xs, mod_s[:, 3], mod_s[:, 4])
        fp = peer_ffn(hp, peer_wq[i], peer_keys[i], peer_u[i], peer_v[i])
        fs = peer_ffn(hs, peer_wq[i], peer_keys[i], peer_u[i], peer_v[i])
        xp = residual_post_norm(xp, mod_p[:, 5][:, None] * fp, ln_g[i, 1], ln_b[i, 1])
        xs = residual_post_norm(xs, mod_s[:, 5][:, None] * fs, ln_g[i, 1], ln_b[i, 1])
    new_state_rwkv = jnp.stack(states, axis=1)
    new_cache_k = jnp.stack(keys_out, axis=1)
    new_cache_v = jnp.stack(vals_out, axis=1)
    return (xp, xs, new_state_rwkv, new_cache_k, new_cache_v)
```

```python
import contextlib
import numpy as np
import concourse.bass as bass
import concourse.mybir as mybir
from concourse.bass_utils import run_bass_kernel_spmd

F32 = mybir.dt.float32
BF16 = mybir.dt.bfloat16
I32 = mybir.dt.int32
U32 = mybir.dt.uint32
AF = mybir.ActivationFunctionType
ALU = mybir.AluOpType
AX = mybir.AxisListType

D = 1024
NT = 12
NTOK = 1536
DEPTH = 4
ALPHA = float((2 * DEPTH) ** 0.25)
LN_EPS = 1e-5
GN_EPS = 64 * 1e-5
RMS_EPS = 1e-6
ATTN_SCALE = 0.125
NEG_EXP_HALF = -float(np.exp(-0.5))
SEQS = [(0, [0, 1]), (1, [2, 3]), (2, [4, 5, 6, 7, 8, 9, 10, 11])]
HPAD = 1540

C_ID, C_IOTA, C_ONES = 0, 128, 144
CR0 = 272
C_TRIF, C_TRIB, C_M4F, C_M4B, C_MU, C_ML = 272, 400, 528, 1040, 1552, 2064
NCST = 2576


def padcol(tt):
    return tt * 128 + 1 + (1 if tt >= 2 else 0) + (1 if tt >= 4 else 0)


class KB:
    def __init__(self, nc, stack, n_dma_sems=4):
        self.nc = nc
        self.st = stack
        self.eng = {'pe': nc.tensor, 'dve': nc.vector, 'act': nc.scalar, 'pool': nc.gpsimd, 'sp': nc.sync}
        self.sem = {}
        self.cnt = {}
        for e in self.eng:
            self.sem[e] = stack.enter_context(nc.semaphore("s_" + e))
            self.cnt[e] = 0
        self.n_dma_sems = n_dma_sems
        self.dsem = {}
        self.dcnt = {}
        self.drr = {}
        self.nds = {'sp': n_dma_sems, 'act': 1, 'pool': 8}
        for q in ('sp', 'act', 'pool'):
            self.drr[q] = 0
            for k in range(self.nds[q]):
                self.dsem[(q, k)] = stack.enter_context(nc.semaphore(f"d_{q}{k}"))
                self.dcnt[(q, k)] = 0
        self.bgsem = stack.enter_context(nc.semaphore("bgsem"))
        self.bgcnt = 0
        self.bs_arrive = stack.enter_context(nc.semaphore("bs_arrive"))
        self.bs_go = stack.enter_context(nc.semaphore("bs_go"))
        self.nbar = 0
        self.seen = {e: {} for e in self.eng}
        self.lastw = {}
        self.readers = {}
        self.ninst = 0
        self.uid = 0
        self.rec = {e: [] for e in self.eng}

    def sb(self, name, shape, dt=F32):
        self.uid += 1
        return self.st.enter_context(self.nc.sbuf_tensor(f"{name}_{self.uid}", list(shape), dt))

    def ps(self, name, shape, dt=F32):
        return self.st.enter_context(self.nc.psum_tensor(name, list(shape), dt))

    EXCL = frozenset(f'ps{i}' for i in range(8))

    def _deps(self, reads, writes, e=None):
        deps = []
        for k in reads:
            if k in self.lastw:
                deps.append(self.lastw[k])
            if k in self.EXCL:
                deps.extend((sk, v) for sk, v in self.readers.get(k, {}).items() if sk != e)
        for k in writes:
            if k in self.lastw:
                deps.append(self.lastw[k])
            deps.extend(self.readers.get(k, {}).items())
        return deps

    def _wait(self, e, deps):
        best = {}
        for (sk, v) in deps:
            if v > best.get(sk, 0):
                best[sk] = v
        for sk, v in best.items():
            if self.seen[e].get(sk, 0) >= v:
                continue
            sem = self.sem[sk] if isinstance(sk, str) else self.dsem[sk]
            self.eng[e].wait_ge(sem, v)
            self.rec[e].append(('w', sk, v))
            self.seen[e][sk] = v

    def _record(self, tok, reads, writes):
        for k in reads:
            d = self.readers.setdefault(k, {})
            if tok[1] > d.get(tok[0], 0):
                d[tok[0]] = tok[1]
        for k in writes:
            self.lastw[k] = tok
            self.readers[k] = {}

    def op(self, e, fn, reads=(), writes=()):
        self._wait(e, self._deps(reads, writes, e))
        ins = fn(self.eng[e])
        self.cnt[e] += 1
        ins.then_inc(self.sem[e], 1)
        self.rec[e].append(('i', e, 1))
        self._record((e, self.cnt[e]), reads, writes)
        self.ninst += 1
        return ins

    def _dma_fin(self, q, ins, reads, writes):
        k = self.drr[q]
        self.drr[q] = (k + 1) % self.nds[q]
        self.dcnt[(q, k)] += 16
        ins.then_inc(self.dsem[(q, k)], 16)
        self.rec[q].append(('i', (q, k), 16))
        self._record(((q, k), self.dcnt[(q, k)]), reads, writes)
        self.ninst += 1

    def dma(self, q, out, in_, reads=(), writes=(), **kw):
        self._wait(q, self._deps(reads, writes))
        ins = self.eng[q].dma_start(out=out, in_=in_, **kw)
        self._dma_fin(q, ins, reads, writes)
        return ins

    def gather(self, out, in_, idx_ap, reads=(), writes=()):
        q = 'pool'
        self._wait(q, self._deps(reads, writes))
        ins = self.nc.gpsimd.indirect_dma_start(
            out=out, out_offset=None, in_=in_,
            in_offset=bass.IndirectOffsetOnAxis(ap=idx_ap, axis=0))
        self._dma_fin(q, ins, reads, writes)
        return ins

    def bg_dma(self, out, in_):
        ins = self.eng['pool'].dma_start(out=out, in_=in_)
        self.bgcnt += 16
        ins.then_inc(self.bgsem, 16)
        self.ninst += 1

    def bg_wait(self, e):
        if self.bgcnt:
            self.eng[e].wait_ge(self.bgsem, self.bgcnt)

    def all_tokens(self):
        deps = [(e, c) for e, c in self.cnt.items() if c > 0]
        deps += [(k, c) for k, c in self.dcnt.items() if c > 0]
        return deps

    RESET = True

    def barrier(self, reset=True):
        reset = reset and KB.RESET
        deps = self.all_tokens()
        for e in self.eng:
            self._wait(e, deps)
        if reset:
            self.nbar += 1
            for e in self.eng:
                self.rec[e].append(('b', self.nbar, 0))
            for e in self.eng:
                self.eng[e].sem_inc(self.bs_arrive, 1)
            m = self.eng['sp']
            m.wait_ge(self.bs_arrive, len(self.eng) * self.nbar)
            for sm in list(self.sem.values()) + [v for k, v in self.dsem.items() if k[0] != 'pool']:
                m.sem_clear(sm)
            m.sem_inc(self.bs_go, 1)
            for e in self.eng:
                self.eng[e].wait_ge(self.bs_go, self.nbar)
            for e in self.cnt:
                self.cnt[e] = 0
            for k in self.dcnt:
                if k[0] != 'pool':
                    self.dcnt[k] = 0
            self.seen = {e: {} for e in self.eng}
        self.lastw = {}
        self.readers = {}

    @contextlib.contextmanager
    def phase(self):
        with contextlib.ExitStack() as ph:
            old = self.st
            self.st = ph
            try:
                yield
            finally:
                self.barrier()
                self.st = old


def build(cfg=None):
    cfg = cfg or {}
    subs = cfg.get('subs', [(i, w) for i in range(DEPTH) for w in (0, 1)])
    nc = bass.Bass("TRN2", target_bir_lowering=False)

    def din(name, shape, dt=F32):
        return nc.dram_tensor(name, list(shape), dt, kind="ExternalInput").ap()

    def dout(name, shape, dt=F32):
        return nc.dram_tensor(name, list(shape), dt, kind="ExternalOutput").ap()

    def dscr(name, shape, dt=F32):
        return nc.dram_tensor(name, list(shape), dt, kind="Internal").ap()

    xin = din("xin", [NTOK, D])
    cond_d = din("cond", [2, D])
    st_in = din("st_in", [2, 2, 16, 64, 64])
    ck_d = din("ck", [2, 512, 256])
    cv_d = din("cv", [2, 512, 256])
    ada_w = din("ada_w", [4, D, 6 * D])
    ada_b = din("ada_b", [4, 6 * D])
    ln_g = din("ln_g", [4, 2, D])
    ln_b = din("ln_b", [4, 2, D])
    muT_d = din("muT", [2, 128, 6, 8])
    wrkv_d = din("wrkv", [2, 3, D, D])
    rwo_d = din("rwo", [2, D, D])
    w0_d = din("w0", [2, 2, D])
    w1c_d = din("w1c", [2, D, 128])
    w2c_d = din("w2c", [2, 128, D])
    a0_d = din("a0", [2, 2, D])
    a1c_d = din("a1c", [2, D, 128])
    a2c_d = din("a2c", [2, 128, D])
    g1_d = din("g1", [2, D, 128])
    g2_d = din("g2", [2, 128, D])
    rkk_d = din("rkk", [2, D])
    rka_d = din("rka", [2, D])
    rrk_d = din("rrk", [2, D])
    lnxg_d = din("lnxg", [2, D])
    lnxb_d = din("lnxb", [2, D])
    wqkv_d = din("wqkv", [2, D, 1536])
    awo_d = din("awo", [2, D, D])
    qn_d = din("qn", [2, 64])
    kn_d = din("kn", [2, 64])
    pwq_d = din("pwq", [4, D, 2048])
    pkT_d = din("pkT", [4, 2, 128, 128])
    pu_d = [din(f"pu{i}", [16384, D]) for i in range(4)]
    pv_d = [din(f"pv{i}", [16384, D]) for i in range(4)]
    cst_d = din("cst", [128, NCST])
    sel2_d = din("sel2", [2, 256])
    rope_d = din("rope", [1024, 64])

    y_d = dout("y", [NTOK, D])
    nst_d = dout("nst", [2, 2, 2, 16, 64, 64])
    nk_d = dout("nk", [2, 2, 256, 256])
    nv_d = dout("nv", [2, 2, 256, 256])

    mods_d = dscr("mods", [4, 2, 6 * D])
    r_s = dscr("r_s", [NTOK, D])
    k_s = dscr("k_s", [NTOK, D])
    v_s = dscr("v_s", [NTOK, D])
    g_s = dscr("g_s", [NTOK, D])
    lw_s = dscr("lw_s", [2, NTOK, D])
    a_s = dscr("a_s", [2, NTOK, D])
    yf_s = dscr("yf_s", [NTOK, D])
    z_s = dscr("z_s", [NTOK, D])

    with contextlib.ExitStack() as top:
        kb = KB(nc, top)

        def V(fn, r=(), w=()):
            return kb.op('dve', fn, r, w)

        def A(fn, r=(), w=()):
            return kb.op('act', fn, r, w)

        def G(fn, r=(), w=()):
            return kb.op('pool', fn, r, w)

        def T(fn, r=(), w=()):
            return kb.op('pe', fn, r, w)

        x_res = kb.sb('x_res', [128, NT, D])
        cst = kb.sb('cst', [128, CR0])
        identb = kb.sb('identb', [128, 128], BF16)
        sel2 = kb.sb('sel2', [2, 256])
        siluT = kb.sb('siluT', [128, 16], BF16)
        psb = [kb.ps(f"psb{i}", [128, 512]) for i in range(8)]
        PS = [f'ps{i}' for i in range(8)]
        ident = cst[:, C_ID:C_ID + 128]

        def psbf(i):
            return psb[i][:].bitcast(BF16)

        nc.all_engine_barrier()
        for sm in list(kb.sem.values()) + list(kb.dsem.values()) + [kb.bgsem, kb.bs_arrive, kb.bs_go]:
            nc.sync.sem_clear(sm)
        nc.all_engine_barrier()
        kb.dma('sp', cst[:], cst_d[:, 0:CR0], writes=['cst'])
        kb.dma('sp', sel2[:], sel2_d, writes=['sel2'])
        if cfg.get('load_x', True):
            for tt in range(NT):
                kb.dma('sp', x_res[:, tt, :], xin[tt * 128:(tt + 1) * 128, :], writes=[f'x{tt}'])
        V(lambda e: e.tensor_copy(out=identb[:], in_=ident), ['cst'], ['identb'])

        with kb.phase():
            cnd = kb.sb('cnd', [2, D])
            sil = kb.sb('sil', [2, D])
            kb.dma('sp', cnd[:], cond_d, writes=['cnd'])
            A(lambda e: e.activation(out=sil[:], in_=cnd[:], func=AF.Silu), ['cnd'], ['sil'])
            for c in range(8):
                T(lambda e: e.transpose(out=psb[0][:, c * 2:(c + 1) * 2], in_=sil[0:2, c * 128:(c + 1) * 128],
                                        identity=cst[0:2, C_ID:C_ID + 2]), ['sil', 'cst'], [PS[0]])
            V(lambda e: e.tensor_copy(out=siluT[:], in_=psb[0][:, 0:16]), [PS[0]], ['siluT'])

        def load_cast(dst, src, r=(), w=()):
            kb.dma('pool', dst, src, reads=r, writes=w)

        def adaln(i):
            with kb.phase():
                brow = kb.sb('brow', [2, 3072])
                mrow = kb.sb('mrow', [2, 3072])
                awb = [kb.sb(f'awb{b}', [128, 3072], BF16) for b in range(2)]
                for half in range(2):
                    c0 = half * 3072
                    kb.dma('sp', brow[:], ada_b[i:i + 1, c0:c0 + 3072].partition_broadcast(2), writes=['brow'])
                    for k in range(8):
                        b = k % 2
                        for q in range(2):
                            load_cast(awb[b][:, q * 1536:(q + 1) * 1536],
                                      ada_w[i, k * 128:(k + 1) * 128, c0 + q * 1536:c0 + (q + 1) * 1536],
                                      w=[f'awb{b}'])
                        for cb in range(6):
                            T(lambda e: e.matmul(psb[cb][0:2, :], lhsT=siluT[:, k * 2:(k + 1) * 2],
                                                 rhs=awb[b][:, cb * 512:(cb + 1) * 512], start=(k == 0), stop=(k == 7)),
                              ['siluT', f'awb{b}'], [PS[cb]])
                    for cb in range(6):
                        V(lambda e: e.tensor_tensor(out=mrow[:, cb * 512:(cb + 1) * 512], in0=psb[cb][0:2, :],
                                                    in1=brow[:, cb * 512:(cb + 1) * 512], op=ALU.add),
                          [PS[cb], 'brow'], ['mrow'])
                    kb.dma('sp', mods_d[i, :, c0:c0 + 3072], mrow[:], reads=['mrow'], writes=['mods'])

        def setup_mod(i, which, want=(0, 1, 2), ln=True):
            modb = kb.sb('modb', [128, 6, D])
            lnb = kb.sb('lnb', [128, 2, D]) if ln else None
            with kb.phase():
                mrow3 = kb.sb('mrow3', [2, 3072])
                kb.dma('sp', mrow3[:], mods_d[i, :, which * 3072:(which + 1) * 3072], reads=['mods'], writes=['mrow3'])
                n = 0
                for v in want:
                    for cond in range(2):
                        for half in range(2):
                            b = n % 4
                            n += 1
                            T(lambda e: e.matmul(psb[b][:], lhsT=sel2[:, cond * 128:(cond + 1) * 128],
                                                 rhs=mrow3[:, v * 1024 + half * 512:v * 1024 + (half + 1) * 512],
                                                 start=True, stop=True), ['sel2', 'mrow3'], [PS[b]])
                            dst = modb[:, v * 2 + cond, half * 512:(half + 1) * 512]
                            if v == 1:
                                V(lambda e: e.tensor_scalar(out=dst, in0=psb[b][:], scalar1=1.0, scalar2=None,
                                                            op0=ALU.add), [PS[b]], ['modb'])
                            else:
                                A(lambda e: e.copy(out=dst, in_=psb[b][:]), [PS[b]], ['modb'])
                if ln:
                    kb.dma('sp', lnb[:, 0, :], ln_g[i, which:which + 1, :].partition_broadcast(128), writes=['lnb'])
                    kb.dma('sp', lnb[:, 1, :], ln_b[i, which:which + 1, :].partition_broadcast(128), writes=['lnb'])
            return modb, lnb

        def make_h(tt, modb, htok, hkey):
            cond = 0 if tt < 4 else 1
            V(lambda e: e.tensor_tensor(out=htok, in0=x_res[:, tt, :], in1=modb[:, 2 + cond, :], op=ALU.mult),
              [f'x{tt}', 'modb'], [hkey])
            V(lambda e: e.tensor_tensor(out=htok, in0=htok, in1=modb[:, 0 + cond, :], op=ALU.add),
              [hkey, 'modb'], [hkey])

        def transpose8(src_bf, skey, dst3, dkey, bank):
            pv = psbf(bank)
            for c in range(8):
                T(lambda e: e.transpose(out=pv[:, c * 128:(c + 1) * 128], in_=src_bf[:, c * 128:(c + 1) * 128],
                                        identity=identb[:]), [skey, 'identb'], [PS[bank]])
            A(lambda e: e.copy(out=dst3, in_=pv.rearrange("p (c t) -> p c t", c=8)), [PS[bank]], [dkey])

        def post_sublayer(tt, banks, modb, lnb, wk):
            cond = 0 if tt < 4 else 1
            z, stats, mv, rs = wk
            xk = f'x{tt}'
            for half in range(2):
                sl = slice(half * 512, (half + 1) * 512)
                V(lambda e: e.tensor_tensor(out=z[:, sl], in0=psb[banks[half]][:], in1=modb[:, 4 + cond, sl], op=ALU.mult),
                  [PS[banks[half]], 'modb'], ['z'])
                V(lambda e: e.scalar_tensor_tensor(out=z[:, sl], in0=x_res[:, tt, sl], scalar=ALPHA, in1=z[:, sl],
                                                   op0=ALU.mult, op1=ALU.add), [xk, 'z'], ['z'])
                V(lambda e: e.bn_stats(out=stats[:, half, :], in_=z[:, sl]), ['z'], ['stats'])
            V(lambda e: e.bn_aggr(out=mv[:], in_=stats[:].rearrange("p a b -> p (a b)")), ['stats'], ['mv'])
            V(lambda e: e.tensor_scalar(out=rs[:, 0:1], in0=mv[:, 1:2], scalar1=LN_EPS, scalar2=None, op0=ALU.add),
              ['mv'], ['rs'])
            A(lambda e: e.activation(out=rs[:, 1:2], in_=rs[:, 0:1], func=AF.Sqrt), ['rs'], ['rs'])
            V(lambda e: e.reciprocal(out=rs[:, 2:3], in_=rs[:, 1:2]), ['rs'], ['rs'])
            V(lambda e: e.tensor_scalar(out=z[:], in0=z[:], scalar1=mv[:, 0:1], scalar2=rs[:, 2:3],
                                        op0=ALU.subtract, op1=ALU.mult), ['z', 'mv', 'rs'], ['z'])
            V(lambda e: e.tensor_tensor(out=z[:], in0=z[:], in1=lnb[:, 0, :], op=ALU.mult), ['z', 'lnb'], ['z'])
            V(lambda e: e.tensor_tensor(out=x_res[:, tt, :], in0=z[:], in1=lnb[:, 1, :], op=ALU.add),
              ['z', 'lnb'], [xk])

        def post_work():
            return (kb.sb('z', [128, D]), kb.sb('stats', [128, 2, 6]), kb.sb('mv', [128, 2]), kb.sb('rs', [128, 4]))

        ubv_d = dscr("ubv", [16384, 2048], BF16)

        def peer_convert(i):
            for c in range(16):
                rs_ = slice(c * 1024, (c + 1) * 1024)
                kb.bg_dma(ubv_d[rs_, 0:1024], pu_d[i][rs_, :])
                kb.bg_dma(ubv_d[rs_, 1024:2048], pv_d[i][rs_, :])

        def peer_sublayer(i):
            with kb.phase():
                modb, lnb = setup_mod(i, 1)
                kb.bg_wait('pool')
                wk = post_work()
                wq = kb.sb('wq', [128, 8, 2048], BF16)
                for c in range(8):
                    load_cast(wq[:, c, :], pwq_d[i, c * 128:(c + 1) * 128, :], w=['wq'])
                keyT = kb.sb('keyT', [128, 2, 128], BF16)
                load_cast(keyT[:], pkT_d[i].rearrange("z d k -> d z k"), w=['keyT'])
                htok1 = kb.sb('htok', [128, D])
                htok = [htok1, htok1]
                hbs = [kb.sb(f'hb{b}', [128, D], BF16) for b in range(2)]
                hTt = kb.sb('hTt', [128, 8, 128], BF16)
                qT = kb.sb('qT', [128, 16, 128], BF16)
                s_sb = kb.sb('s_sb', [128, 16, 128])
                sv = kb.sb('sv', [128, 16, 16])
                si = kb.sb('si', [128, 16, 16], U32)
                si_f = kb.sb('si_f', [128, 16, 16])
                cand = kb.sb('cand', [128, 8, 256])
                cv = kb.sb('cv', [128, 8, 16])
                ci = kb.sb('ci', [128, 128], U32)
                ab_i = kb.sb('ab_i', [128, 2, 128], U32)
                ab_f = kb.sb('ab_f', [128, 2, 128])
                oh = cand
                candk = [f'cand{h}' for h in range(8)]
                i12 = kb.sb('i12', [128, 2, 128])
                idx_i = [kb.sb(f'idx_i{b}', [128, 128], I32) for b in range(2)]
                gs = kb.sb('gs', [128, 8])
                gate = [kb.sb(f'gate{b}', [128, 128]) for b in range(2)]
                pre = kb.sb('pre', [128, 128])
                ga = kb.sb('ga', [128, 128])
                GK, NBUF = 2, 6
                uv = [kb.sb(f'uv{b}', [128, GK, 2048], BF16) for b in range(NBUF)]
                Dk = [kb.sb(f'Dk{b}', [128, 128], BF16) for b in range(2)]
                iota16 = cst[:, C_IOTA:C_IOTA + 16]

                def topk_stage(tt, sl):
                    hk, ik, gk = 'htok', f'idx_i{sl}', f'gate{sl}'
                    hb = hbs[sl]
                    make_h(tt, modb, htok[sl][:], hk)
                    A(lambda e: e.copy(out=hb[:], in_=htok[sl][:]), [hk], [f'hb{sl}'])
                    yield
                    transpose8(hb, f'hb{sl}', hTt[:], 'hTt', 6)
                    yield
                    for rnd in range(2):
                        for hz in range(rnd * 8, rnd * 8 + 8):
                            bk = 2 + (hz % 8) // 4
                            for c in range(8):
                                T(lambda e: e.matmul(psb[bk][:, (hz % 4) * 128:(hz % 4 + 1) * 128],
                                                     lhsT=wq[:, c, hz * 128:(hz + 1) * 128], rhs=hTt[:, c, :],
                                                     start=(c == 0), stop=(c == 7)), ['wq', 'hTt'], [PS[bk]])
                            if hz % 2 == 1:
                                yield
                        for b2 in range(2):
                            A(lambda e: e.copy(out=qT[:, rnd * 8 + b2 * 4:rnd * 8 + b2 * 4 + 4, :],
                                               in_=psb[2 + b2][:].rearrange("p (a t) -> p a t", a=4)), [PS[2 + b2]], ['qT'])
                        yield
                    for rnd in range(2):
                        for hz in range(rnd * 8, rnd * 8 + 8):
                            bk = 4 + (hz % 8) // 4
                            T(lambda e: e.matmul(psb[bk][:, (hz % 4) * 128:(hz % 4 + 1) * 128],
                                                 lhsT=qT[:, hz, :], rhs=keyT[:, hz % 2, :], start=True, stop=True),
                              ['qT', 'keyT'], [PS[bk]])
                        for b2 in range(2):
                            h0 = rnd * 8 + b2 * 4
                            A(lambda e: e.copy(out=s_sb[:, h0:h0 + 4, :],
                                               in_=psb[4 + b2][:].rearrange("p (a t) -> p a t", a=4)),
                              [PS[4 + b2]], [f's_sb{h0 + a}' for a in range(4)])
                        yield
                    for hz in range(16):
                        ks, kv_, ki = f's_sb{hz}', f'sv{hz}', f'si{hz}'
                        V(lambda e: e.max(out=sv[:, hz, 0:8], in_=s_sb[:, hz, :]), [ks], [kv_])
                        V(lambda e: e.max_index(out=si[:, hz, 0:8], in_max=sv[:, hz, 0:8], in_values=s_sb[:, hz, :]),
                          [ks, kv_], [ki])
                        V(lambda e: e.match_replace(out=s_sb[:, hz, :], in_to_replace=sv[:, hz, 0:8],
                                                    in_values=s_sb[:, hz, :], imm_value=-1e30), [ks, kv_], [ks])
                        yield
                        V(lambda e: e.max(out=sv[:, hz, 8:16], in_=s_sb[:, hz, :]), [ks], [kv_])
                        V(lambda e: e.max_index(out=si[:, hz, 8:16], in_max=sv[:, hz, 8:16], in_values=s_sb[:, hz, :]),
                          [ks, kv_], [ki])
                        yield
                    svk = [f'sv{hz}' for hz in range(16)]
                    sik = [f'si{hz}' for hz in range(16)]
                    V(lambda e: e.tensor_copy(out=si_f[:], in_=si[:]), sik, ['si_f'])
                    sv4 = sv[:].rearrange("p (h z) k -> p h z k", z=2)
                    sif4 = si_f[:].rearrange("p (h z) k -> p h z k", z=2)
                    V(lambda e: e.tensor_tensor(out=cand[:].rearrange("p h (a b) -> p h a b", a=16),
                                                in0=sv4[:, :, 0, :].unsqueeze(3).to_broadcast([128, 8, 16, 16]),
                                                in1=sv4[:, :, 1, :].unsqueeze(2).to_broadcast([128, 8, 16, 16]),
                                                op=ALU.add), svk, [f'cand{h}' for h in range(8)])
                    yield
                    for h in range(8):
                        kc, kcv, kci = f'cand{h}', f'cv{h}', f'ci{h}'
                        V(lambda e: e.max(out=cv[:, h, 0:8], in_=cand[:, h, :]), [kc], [kcv])
                        V(lambda e: e.max_index(out=ci[:, h * 16:h * 16 + 8], in_max=cv[:, h, 0:8], in_values=cand[:, h, :]),
                          [kc, kcv], [kci])
                        V(lambda e: e.match_replace(out=cand[:, h, :], in_to_replace=cv[:, h, 0:8],
                                                    in_values=cand[:, h, :], imm_value=-1e30), [kc, kcv], [kc])
                        yield
                        V(lambda e: e.max(out=cv[:, h, 8:16], in_=cand[:, h, :]), [kc], [kcv])
                        V(lambda e: e.max_index(out=ci[:, h * 16 + 8:h * 16 + 16], in_max=cv[:, h, 8:16],
                                                in_values=cand[:, h, :]), [kc, kcv], [kci])
                        yield
                    cvk = [f'cv{h}' for h in range(8)]
                    cik = [f'ci{h}' for h in range(8)]
                    V(lambda e: e.tensor_scalar(out=ab_i[:, 0, :], in0=ci[:], scalar1=4, scalar2=None,
                                                op0=ALU.logical_shift_right), cik, ['ab_i'])
                    V(lambda e: e.tensor_scalar(out=ab_i[:, 1, :], in0=ci[:], scalar1=15, scalar2=None,
                                                op0=ALU.bitwise_and), cik, ['ab_i'])
                    V(lambda e: e.tensor_copy(out=ab_f[:], in_=ab_i[:]), ['ab_i'], ['ab_f'])
                    yield
                    for zz in range(2):
                        V(lambda e: e.tensor_tensor(out=oh[:].rearrange("p h (k a) -> p (h k) a", k=16), in0=ab_f[:, zz, :].unsqueeze(2).to_broadcast([128, 128, 16]),
                                                    in1=iota16.unsqueeze(1).to_broadcast([128, 128, 16]),
                                                    op=ALU.is_equal), ['ab_f', 'cst'], candk)
                        yield
                        V(lambda e: e.tensor_tensor(out=oh[:].rearrange("p h (k a) -> p h k a", k=16),
                                                    in0=oh[:].rearrange("p h (k a) -> p h k a", k=16),
                                                    in1=sif4[:, :, zz, :].unsqueeze(2).to_broadcast([128, 8, 16, 16]),
                                                    op=ALU.mult), candk + ['si_f'], candk)
                        yield
                        V(lambda e: e.tensor_reduce(out=i12[:, zz, :], in_=oh[:].rearrange("p h (k a) -> p (h k) a", k=16), axis=AX.X, op=ALU.add), candk, ['i12'])
                        yield
                    V(lambda e: e.scalar_tensor_tensor(out=i12[:, 0, :], in0=i12[:, 0, :], scalar=128.0, in1=i12[:, 1, :],
                                                       op0=ALU.mult, op1=ALU.add), ['i12'], ['i12'])
                    V(lambda e: e.tensor_scalar(out=i12[:, 0, :], in0=i12[:, 0, :], scalar1=0.0, scalar2=16383.0, op0=ALU.max, op1=ALU.min),
                      ['i12'], ['i12'])
                    V(lambda e: e.tensor_copy(out=idx_i[sl][:], in_=i12[:, 0, :]), ['i12'], [ik])
                    g3 = gate[sl][:].rearrange("p (h k) -> p h k", h=8)
                    V(lambda e: e.tensor_tensor(out=g3, in0=cv[:], in1=cv[:, :, 0:1].to_broadcast([128, 8, 16]),
                                                op=ALU.subtract), cvk, [gk])
                    A(lambda e: e.activation(out=g3, in_=g3, func=AF.Exp), [gk], [gk])
                    V(lambda e: e.tensor_reduce(out=gs[:], in_=g3, axis=AX.X, op=ALU.add), [gk], ['gs'])
                    V(lambda e: e.reciprocal(out=gs[:], in_=gs[:]), ['gs'], ['gs'])
                    V(lambda e: e.tensor_tensor(out=g3, in0=g3, in1=gs[:].unsqueeze(2).to_broadcast([128, 8, 16]), op=ALU.mult),
                      [gk, 'gs'], [gk])
                    yield

                def drain(gen, n=None):
                    if gen is None:
                        return None
                    try:
                        if n is None:
                            while True:
                                next(gen)
                        for _ in range(n):
                            next(gen)
                    except StopIteration:
                        return None
                    return gen

                def expert_stage(tt, sl, nxt):
                    hk, ik, gk = f'htok{sl}', f'idx_i{sl}', f'gate{sl}'
                    V(lambda e: e.memset(pre[:], 0.0), [], ['pre'])
                    ngrp = 128 // GK

                    def S0(g):
                        gb = g % NBUF
                        for jj in range(GK):
                            k = g * GK + jj
                            kb.gather(uv[gb][:, jj, :], ubv_d, idx_i[sl][:, k:k + 1], reads=[ik, 'ubv'], writes=[f'uv{gb}_{jj}'])

                    def S1(g):
                        gb = g % NBUF
                        for jj in range(GK):
                            k = g * GK + jj
                            V(lambda e: e.scalar_tensor_tensor(out=uv[gb][:, jj, 0:1024], in0=uv[gb][:, jj, 0:1024], scalar=1.0, in1=hbs[sl][:],
                                                               op0=ALU.mult, op1=ALU.mult, accum_out=pre[:, k:k + 1]),
                              [f'uv{gb}_{jj}', f'hb{sl}', 'pre'], [f'pre{g}', f'uv{gb}_{jj}'])

                    def S2(g):
                        k0 = g * GK
                        A(lambda e: e.activation(out=ga[:, k0:k0 + GK], in_=pre[:, k0:k0 + GK], func=AF.Gelu), [f'pre{g}', 'pre'], [f'ga{g}'])
                        V(lambda e: e.tensor_tensor(out=ga[:, k0:k0 + GK], in0=ga[:, k0:k0 + GK], in1=gate[sl][:, k0:k0 + GK],
                                                    op=ALU.mult), [f'ga{g}', gk], [f'ga{g}'])

                    def S3(g):
                        gb = g % NBUF
                        for jj in range(GK):
                            k = g * GK + jj
                            A(lambda e: e.activation(out=Dk[k % 2][:], in_=identb[:], func=AF.Copy, scale=ga[:, k:k + 1]),
                              ['identb', f'ga{g}'], [f'Dk{k % 2}'])
                            for half in range(2):
                                T(lambda e: e.matmul(psb[half][:], lhsT=Dk[k % 2][:],
                                                     rhs=uv[gb][:, jj, 1024 + half * 512:1024 + (half + 1) * 512],
                                                     start=(k == 0), stop=(k == 127)), [f'Dk{k % 2}', f'uv{gb}_{jj}'], [PS[half]])

                    for it in range(ngrp + 3):
                        if it < ngrp:
                            S0(it)
                        if 0 <= it - 1 < ngrp:
                            S1(it - 1)
                        if 0 <= it - 2 < ngrp:
                            S2(it - 2)
                        if 0 <= it - 3 < ngrp:
                            S3(it - 3)
                        nxt = drain(nxt, 2)
                    drain(nxt)
                    post_sublayer(tt, (0, 1), modb, lnb, wk)

                drain(topk_stage(0, 0))
                for tt in range(NT):
                    sl = tt % 2
                    nxt = topk_stage(tt + 1, 1 - sl) if tt + 1 < NT else None
                    expert_stage(tt, sl, nxt)

        def rwkv_sublayer(i):
            j = i // 2
            one1 = cst[0:1, C_ONES:C_ONES + 128]
            onec = cst[:, C_ONES:C_ONES + 1]
            with contextlib.ExitStack() as sub_st:
                old_st = kb.st
                kb.st = sub_st
                bon = kb.sb('bon', [128, NT, 16])
                with kb.phase():
                    modb, _ = setup_mod(i, 0, want=(0, 1), ln=False)
                    hT = kb.sb('hT', [128, 8, HPAD], BF16)
                    G(lambda e: e.memset(hT[:], 0.0), [], ['hT'])
                    htok = kb.sb('htok', [128, D])
                    hb = kb.sb('hb', [128, D], BF16)
                    for tt in range(NT):
                        pc = padcol(tt)
                        make_h(tt, modb, htok[:], 'htok')
                        A(lambda e: e.copy(out=hb[:], in_=htok[:]), ['htok'], ['hb'])
                        transpose8(hb, 'hb', hT[:, :, pc:pc + 128], 'hT', 7)
                    muT = kb.sb('muT', [128, 6, 8])
                    kb.dma('sp', muT[:], muT_d[j], writes=['muT'])
                    w0row = kb.sb('w0row', [1, 2, D])
                    a0row = kb.sb('a0row', [1, 2, D])
                    kb.dma('sp', w0row[:], w0_d[j:j + 1], writes=['w0row'])
                    kb.dma('sp', a0row[:], a0_d[j:j + 1], writes=['a0row'])
                    xxs = [kb.sb(f'xx{b}', [128, 8, 128]) for b in range(2)]
                    xms = [kb.sb(f'xm{b}', [128, 8, 128], BF16) for b in range(2)]
                    W = kb.sb('W', [128, 8, D], BF16)
                    l1 = kb.sb('l1', [128, 8, 128], BF16)
                    l2 = kb.sb('l2', [128, D], BF16)
                    hid = kb.sb('hid', [128, 128], BF16)
                    ot = [kb.sb(f'ot{b}', [128, D]) for b in range(2)]
                    nout = [0]

                    def xm_tile(tt, m, bi):
                        pc = padcol(tt)
                        xx, xm, kx, km = xxs[bi], xms[bi], f'xx{bi}', f'xm{bi}'
                        V(lambda e: e.tensor_tensor(out=xx[:], in0=hT[:, :, pc - 1:pc + 127], in1=hT[:, :, pc + 1:pc + 129],
                                                    op=ALU.add), ['hT'], [kx])
                        V(lambda e: e.scalar_tensor_tensor(out=xx[:], in0=xx[:], scalar=0.5, in1=hT[:, :, pc:pc + 128],
                                                           op0=ALU.mult, op1=ALU.subtract), [kx, 'hT'], [kx])
                        V(lambda e: e.tensor_tensor(out=xx[:], in0=xx[:], in1=muT[:, m, :].unsqueeze(2).to_broadcast([128, 8, 128]),
                                                    op=ALU.mult), [kx, 'muT'], [kx])
                        V(lambda e: e.tensor_tensor(out=xm[:], in0=xx[:], in1=hT[:, :, pc:pc + 128], op=ALU.add),
                          [kx, 'hT'], [km])

                    def xm_iter(m):
                        xm_tile(0, m, 0)
                        for tt in range(NT):
                            if tt + 1 < NT:
                                xm_tile(tt + 1, m, (tt + 1) % 2)
                            yield tt, xms[tt % 2], f'xm{tt % 2}'

                    def store(dst_rows, func=None, post_scale=None):
                        b = nout[0] % 2
                        nout[0] += 1
                        for half in range(2):
                            sl = slice(half * 512, (half + 1) * 512)
                            if func is None:
                                A(lambda e: e.copy(out=ot[b][:, sl], in_=psb[half][:]), [PS[half]], [f'ot{b}'])
                            else:
                                A(lambda e: e.activation(out=ot[b][:, sl], in_=psb[half][:], func=func), [PS[half]], [f'ot{b}'])
                        if post_scale is not None:
                            V(lambda e: e.tensor_scalar(out=ot[b][:], in0=ot[b][:], scalar1=post_scale, scalar2=None,
                                                        op0=ALU.mult), [f'ot{b}'], [f'ot{b}'])
                        kb.dma('sp', dst_rows, ot[b][:], reads=[f'ot{b}'], writes=['scr'])

                    for (m, widx, dst_s) in ((0, 0, r_s), (2, 1, k_s), (3, 2, v_s)):
                        for c in range(8):
                            load_cast(W[:, c, :], wrkv_d[j, widx, c * 128:(c + 1) * 128, :], w=['W'])
                        for tt, xm, km in xm_iter(m):
                            for half in range(2):
                                for c in range(8):
                                    T(lambda e: e.matmul(psb[half][:], lhsT=xm[:, c, :], rhs=W[:, c, half * 512:(half + 1) * 512],
                                                         start=(c == 0), stop=(c == 7)), [km, 'W'], [PS[half]])
                            store(dst_s[tt * 128:(tt + 1) * 128, :])
                    for (m, l1_d, l2_d, brow, hfunc, dst2, ofunc, oscale) in (
                            (1, w1c_d, w2c_d, w0row, AF.Tanh, lw_s, AF.Sigmoid, NEG_EXP_HALF),
                            (4, a1c_d, a2c_d, a0row, None, a_s, AF.Sigmoid, None)):
                        load_cast(l1[:], l1_d[j].rearrange("(c p) l -> p c l", p=128), w=['l1'])
                        load_cast(l2[:], l2_d[j], w=['l2'])
                        for tt, xm, km in xm_iter(m):
                            for c in range(8):
                                T(lambda e: e.matmul(psb[2][:, 0:128], lhsT=l1[:, c, :], rhs=xm[:, c, :],
                                                     start=(c == 0), stop=(c == 7)), ['l1', km], [PS[2]])
                            if hfunc is None:
                                A(lambda e: e.copy(out=hid[:], in_=psb[2][:, 0:128]), [PS[2]], ['hid'])
                            else:
                                A(lambda e: e.activation(out=hid[:], in_=psb[2][:, 0:128], func=hfunc), [PS[2]], ['hid'])
                            for z in range(2):
                                for half in range(2):
                                    sl = slice(half * 512, (half + 1) * 512)
                                    T(lambda e: e.matmul(psb[half][:], lhsT=hid[z * 64:(z + 1) * 64, :],
                                                         rhs=l2[z * 64:(z + 1) * 64, sl], start=True, stop=False),
                                      ['hid', 'l2'], [PS[half]])
                                    T(lambda e: e.matmul(psb[half][:], lhsT=one1, rhs=brow[0:1, z, sl], start=False, stop=True),
                                      ['cst', 'w0row', 'a0row'], [PS[half]])
                                store(dst2[z, tt * 128:(tt + 1) * 128, :], func=ofunc, post_scale=oscale)
                    load_cast(l1[:], g1_d[j].rearrange("(c p) l -> p c l", p=128), w=['l1'])
                    load_cast(l2[:], g2_d[j], w=['l2'])
                    for tt, xm, km in xm_iter(5):
                        for c in range(8):
                            T(lambda e: e.matmul(psb[2][:, 0:128], lhsT=l1[:, c, :], rhs=xm[:, c, :],
                                                 start=(c == 0), stop=(c == 7)), ['l1', km], [PS[2]])
                        A(lambda e: e.activation(out=hid[:], in_=psb[2][:, 0:128], func=AF.Sigmoid), [PS[2]], ['hid'])
                        for half in range(2):
                            T(lambda e: e.matmul(psb[half][:], lhsT=hid[:], rhs=l2[:, half * 512:(half + 1) * 512],
                                                 start=True, stop=True), ['hid', 'l2'], [PS[half]])
                        store(g_s[tt * 128:(tt + 1) * 128, :])

                with kb.phase():
                    kkb = kb.sb('kkb', [128, D])
                    kab = kb.sb('kab', [128, D])
                    rkb = kb.sb('rkb', [128, D])
                    kb.dma('sp', kkb[:], rkk_d[j:j + 1, :].partition_broadcast(128), writes=['kkb'])
                    kb.dma('sp', kab[:], rka_d[j:j + 1, :].partition_broadcast(128), writes=['kab'])
                    kb.dma('sp', rkb[:], rrk_d[j:j + 1, :].partition_broadcast(128), writes=['rkb'])
                    r_t = kb.sb('r_t', [128, D])
                    k_t = kb.sb('k_t', [128, D])
                    v_t = kb.sb('v_t', [128, D])
                    lw_t = kb.sb('lw_t', [128, D])
                    a_t = kb.sb('a_t', [128, D])
                    f1 = kb.sb('f1', [128, D])
                    f2 = kb.sb('f2', [128, D])
                    f3 = kb.sb('f3', [128, D])
                    fP = kb.sb('fP', [128, D])
                    fPi = kb.sb('fPi', [128, D])
                    n16 = kb.sb('n16', [128, 4, 16])
                    at_tok = kb.sb('at_tok', [128, D], BF16)
                    rt_tok = kb.sb('rt_tok', [128, D], BF16)
                    bt_tok = kb.sb('bt_tok', [128, D], BF16)
                    kt_tok = kb.sb('kt_tok', [128, D], BF16)
                    vb = kb.sb('vb', [128, D], BF16)
                    arT = kb.sb('arT', [64, 16, 2, 128], BF16)
                    btT = kb.sb('btT', [64, 16, 128], BF16)
                    ktT = kb.sb('ktT', [64, 16, 128], BF16)
                    Am = kb.sb('Am', [128, 16, 512], BF16)
                    A4 = [kb.sb(f'A4{s}', [128, 4, 128], BF16) for s in range(8)]
                    AT4 = [kb.sb(f'AT4{s}', [128, 4, 128], BF16) for s in range(8)]
                    NB = [kb.sb(f'NB{s}', [128, 4, 128], BF16) for s in range(8)]
                    Tb = [kb.sb(f'Tb{s}', [128, 2, 128], BF16) for s in range(8)]
                    N_all = kb.sb('N_all', [128, 16, 128], BF16)
                    Xb = kb.sb('Xb', [128, D], BF16)
                    Ub = kb.sb('Ub', [128, D], BF16)
                    S_T = kb.sb('S_T', [64, 16, 64])
                    Sb = kb.sb('Sb', [64, 16, 64], BF16)
                    PC = kb.sb('PC', [64, 16])
                    stl = kb.sb('stl', [64, 16, 64])
                    yt = kb.sb('yt', [128, D])
                    cstR = kb.sb('cstR', [128, NCST - CR0])
                    kb.dma('sp', cstR[:], cst_d[:, CR0:NCST], writes=['cst'])
                    for (sq_i, tiles) in SEQS:
                        sample = (sq_i == 2)
                        for dr in range(2):
                            mask4 = cstR[:, C_M4F - CR0:C_M4F - CR0 + 512] if dr == 0 else cstR[:, C_M4B - CR0:C_M4B - CR0 + 512]
                            if sample:
                                kb.dma('sp', stl[:], st_in[j, dr].rearrange("h i j -> i h j"), writes=['stl'])
                                for h in range(16):
                                    T(lambda e: e.transpose(out=psb[h // 8][0:64, (h % 8) * 64:(h % 8 + 1) * 64], in_=stl[:, h, :],
                                                            identity=cst[0:64, C_ID:C_ID + 64]), ['stl', 'cst'], [PS[h // 8]])
                                for hq in range(2):
                                    V(lambda e: e.tensor_copy(out=S_T[:, hq * 8:(hq + 1) * 8, :],
                                                              in_=psb[hq][0:64, :].rearrange("p (h i) -> p h i", h=8)),
                                      [PS[hq]], ['S_T'])
                            else:
                                V(lambda e: e.memset(S_T[:], 0.0), [], ['S_T'])
                            A(lambda e: e.copy(out=Sb[:], in_=S_T[:]), ['S_T'], ['Sb'])
                            order = tiles if dr == 0 else tiles[::-1]
                            for tt in order:
                                rows = slice(tt * 128, (tt + 1) * 128)
                                kb.dma('sp', r_t[:], r_s[rows, :], reads=['scr'], writes=['r_t'])
                                kb.dma('sp', k_t[:], k_s[rows, :], reads=['scr'], writes=['k_t'])
                                kb.dma('sp', v_t[:], v_s[rows, :], reads=['scr'], writes=['v_t'])
                                kb.dma('sp', lw_t[:], lw_s[dr, rows, :], reads=['scr'], writes=['lw_t'])
                                kb.dma('sp', a_t[:], a_s[dr, rows, :], reads=['scr'], writes=['a_t'])
                                tri = cstR[:, C_TRIF - CR0:C_TRIF - CR0 + 128] if dr == 0 else cstR[:, C_TRIB - CR0:C_TRIB - CR0 + 128]
                                for half in range(2):
                                    T(lambda e: e.matmul(psb[half][:], lhsT=tri, rhs=lw_t[:, half * 512:(half + 1) * 512],
                                                         start=True, stop=True), ['cst', 'lw_t'], [PS[half]])
                                for h in range(16):
                                    T(lambda e: e.matmul(psb[2][0:64, h:h + 1], lhsT=lw_t[:, h * 64:(h + 1) * 64], rhs=onec,
                                                         start=True, stop=True), ['lw_t', 'cst'], [PS[2]])
                                for half in range(2):
                                    sl = slice(half * 512, (half + 1) * 512)
                                    A(lambda e: e.activation(out=fP[:, sl], in_=psb[half][:], func=AF.Exp), [PS[half]], ['fP'])
                                for half in range(2):
                                    sl = slice(half * 512, (half + 1) * 512)
                                    A(lambda e: e.activation(out=fPi[:, sl], in_=psb[half][:], func=AF.Exp, scale=-1.0),
                                      [PS[half]], ['fPi'])
                                for half in range(2):
                                    sl = slice(half * 512, (half + 1) * 512)
                                    V(lambda e: e.tensor_tensor(out=f3[:, sl], in0=psb[half][:], in1=lw_t[:, sl], op=ALU.subtract),
                                      [PS[half], 'lw_t', 'fPi'], ['f3'])
                                A(lambda e: e.activation(out=f3[:], in_=f3[:], func=AF.Exp), ['f3'], ['f3'])
                                A(lambda e: e.copy(out=vb[:], in_=v_t[:]), ['v_t'], ['vb'])
                                A(lambda e: e.activation(out=PC[:], in_=psb[2][0:64, 0:16], func=AF.Exp), [PS[2]], ['PC'])
                                V(lambda e: e.tensor_tensor(out=rt_tok[:], in0=r_t[:], in1=fP[:], op=ALU.mult), ['r_t', 'fP'], ['rt_tok'])
                                V(lambda e: e.scalar_tensor_tensor(out=f2[:], in0=a_t[:], scalar=-1.0, in1=kab[:],
                                                                   op0=ALU.add, op1=ALU.mult), ['a_t', 'kab'], ['f2'])
                                V(lambda e: e.scalar_tensor_tensor(out=f2[:], in0=f2[:], scalar=1.0, in1=k_t[:],
                                                                   op0=ALU.add, op1=ALU.mult), ['f2', 'k_t'], ['f2'])
                                V(lambda e: e.tensor_tensor(out=kt_tok[:], in0=f2[:], in1=fPi[:], op=ALU.mult), ['f2', 'fPi'], ['kt_tok'])
                                V(lambda e: e.tensor_tensor(out=fP[:], in0=r_t[:], in1=f2[:], op=ALU.mult), ['r_t', 'f2'], ['fP'])
                                V(lambda e: e.tensor_tensor(out=fP[:], in0=fP[:], in1=rkb[:], op=ALU.mult), ['fP', 'rkb'], ['fP'])
                                if dr == 0:
                                    V(lambda e: e.tensor_reduce(out=bon[:, tt, :], in_=fP[:].rearrange("p (h d) -> p h d", h=16),
                                                                axis=AX.X, op=ALU.add), ['fP'], ['bon'])
                                else:
                                    V(lambda e: e.tensor_reduce(out=n16[:, 3, :], in_=fP[:].rearrange("p (h d) -> p h d", h=16),
                                                                axis=AX.X, op=ALU.add), ['fP'], ['n16b'])
                                    V(lambda e: e.tensor_tensor(out=bon[:, tt, :], in0=bon[:, tt, :], in1=n16[:, 3, :], op=ALU.add),
                                      ['bon', 'n16b'], ['bon'])
                                V(lambda e: e.tensor_tensor(out=f1[:], in0=k_t[:], in1=kkb[:], op=ALU.mult), ['k_t', 'kkb'], ['f1'])
                                A(lambda e: e.activation(out=f2[:], in_=f1[:], func=AF.Square), ['f1'], ['f2'])
                                V(lambda e: e.tensor_reduce(out=n16[:, 0, :], in_=f2[:].rearrange("p (h d) -> p h d", h=16),
                                                            axis=AX.X, op=ALU.add), ['f2'], ['n16'])
                                A(lambda e: e.activation(out=n16[:, 1, :], in_=n16[:, 0, :], func=AF.Sqrt), ['n16'], ['n16'])
                                V(lambda e: e.tensor_scalar(out=n16[:, 1, :], in0=n16[:, 1, :], scalar1=1e-12, scalar2=None,
                                                            op0=ALU.max), ['n16'], ['n16'])
                                V(lambda e: e.reciprocal(out=n16[:, 2, :], in_=n16[:, 1, :]), ['n16'], ['n16'])
                                V(lambda e: e.tensor_tensor(out=f1[:].rearrange("p (h d) -> p h d", h=16),
                                                            in0=f1[:].rearrange("p (h d) -> p h d", h=16),
                                                            in1=n16[:, 2, :].unsqueeze(2).to_broadcast([128, 16, 64]),
                                                            op=ALU.mult), ['f1', 'n16'], ['f1'])
                                V(lambda e: e.tensor_tensor(out=f2[:], in0=f1[:], in1=a_t[:], op=ALU.mult), ['f1', 'a_t'], ['f2'])
                                V(lambda e: e.tensor_tensor(out=bt_tok[:], in0=f2[:], in1=fPi[:], op=ALU.mult), ['f2', 'fPi'], ['bt_tok'])
                                V(lambda e: e.scalar_tensor_tensor(out=at_tok[:], in0=f1[:], scalar=-1.0, in1=f3[:],
                                                                   op0=ALU.mult, op1=ALU.mult), ['f1', 'f3'], ['at_tok'])
                                nb = 0
                                for (src, skey, dstf, dkey) in ((rt_tok, 'rt_tok', lambda hq: arT[:, hq * 8:(hq + 1) * 8, 1, :], 'arT'),
                                                                (kt_tok, 'kt_tok', lambda hq: ktT[:, hq * 8:(hq + 1) * 8, :], 'ktT'),
                                                                (bt_tok, 'bt_tok', lambda hq: btT[:, hq * 8:(hq + 1) * 8, :], 'btT'),
                                                                (at_tok, 'at_tok', lambda hq: arT[:, hq * 8:(hq + 1) * 8, 0, :], 'arT')):
                                    for hq in range(2):
                                        bank = 3 + (nb % 2)
                                        nb += 1
                                        pv = psbf(bank)
                                        for h8 in range(8):
                                            h = hq * 8 + h8
                                            T(lambda e: e.transpose(out=pv[0:64, h8 * 128:(h8 + 1) * 128], in_=src[:, h * 64:(h + 1) * 64],
                                                                    identity=identb[:]), [skey, 'identb'], [PS[bank]])
                                        A(lambda e: e.copy(out=dstf(hq), in_=pv[0:64, :].rearrange("p (a t) -> p a t", a=8)),
                                          [PS[bank]], [dkey])
                                cMU = cstR[:, C_MU - CR0:C_MU - CR0 + 512]
                                cML = cstR[:, C_ML - CR0:C_ML - CR0 + 512]
                                mA = (cMU if dr == 0 else cML).rearrange("p (a t) -> p a t", a=4)
                                mAT = (cML if dr == 0 else cMU).rearrange("p (a t) -> p a t", a=4)
                                def head_chain(h, s_):
                                    bA = s_
                                    ka4, kat4, knb, ktb = f'A4{s_}', f'AT4{s_}', f'NB{s_}', f'Tb{s_}'
                                    buf = NB[s_]
                                    T(lambda e: e.matmul(psb[bA][:, 0:256], lhsT=btT[:, h, :], rhs=arT[:, h, :, :], start=True, stop=True),
                                      ['btT', 'arT'], [PS[bA]])
                                    T(lambda e: e.matmul(psb[bA][:, 256:512], lhsT=ktT[:, h, :], rhs=arT[:, h, :, :], start=True, stop=True),
                                      ['ktT', 'arT'], [PS[bA]])
                                    V(lambda e: e.tensor_tensor(out=Am[:, h, :], in0=psb[bA][:], in1=mask4, op=ALU.mult),
                                      [PS[bA], 'cst'], [f'Am{h}'])
                                    V(lambda e: e.tensor_tensor(out=A4[s_][:], in0=psb[bA][:, 0:128].unsqueeze(1).to_broadcast([128, 4, 128]),
                                                                in1=mA, op=ALU.mult), [PS[bA], 'cst'], [ka4])
                                    yield
                                    T(lambda e: e.matmul(psb[s_][:, 0:128], lhsT=arT[:, h, 0, :], rhs=btT[:, h, :],
                                                         start=True, stop=True), ['arT', 'btT'], [PS[s_]])
                                    yield
                                    V(lambda e: e.tensor_tensor(out=AT4[s_][:], in0=psb[s_][:, 0:128].unsqueeze(1).to_broadcast([128, 4, 128]),
                                                                in1=mAT, op=ALU.mult), [PS[s_], 'cst'], [kat4])
                                    G(lambda e: e.tensor_copy(out=buf[:, 1:3, :], in_=identb[:].unsqueeze(1).to_broadcast([128, 2, 128])),
                                      ['identb'], [knb])
                                    G(lambda e: e.tensor_copy(out=buf[:, 0, :], in_=A4[s_][:, 0, :]), [ka4], [knb])
                                    G(lambda e: e.tensor_copy(out=buf[:, 3, :], in_=AT4[s_][:, 0, :]), [kat4], [knb])
                                    yield
                                    for it in range(4):
                                        T(lambda e: e.matmul(psb[s_][:, 0:256], lhsT=buf[:, 3, :], rhs=buf[:, 0:2, :], start=True, stop=True),
                                          [knb], [PS[s_]])
                                        T(lambda e: e.matmul(psb[s_][:, 256:512], lhsT=buf[:, 0, :], rhs=buf[:, 2:4, :], start=True, stop=True),
                                          [knb], [PS[s_]])
                                        yield
                                        V(lambda e: e.tensor_tensor(out=buf[:, 1:3, :], in0=buf[:, 1:3, :],
                                                                    in1=psb[s_][:, 128:384].rearrange("p (a t) -> p a t", a=2), op=ALU.add),
                                          [knb, PS[s_]], [knb])
                                        if it < 3:
                                            A(lambda e: e.copy(out=buf[:, 0, :], in_=psb[s_][:, 0:128]), [PS[s_]], [knb])
                                            A(lambda e: e.copy(out=buf[:, 3, :], in_=psb[s_][:, 384:512]), [PS[s_]], [knb])
                                        yield
                                    for lv in range(1, 4):
                                        lastk = (lv == 3)
                                        T(lambda e: e.matmul(psb[s_][:, 0:128], lhsT=AT4[s_][:, lv, :], rhs=buf[:, 1, :], start=True, stop=True),
                                          [kat4, knb], [PS[s_]])
                                        if not lastk:
                                            T(lambda e: e.matmul(psb[s_][:, 128:256], lhsT=A4[s_][:, lv, :], rhs=buf[:, 2, :], start=True, stop=True),
                                              [ka4, knb], [PS[s_]])
                                        yield
                                        if not lastk:
                                            A(lambda e: e.copy(out=Tb[s_][:], in_=psb[s_][:, 0:256].rearrange("p (a t) -> p a t", a=2)),
                                              [PS[s_]], [ktb])
                                        else:
                                            A(lambda e: e.copy(out=Tb[s_][:, 0, :], in_=psb[s_][:, 0:128]), [PS[s_]], [ktb])
                                        yield
                                        T(lambda e: e.matmul(psb[s_][:, 256:384], lhsT=buf[:, 2, :], rhs=Tb[s_][:, 0, :], start=True, stop=True),
                                          [knb, ktb], [PS[s_]])
                                        if not lastk:
                                            T(lambda e: e.matmul(psb[s_][:, 384:512], lhsT=buf[:, 1, :], rhs=Tb[s_][:, 1, :], start=True, stop=True),
                                              [knb, ktb], [PS[s_]])
                                        yield
                                        if not lastk:
                                            V(lambda e: e.tensor_tensor(out=buf[:, 1:3, :], in0=buf[:, 1:3, :],
                                                                        in1=psb[s_][:, 256:512].rearrange("p (a t) -> p a t", a=2), op=ALU.add),
                                              [knb, PS[s_]], [knb])
                                        else:
                                            V(lambda e: e.tensor_tensor(out=N_all[:, h, :], in0=buf[:, 1, :], in1=psb[s_][:, 256:384], op=ALU.add),
                                              [knb, PS[s_]], [f'N{h}'])
                                for grp in range(2):
                                    gens = [head_chain(grp * 8 + q8, q8) for q8 in range(8)]
                                    while gens:
                                        alive = []
                                        for gch in gens:
                                            try:
                                                next(gch)
                                                alive.append(gch)
                                            except StopIteration:
                                                pass
                                        gens = alive
                                amk = [f'Am{h}' for h in range(16)]
                                for h in range(16):
                                    o = psb[h // 8][:, (h % 8) * 64:(h % 8 + 1) * 64]
                                    T(lambda e: e.matmul(o, lhsT=arT[:, h, 0, :], rhs=Sb[:, h, :], start=True, stop=False),
                                      ['arT', 'Sb'], [PS[h // 8]])
                                    T(lambda e: e.matmul(o, lhsT=Am[:, h, 256:384], rhs=vb[:, h * 64:(h + 1) * 64], start=False, stop=True),
                                      [f'Am{h}', 'vb'], [PS[h // 8]])
                                for hq in range(2):
                                    A(lambda e: e.copy(out=Xb[:, hq * 512:(hq + 1) * 512], in_=psb[hq][:]), [PS[hq]], ['Xb'])
                                for h in range(16):
                                    o = psb[2 + h // 8][:, (h % 8) * 64:(h % 8 + 1) * 64]
                                    T(lambda e: e.matmul(o, lhsT=N_all[:, h, :], rhs=Xb[:, h * 64:(h + 1) * 64], start=True, stop=True),
                                      [f'N{h}', 'Xb'], [PS[2 + h // 8]])
                                for hq in range(2):
                                    V(lambda e: e.tensor_copy(out=Ub[:, hq * 512:(hq + 1) * 512], in_=psb[2 + hq][:]), [PS[2 + hq]], ['Ub'])
                                for h in range(16):
                                    o = psb[4 + h // 8][:, (h % 8) * 64:(h % 8 + 1) * 64]
                                    hs = slice(h * 64, (h + 1) * 64)
                                    T(lambda e: e.matmul(o, lhsT=arT[:, h, 1, :], rhs=Sb[:, h, :], start=True, stop=False),
                                      ['arT', 'Sb'], [PS[4 + h // 8]])
                                    T(lambda e: e.matmul(o, lhsT=Am[:, h, 128:256], rhs=Ub[:, hs], start=False, stop=False),
                                      [f'Am{h}', 'Ub'], [PS[4 + h // 8]])
                                    T(lambda e: e.matmul(o, lhsT=Am[:, h, 384:512], rhs=vb[:, hs], start=False, stop=True),
                                      [f'Am{h}', 'vb'], [PS[4 + h // 8]])
                                for h in range(16):
                                    o = psb[6 + h // 8][0:64, (h % 8) * 64:(h % 8 + 1) * 64]
                                    hs = slice(h * 64, (h + 1) * 64)
                                    T(lambda e: e.matmul(o, lhsT=bt_tok[:, hs], rhs=Ub[:, hs], start=True, stop=False),
                                      ['bt_tok', 'Ub'], [PS[6 + h // 8]])
                                    T(lambda e: e.matmul(o, lhsT=kt_tok[:, hs], rhs=vb[:, hs], start=False, stop=True),
                                      ['kt_tok', 'vb'], [PS[6 + h // 8]])
                                for hq in range(2):
                                    V(lambda e: e.tensor_tensor(out=S_T[:, hq * 8:(hq + 1) * 8, :], in0=S_T[:, hq * 8:(hq + 1) * 8, :],
                                                                in1=psb[6 + hq][0:64, :].rearrange("p (h i) -> p h i", h=8), op=ALU.add),
                                      ['S_T', PS[6 + hq]], ['S_T'])
                                V(lambda e: e.tensor_tensor(out=S_T[:], in0=S_T[:], in1=PC[:].unsqueeze(2).to_broadcast([64, 16, 64]),
                                                            op=ALU.mult), ['S_T', 'PC'], ['S_T'])
                                A(lambda e: e.copy(out=Sb[:], in_=S_T[:]), ['S_T'], ['Sb'])
                                if dr == 0:
                                    for hq in range(2):
                                        A(lambda e: e.copy(out=yt[:, hq * 512:(hq + 1) * 512], in_=psb[4 + hq][:]), [PS[4 + hq]], ['yt'])
                                else:
                                    kb.dma('sp', yt[:], yf_s[rows, :], reads=['yfs'], writes=['yt'])
                                    for hq in range(2):
                                        V(lambda e: e.tensor_tensor(out=yt[:, hq * 512:(hq + 1) * 512], in0=yt[:, hq * 512:(hq + 1) * 512],
                                                                    in1=psb[4 + hq][:], op=ALU.add), ['yt', PS[4 + hq]], ['yt'])
                                kb.dma('sp', yf_s[rows, :], yt[:], reads=['yt'], writes=['yfs'])
                            if not sample:
                                for h in range(16):
                                    T(lambda e: e.transpose(out=psb[h // 8][0:64, (h % 8) * 64:(h % 8 + 1) * 64], in_=S_T[:, h, :],
                                                            identity=cst[0:64, C_ID:C_ID + 64]), ['S_T', 'cst'], [PS[h // 8]])
                                for hq in range(2):
                                    V(lambda e: e.tensor_copy(out=stl[:, hq * 8:(hq + 1) * 8, :],
                                                              in_=psb[hq][0:64, :].rearrange("p (h i) -> p h i", h=8)),
                                      [PS[hq]], ['stl'])
                                kb.dma('sp', nst_d[sq_i, j, dr].rearrange("h i j -> i h j"), stl[:], reads=['stl'], writes=['nst'])

                with kb.phase():
                    modb, lnb = setup_mod(i, 0, want=(2,), ln=True)
                    wk = post_work()
                    wo = kb.sb('wo', [128, 8, D], BF16)
                    for c in range(8):
                        load_cast(wo[:, c, :], rwo_d[j, c * 128:(c + 1) * 128, :], w=['wo'])
                    lxg = kb.sb('lxg', [128, D])
                    lxb = kb.sb('lxb', [128, D])
                    kb.dma('sp', lxg[:], lnxg_d[j:j + 1, :].partition_broadcast(128), writes=['lxg'])
                    kb.dma('sp', lxb[:], lnxb_d[j:j + 1, :].partition_broadcast(128), writes=['lxb'])
                    y_t = kb.sb('y_t', [128, D])
                    v_t = kb.sb('v_t', [128, D])
                    g_t = kb.sb('g_t', [128, D])
                    f1 = kb.sb('f1', [128, D])
                    m16 = kb.sb('m16', [128, 6, 16])
                    zb = kb.sb('zb', [128, D], BF16)
                    zT = kb.sb('zT', [128, 8, 128], BF16)
                    for tt in range(NT):
                        rows = slice(tt * 128, (tt + 1) * 128)
                        kb.dma('sp', y_t[:], yf_s[rows, :], writes=['y_t'])
                        kb.dma('sp', v_t[:], v_s[rows, :], writes=['v_t'])
                        kb.dma('sp', g_t[:], g_s[rows, :], writes=['g_t'])
                        y3 = y_t[:].rearrange("p (h d) -> p h d", h=16)
                        V(lambda e: e.tensor_reduce(out=m16[:, 0, :], in_=y3, axis=AX.X, op=ALU.add), ['y_t'], ['m16'])
                        V(lambda e: e.tensor_scalar(out=m16[:, 0, :], in0=m16[:, 0, :], scalar1=1.0 / 64, scalar2=None, op0=ALU.mult),
                          ['m16'], ['m16'])
                        V(lambda e: e.tensor_tensor(out=y3, in0=y3, in1=m16[:, 0, :].unsqueeze(2).to_broadcast([128, 16, 64]),
                                                    op=ALU.subtract), ['y_t', 'm16'], ['y_t'])
                        A(lambda e: e.activation(out=f1[:], in_=y_t[:], func=AF.Square), ['y_t'], ['f1'])
                        V(lambda e: e.tensor_reduce(out=m16[:, 1, :], in_=f1[:].rearrange("p (h d) -> p h d", h=16), axis=AX.X,
                                                    op=ALU.add), ['f1'], ['m16'])
                        V(lambda e: e.tensor_scalar(out=m16[:, 1, :], in0=m16[:, 1, :], scalar1=1.0 / 64, scalar2=GN_EPS,
                                                    op0=ALU.mult, op1=ALU.add), ['m16'], ['m16'])
                        A(lambda e: e.activation(out=m16[:, 2, :], in_=m16[:, 1, :], func=AF.Sqrt), ['m16'], ['m16'])
                        V(lambda e: e.reciprocal(out=m16[:, 3, :], in_=m16[:, 2, :]), ['m16'], ['m16'])
                        V(lambda e: e.tensor_tensor(out=y3, in0=y3, in1=m16[:, 3, :].unsqueeze(2).to_broadcast([128, 16, 64]),
                                                    op=ALU.mult), ['y_t', 'm16'], ['y_t'])
                        V(lambda e: e.tensor_tensor(out=y_t[:], in0=y_t[:], in1=lxg[:], op=ALU.mult), ['y_t', 'lxg'], ['y_t'])
                        V(lambda e: e.tensor_tensor(out=y_t[:], in0=y_t[:], in1=lxb[:], op=ALU.add), ['y_t', 'lxb'], ['y_t'])
                        V(lambda e: e.tensor_tensor(out=f1[:].rearrange("p (h d) -> p h d", h=16),
                                                    in0=v_t[:].rearrange("p (h d) -> p h d", h=16),
                                                    in1=bon[:, tt, :].unsqueeze(2).to_broadcast([128, 16, 64]), op=ALU.mult),
                          ['v_t', 'bon'], ['f1'])
                        V(lambda e: e.tensor_tensor(out=y_t[:], in0=y_t[:], in1=f1[:], op=ALU.add), ['y_t', 'f1'], ['y_t'])
                        V(lambda e: e.tensor_tensor(out=zb[:], in0=y_t[:], in1=g_t[:], op=ALU.mult), ['y_t', 'g_t'], ['zb'])
                        transpose8(zb, 'zb', zT[:], 'zT', 7)
                        for half in range(2):
                            for c in range(8):
                                T(lambda e: e.matmul(psb[half][:], lhsT=zT[:, c, :], rhs=wo[:, c, half * 512:(half + 1) * 512],
                                                     start=(c == 0), stop=(c == 7)), ['zT', 'wo'], [PS[half]])
                        post_sublayer(tt, (0, 1), modb, lnb, wk)
                kb.st = old_st

        def rope_apply(src, skey, H, dst, dkey, ropet, tmp):
            xv = src.rearrange("p (h a s f) -> p h a s f", h=H, a=2, s=2)
            dv = dst.rearrange("p (h a s f) -> p h a s f", h=H, a=2, s=2)
            x1, x2 = xv[:, :, :, 0, :], xv[:, :, :, 1, :]
            d1, d2 = dv[:, :, :, 0, :], dv[:, :, :, 1, :]
            cosb = ropet[:, 0:32].rearrange("p (a f) -> p a f", a=2).unsqueeze(1).to_broadcast([128, H, 2, 16])
            sinb = ropet[:, 32:64].rearrange("p (a f) -> p a f", a=2).unsqueeze(1).to_broadcast([128, H, 2, 16])
            t1 = tmp[0][:, 0:H * 32].rearrange("p (h a f) -> p h a f", h=H, a=2)
            t2 = tmp[1][:, 0:H * 32].rearrange("p (h a f) -> p h a f", h=H, a=2)
            V(lambda e: e.tensor_tensor(out=t1, in0=x1, in1=cosb, op=ALU.mult), [skey, 'ropet'], ['rt1'])
            V(lambda e: e.tensor_tensor(out=t2, in0=x2, in1=sinb, op=ALU.mult), [skey, 'ropet'], ['rt2'])
            V(lambda e: e.tensor_tensor(out=d1, in0=t1, in1=t2, op=ALU.subtract), ['rt1', 'rt2'], [dkey])
            V(lambda e: e.tensor_tensor(out=t1, in0=x2, in1=cosb, op=ALU.mult), [skey, 'ropet'], ['rt1'])
            V(lambda e: e.tensor_tensor(out=t2, in0=x1, in1=sinb, op=ALU.mult), [skey, 'ropet'], ['rt2'])
            V(lambda e: e.tensor_tensor(out=d2, in0=t1, in1=t2, op=ALU.add), ['rt1', 'rt2'], [dkey])

        def attn_sublayer(i):
            j = i // 2
            with kb.phase():
                modb, lnb = setup_mod(i, 0)
                wk = post_work()
                wqkv = kb.sb('wqkv', [128, 8, 1536], BF16)
                for c in range(8):
                    load_cast(wqkv[:, c, :], wqkv_d[j, c * 128:(c + 1) * 128, :], w=['wqkv'])
                wo = kb.sb('wo', [128, 8, D], BF16)
                for c in range(8):
                    load_cast(wo[:, c, :], awo_d[j, c * 128:(c + 1) * 128, :], w=['wo'])
                qnb = kb.sb('qnb', [128, 64])
                knb = kb.sb('knb', [128, 64])
                kb.dma('sp', qnb[:], qn_d[j:j + 1, :].partition_broadcast(128), writes=['qnb'])
                kb.dma('sp', knb[:], kn_d[j:j + 1, :].partition_broadcast(128), writes=['knb'])
                hT = kb.sb('hT', [128, 8, NTOK], BF16)
                kT = kb.sb('kT', [64, 4, 1536], BF16)
                Vx = kb.sb('Vx', [128, 12, 4, 65], BF16)
                htok = kb.sb('htok', [128, D])
                hb = kb.sb('hb', [128, D], BF16)
                ckt = kb.sb('ckt', [128, 256], BF16)
                sq = htok
                ss = kb.sb('ss', [128, 3, 16])
                qf = kb.sb('qf', [128, D])
                qb = kb.sb('qb', [128, D], BF16)
                kf = kb.sb('kf', [128, 256])
                vf = kb.sb('vf', [128, 256])
                kbb = kb.sb('kbb', [128, 256], BF16)
                ropet = kb.sb('ropet', [128, 64])
                rtmp = [kb.sb(f'rtmp{b}', [128, 256]) for b in range(2)]
                qT = kb.sb('qT', [64, 16, 128], BF16)
                PT = [kb.sb(f'PT{b}', [128, 512], BF16) for b in range(2)]
                rc4 = kb.sb('rc4', [128, 4])
                ob = kb.sb('ob', [128, D], BF16)
                oT = kb.sb('oT', [128, 8, 128], BF16)
                V(lambda e: e.memset(Vx[:, :, :, 64:65], 1.0), [], ['Vx'])

                def rms(src_ap_list, keys, H, gb, out_f, okey):
                    off = 0
                    for ap, kk_ in zip(src_ap_list, keys):
                        w_ = ap.shape[-1]
                        A(lambda e: e.activation(out=sq[:, off:off + w_], in_=ap, func=AF.Square), [kk_], ['htok'])
                        off += w_
                    V(lambda e: e.tensor_reduce(out=ss[:, 0, 0:H], in_=sq[:, 0:H * 64].rearrange("p (h d) -> p h d", h=H),
                                                axis=AX.X, op=ALU.add), ['htok'], ['ss'])
                    V(lambda e: e.tensor_scalar(out=ss[:, 0, 0:H], in0=ss[:, 0, 0:H], scalar1=1.0 / 64, scalar2=RMS_EPS,
                                                op0=ALU.mult, op1=ALU.add), ['ss'], ['ss'])
                    A(lambda e: e.activation(out=ss[:, 1, 0:H], in_=ss[:, 0, 0:H], func=AF.Sqrt), ['ss'], ['ss'])
                    V(lambda e: e.reciprocal(out=ss[:, 2, 0:H], in_=ss[:, 1, 0:H]), ['ss'], ['ss'])
                    off = 0
                    for ap, kk_ in zip(src_ap_list, keys):
                        w_ = ap.shape[-1]
                        hh = w_ // 64
                        h0 = off // 64
                        V(lambda e: e.tensor_tensor(out=out_f[:, off:off + w_].rearrange("p (h d) -> p h d", h=hh),
                                                    in0=ap.rearrange("p (h d) -> p h d", h=hh),
                                                    in1=ss[:, 2, h0:h0 + hh].unsqueeze(2).to_broadcast([128, hh, 64]),
                                                    op=ALU.mult), [kk_, 'ss'], [okey])
                        off += w_
                    V(lambda e: e.tensor_tensor(out=out_f[:, 0:H * 64].rearrange("p (h d) -> p h d", h=H),
                                                in0=out_f[:, 0:H * 64].rearrange("p (h d) -> p h d", h=H),
                                                in1=gb[:].unsqueeze(1).to_broadcast([128, H, 64]), op=ALU.mult),
                      [okey, 'qnb', 'knb'], [okey])

                for (sq_i, tiles) in SEQS:
                    sample = (sq_i == 2)
                    kc0 = 4 if sample else 0
                    nch = kc0 + len(tiles)
                    if sample:
                        for kc in range(4):
                            load_cast(ckt[:], ck_d[j, kc * 128:(kc + 1) * 128, :], w=['ckt'])
                            pv = psbf(7)
                            for kv in range(4):
                                T(lambda e: e.transpose(out=pv[0:64, kv * 128:(kv + 1) * 128], in_=ckt[:, kv * 64:(kv + 1) * 64],
                                                        identity=identb[:]), ['ckt', 'identb'], [PS[7]])
                            A(lambda e: e.copy(out=kT[:, :, kc * 128:(kc + 1) * 128],
                                               in_=pv[0:64, 0:512].rearrange("p (a t) -> p a t", a=4)), [PS[7]], ['kT'])
                            load_cast(Vx[:, kc, :, 0:64], cv_d[j, kc * 128:(kc + 1) * 128, :].rearrange("p (a d) -> p a d", a=4),
                                      w=['Vx'])
                    for lt, tt in enumerate(tiles):
                        kc = kc0 + lt
                        make_h(tt, modb, htok[:], 'htok')
                        A(lambda e: e.copy(out=hb[:], in_=htok[:]), ['htok'], ['hb'])
                        transpose8(hb, 'hb', hT[:, :, tt * 128:(tt + 1) * 128], 'hT', 7)
                        for c in range(8):
                            T(lambda e: e.matmul(psb[6][:], lhsT=hT[:, c, tt * 128:(tt + 1) * 128], rhs=wqkv[:, c, 1024:1536],
                                                 start=(c == 0), stop=(c == 7)), ['hT', 'wqkv'], [PS[6]])
                        rms([psb[6][:, 0:256]], [PS[6]], 4, knb, kf, 'kf')
                        if sample:
                            kb.dma('sp', ropet[:], rope_d[(tt - 4) * 128:(tt - 3) * 128, :], writes=['ropet'])
                            rope_apply(kf[:], 'kf', 4, kbb[:], 'kbb', ropet, rtmp)
                        else:
                            kb.dma('sp', nk_d[sq_i, j, lt * 128:(lt + 1) * 128, :], kf[:], reads=['kf'], writes=['nk'])
                            A(lambda e: e.copy(out=vf[:], in_=psb[6][:, 256:512]), [PS[6]], ['vf'])
                            kb.dma('sp', nv_d[sq_i, j, lt * 128:(lt + 1) * 128, :], vf[:], reads=['vf'], writes=['nv'])
                            V(lambda e: e.tensor_copy(out=kbb[:], in_=kf[:]), ['kf'], ['kbb'])
                        A(lambda e: e.copy(out=Vx[:, kc, :, 0:64], in_=psb[6][:, 256:512].rearrange("p (a d) -> p a d", a=4)),
                          [PS[6]], ['Vx'])
                        pv = psbf(7)
                        for kv in range(4):
                            T(lambda e: e.transpose(out=pv[0:64, kv * 128:(kv + 1) * 128], in_=kbb[:, kv * 64:(kv + 1) * 64],
                                                    identity=identb[:]), ['kbb', 'identb'], [PS[7]])
                        A(lambda e: e.copy(out=kT[:, :, kc * 128:(kc + 1) * 128],
                                           in_=pv[0:64, 0:512].rearrange("p (a t) -> p a t", a=4)), [PS[7]], ['kT'])
                    nsc = 0
                    for lt, tt in enumerate(tiles):
                        for half in range(2):
                            for c in range(8):
                                T(lambda e: e.matmul(psb[half][:], lhsT=hT[:, c, tt * 128:(tt + 1) * 128],
                                                     rhs=wqkv[:, c, half * 512:(half + 1) * 512],
                                                     start=(c == 0), stop=(c == 7)), ['hT', 'wqkv'], [PS[half]])
                        rms([psb[0][:], psb[1][:]], [PS[0], PS[1]], 16, qnb, qf, 'qf')
                        if sample:
                            kb.dma('sp', ropet[:], rope_d[(tt - 4) * 128:(tt - 3) * 128, :], writes=['ropet'])
                            for hh in range(2):
                                rope_apply(qf[:, hh * 512:(hh + 1) * 512], 'qf', 8, qb[:, hh * 512:(hh + 1) * 512], 'qb',
                                           ropet, rtmp)
                        else:
                            V(lambda e: e.tensor_copy(out=qb[:], in_=qf[:]), ['qf'], ['qb'])
                        for hh in range(2):
                            pv = psbf(2 + hh)
                            for h8 in range(8):
                                h = hh * 8 + h8
                                T(lambda e: e.transpose(out=pv[0:64, h8 * 128:(h8 + 1) * 128], in_=qb[:, h * 64:(h + 1) * 64],
                                                        identity=identb[:]), ['qb', 'identb'], [PS[2 + hh]])
                            A(lambda e: e.copy(out=qT[:, hh * 8:(hh + 1) * 8, :],
                                               in_=pv[0:64, :].rearrange("p (a t) -> p a t", a=8)), [PS[2 + hh]], ['qT'])
                        work = [(kv, kc) for kv in range(4) for kc in range(nch)]

                        def emit_score(n):
                            kv, kc = work[n]
                            sbk = 2 + (n % 2)
                            T(lambda e: e.matmul(psb[sbk][:], lhsT=kT[:, kv, kc * 128:(kc + 1) * 128],
                                                 rhs=qT[:, kv * 4:(kv + 1) * 4, :], start=True, stop=True),
                              ['kT', 'qT'], [PS[sbk]])

                        emit_score(0)
                        for n, (kv, kc) in enumerate(work):
                            if n + 1 < len(work):
                                emit_score(n + 1)
                            sbk = 2 + (n % 2)
                            pt = PT[n % 2]
                            ptk = f'PT{n % 2}'
                            A(lambda e: e.activation(out=pt[:], in_=psb[sbk][:], func=AF.Exp, scale=ATTN_SCALE),
                              [PS[sbk]], [ptk])
                            for g in range(4):
                                T(lambda e: e.matmul(psb[4 + kv][:, g * 65:(g + 1) * 65], lhsT=pt[:, g * 128:(g + 1) * 128],
                                                     rhs=Vx[:, kc, kv, :], start=(kc == 0), stop=(kc == nch - 1)),
                                  [ptk, 'Vx'], [PS[4 + kv]])
                            if kc == nch - 1:
                                po = psb[4 + kv][:, 0:260].rearrange("p (g d) -> p g d", g=4)
                                V(lambda e: e.reciprocal(out=rc4[:], in_=po[:, :, 64]), [PS[4 + kv]], ['rc4'])
                                V(lambda e: e.tensor_tensor(out=ob[:, kv * 256:(kv + 1) * 256].rearrange("p (g d) -> p g d", g=4),
                                                            in0=po[:, :, 0:64], in1=rc4[:].unsqueeze(2).to_broadcast([128, 4, 64]),
                                                            op=ALU.mult), [PS[4 + kv], 'rc4'], ['ob'])
                        transpose8(ob, 'ob', oT[:], 'oT', 2)
                        for half in range(2):
                            for c in range(8):
                                T(lambda e: e.matmul(psb[half][:], lhsT=oT[:, c, :], rhs=wo[:, c, half * 512:(half + 1) * 512],
                                                     start=(c == 0), stop=(c == 7)), ['oT', 'wo'], [PS[half]])
                        post_sublayer(tt, (0, 1), modb, lnb, wk)

        done_ada = set()
        for (i, which) in subs:
            if i not in done_ada:
                if (i, 1) in subs:
                    peer_convert(i)
                adaln(i)
                done_ada.add(i)
            if which == 1:
                peer_sublayer(i)
            else:
                (rwkv_sublayer if i % 2 == 0 else attn_sublayer)(i)

        for tt in range(NT):
            kb.dma('sp', y_d[tt * 128:(tt + 1) * 128, :], x_res[:, tt, :], reads=[f'x{tt}'], writes=['y'])
        kb.barrier()
        print("ninst", kb.ninst, flush=True)
        _LAST_KB['kb'] = kb
    return nc


_LAST_KB = {}


def _consts():
    c = np.zeros((128, NCST), np.float32)
    s = np.arange(128)[:, None]
    t = np.arange(128)[None, :]
    c[:, C_ID:C_ID + 128] = np.eye(128)
    c[:, C_TRIF:C_TRIF + 128] = (s <= t)
    c[:, C_TRIB:C_TRIB + 128] = (s >= t)
    strictF, inclF = (s < t), (s <= t)
    strictB, inclB = (s > t), (s >= t)
    c[:, C_M4F:C_M4F + 512] = np.concatenate([strictF, inclF, strictF, inclF], 1)
    c[:, C_M4B:C_M4B + 512] = np.concatenate([strictB, inclB, strictB, inclB], 1)
    idx = np.arange(128)
    def bm(b):
        return (idx[:, None] // b == idx[None, :] // b)
    blocks = [bm(16), bm(32) & ~bm(16), bm(64) & ~bm(32), ~bm(64)]
    c[:, C_MU:C_MU + 512] = np.concatenate([strictF & b_ for b_ in blocks], 1)
    c[:, C_ML:C_ML + 512] = np.concatenate([strictB & b_ for b_ in blocks], 1)
    c[:, C_IOTA:C_IOTA + 16] = np.arange(16)[None, :]
    c[:, C_ONES:C_ONES + 128] = 1.0
    sel2 = np.zeros((2, 256), np.float32)
    sel2[0, :128] = 1.0
    sel2[1, 128:] = 1.0
    n = np.arange(1024)
    row = (n // 64).astype(np.float32)
    col = (n % 64).astype(np.float32)
    freqs = (10000.0 ** (-np.arange(16, dtype=np.float32) / 16)).astype(np.float32)
    ang = np.stack([row[:, None] * freqs, col[:, None] * freqs], axis=1).astype(np.float32)
    rope = np.concatenate([np.cos(ang).reshape(1024, 32), np.sin(ang).reshape(1024, 32)], 1).astype(np.float32)
    return c, sel2, rope


def prep_inputs(inputs, x_override=None):
    f = lambda a: np.ascontiguousarray(np.asarray(a, dtype=np.float32))
    I = {k: np.asarray(v) for k, v in inputs.items()}
    cst, sel2, rope = _consts()
    shared = {
        "ada_w": f(I['ada_w']), "ada_b": f(I['ada_b']), "ln_g": f(I['ln_g']), "ln_b": f(I['ln_b']),
        "muT": f(I['rwkv_mu'].reshape(2, 6, 8, 128).transpose(0, 3, 1, 2)),
        "wrkv": f(I['rwkv_wrkv']), "rwo": f(I['rwkv_wo']), "w0": f(I['rwkv_w0']),
        "w1c": f(I['rwkv_w1'].transpose(0, 2, 1, 3).reshape(2, 1024, 128)),
        "w2c": f(I['rwkv_w2'].reshape(2, 128, 1024)),
        "a0": f(I['rwkv_a0']),
        "a1c": f(I['rwkv_a1'].transpose(0, 2, 1, 3).reshape(2, 1024, 128)),
        "a2c": f(I['rwkv_a2'].reshape(2, 128, 1024)),
        "g1": f(I['rwkv_g1']), "g2": f(I['rwkv_g2']),
        "rkk": f(I['rwkv_kk']), "rka": f(I['rwkv_ka']), "rrk": f(I['rwkv_rk'].reshape(2, 1024)),
        "lnxg": f(I['rwkv_lnx_g']), "lnxb": f(I['rwkv_lnx_b']),
        "wqkv": f(I['attn_wqkv']), "awo": f(I['attn_wo']), "qn": f(I['attn_qn']), "kn": f(I['attn_kn']),
        "pwq": f(I['peer_wq']), "pkT": f(I['peer_keys'].transpose(0, 1, 3, 2)),
        "cst": cst, "sel2": sel2, "rope": rope,
    }
    for i in range(4):
        shared[f"pu{i}"] = f(I['peer_u'][i])
        shared[f"pv{i}"] = f(I['peer_v'][i])
    in_maps = []
    for c in range(8):
        m = dict(shared)
        if x_override is not None:
            xp, xs = x_override
        else:
            xp, xs = I['x_prompt'], I['x_sample']
        m["xin"] = f(np.concatenate([xp[2 * c], xp[2 * c + 1], xs[c]], 0))
        m["cond"] = f(np.stack([I['c_ctx'], I['c'][c]], 0))
        m["st_in"] = f(I['state_rwkv'][c])
        m["ck"] = f(I['cache_k'][c].reshape(2, 512, 256))
        m["cv"] = f(I['cache_v'][c].reshape(2, 512, 256))
        in_maps.append(m)
    return in_maps


def assemble(results):
    yp = np.zeros((16, 256, 1024), np.float32)
    ys = np.zeros((8, 1024, 1024), np.float32)
    nst = np.zeros((16, 2, 2, 16, 64, 64), np.float32)
    nk = np.zeros((16, 2, 256, 4, 64), np.float32)
    nv = np.zeros((16, 2, 256, 4, 64), np.float32)
    for c, r in enumerate(results):
        y = r["y"]
        yp[2 * c] = y[0:256]
        yp[2 * c + 1] = y[256:512]
        ys[c] = y[512:]
        nst[2 * c:2 * c + 2] = r["nst"]
        nk[2 * c:2 * c + 2] = r["nk"].reshape(2, 2, 256, 4, 64)
        nv[2 * c:2 * c + 2] = r["nv"].reshape(2, 2, 256, 4, 64)
    return yp, ys, nst, nk, nv


_NC_CACHE = {}


def kernel(**inputs):
    if 'nc' not in _NC_CACHE:
        _NC_CACHE['nc'] = build()
    nc = _NC_CACHE['nc']
    in_maps = prep_inputs(inputs)
    res = run_bass_kernel_spmd(nc, in_maps, core_ids=list(range(8)))
    return assemble(res.results)
```

```python
import contextlib
import numpy as np
import concourse.bass as bass
import concourse.mybir as mybir
from concourse.bass_utils import run_bass_kernel_spmd

F32 = mybir.dt.float32
BF16 = mybir.dt.bfloat16
I32 = mybir.dt.int32
U32 = mybir.dt.uint32
AF = mybir.ActivationFunctionType
ALU = mybir.AluOpType
AX = mybir.AxisListType

D = 1024
NT = 12
NTOK = 1536
DEPTH = 4
ALPHA = float((2 * DEPTH) ** 0.25)
LN_EPS = 1e-5
GN_EPS = 64 * 1e-5
RMS_EPS = 1e-6
ATTN_SCALE = 0.125
NEG_EXP_HALF = -float(np.exp(-0.5))
SEQS = [(0, [0, 1]), (1, [2, 3]), (2, [4, 5, 6, 7, 8, 9, 10, 11])]
HPAD = 1540

C_ID, C_IOTA, C_ONES = 0, 128, 144
CR0 = 272
C_TRIF, C_TRIB, C_M4F, C_M4B, C_MU, C_ML = 272, 400, 528, 1040, 1552, 2064
NCST = 2576


def padcol(tt):
    return tt * 128 + 1 + (1 if tt >= 2 else 0) + (1 if tt >= 4 else 0)


class KB:
    def __init__(self, nc, stack, n_dma_sems=4):
        self.nc = nc
        self.st = stack
        self.eng = {'pe': nc.tensor, 'dve': nc.vector, 'act': nc.scalar, 'pool': nc.gpsimd, 'sp': nc.sync}
        self.sem = {}
        self.cnt = {}
        for e in self.eng:
            self.sem[e] = stack.enter_context(nc.semaphore("s_" + e))
            self.cnt[e] = 0
        self.n_dma_sems = n_dma_sems
        self.dsem = {}
        self.dcnt = {}
        self.drr = {}
        self.nds = {'sp': 8, 'act': 1, 'pool': 16}
        for q in ('sp', 'act', 'pool'):
            self.drr[q] = 0
            for k in range(self.nds[q]):
                self.dsem[(q, k)] = stack.enter_context(nc.semaphore(f"d_{q}{k}"))
                self.dcnt[(q, k)] = 0
        self.bgsems = [stack.enter_context(nc.semaphore(f"bg{i}")) for i in range(32)]
        self.bgcnt = [0] * 32
        self.bgrr = 0
        self.bs_arrive = stack.enter_context(nc.semaphore("bs_arrive"))
        self.bs_go = stack.enter_context(nc.semaphore("bs_go"))
        self.nbar = 0
        self.seen = {e: {} for e in self.eng}
        self.lastw = {}
        self.readers = {}
        self.ninst = 0
        self.uid = 0
        self.rec = {e: [] for e in self.eng}

    def sb(self, name, shape, dt=F32):
        self.uid += 1
        return self.st.enter_context(self.nc.sbuf_tensor(f"{name}_{self.uid}", list(shape), dt))

    def ps(self, name, shape, dt=F32):
        return self.st.enter_context(self.nc.psum_tensor(name, list(shape), dt))

    EXCL = frozenset(f'ps{i}' for i in range(8))

    def _deps(self, reads, writes, e=None):
        deps = []
        for k in reads:
            if k in self.lastw:
                deps.append(self.lastw[k])
            if k in self.EXCL:
                deps.extend((sk, v) for sk, v in self.readers.get(k, {}).items() if sk != e)
        for k in writes:
            if k in self.lastw:
                deps.append(self.lastw[k])
            deps.extend(self.readers.get(k, {}).items())
        return deps

    def _wait(self, e, deps):
        best = {}
        for (sk, v) in deps:
            if v > best.get(sk, 0):
                best[sk] = v
        for sk, v in best.items():
            if self.seen[e].get(sk, 0) >= v:
                continue
            sem = self.sem[sk] if isinstance(sk, str) else self.dsem[sk]
            self.eng[e].wait_ge(sem, v)
            self.rec[e].append(('w', sk, v))
            self.seen[e][sk] = v

    def _record(self, tok, reads, writes):
        for k in reads:
            d = self.readers.setdefault(k, {})
            if tok[1] > d.get(tok[0], 0):
                d[tok[0]] = tok[1]
        for k in writes:
            self.lastw[k] = tok
            self.readers[k] = {}

    def op(self, e, fn, reads=(), writes=()):
        self._wait(e, self._deps(reads, writes, e))
        ins = fn(self.eng[e])
        self.cnt[e] += 1
        ins.then_inc(self.sem[e], 1)
        self.rec[e].append(('i', e, 1))
        self._record((e, self.cnt[e]), reads, writes)
        self.ninst += 1
        return ins

    def _dma_pre(self, q):
        k = self.drr[q]
        if self.dcnt[(q, k)] > 0:
            self._wait(q, [((q, k), self.dcnt[(q, k)])])

    def _dma_fin(self, q, ins, reads, writes):
        k = self.drr[q]
        self.drr[q] = (k + 1) % self.nds[q]
        self.dcnt[(q, k)] += 16
        ins.then_inc(self.dsem[(q, k)], 16)
        self.rec[q].append(('i', (q, k), 16))
        self._record(((q, k), self.dcnt[(q, k)]), reads, writes)
        self.ninst += 1

    def dma(self, q, out, in_, reads=(), writes=(), **kw):
        self._wait(q, self._deps(reads, writes))
        self._dma_pre(q)
        ins = self.eng[q].dma_start(out=out, in_=in_, **kw)
        self._dma_fin(q, ins, reads, writes)
        return ins

    def gather(self, out, in_, idx_ap, reads=(), writes=()):
        q = 'pool'
        self._wait(q, self._deps(reads, writes))
        self._dma_pre(q)
        ins = self.nc.gpsimd.indirect_dma_start(
            out=out, out_offset=None, in_=in_,
            in_offset=bass.IndirectOffsetOnAxis(ap=idx_ap, axis=0))
        self._dma_fin(q, ins, reads, writes)
        return ins

    def bg_dma(self, out, in_):
        k = self.bgrr
        self.bgrr = (k + 1) % 32
        if self.bgcnt[k]:
            self.eng['pool'].wait_ge(self.bgsems[k], self.bgcnt[k])
        ins = self.eng['pool'].dma_start(out=out, in_=in_)
        self.bgcnt[k] += 16
        ins.then_inc(self.bgsems[k], 16)
        self.ninst += 1

    def bg_wait(self, e):
        for k in range(32):
            if self.bgcnt[k]:
                self.eng[e].wait_ge(self.bgsems[k], self.bgcnt[k])

    def all_tokens(self):
        deps = [(e, c) for e, c in self.cnt.items() if c > 0]
        deps += [(k, c) for k, c in self.dcnt.items() if c > 0]
        return deps

    RESET = True

    def barrier(self, reset=True):
        reset = reset and KB.RESET
        deps = self.all_tokens()
        for e in self.eng:
            self._wait(e, deps)
        if reset:
            self.nbar += 1
            for e in self.eng:
                self.rec[e].append(('b', self.nbar, 0))
            for e in self.eng:
                self.eng[e].sem_inc(self.bs_arrive, 1)
            m = self.eng['sp']
            m.wait_ge(self.bs_arrive, len(self.eng) * self.nbar)
            for sm in list(self.sem.values()) + [v for k, v in self.dsem.items() if k[0] != 'pool']:
                m.sem_clear(sm)
            m.sem_inc(self.bs_go, 1)
            for e in self.eng:
                self.eng[e].wait_ge(self.bs_go, self.nbar)
            for e in self.cnt:
                self.cnt[e] = 0
            for k in self.dcnt:
                if k[0] != 'pool':
                    self.dcnt[k] = 0
            self.seen = {e: {} for e in self.eng}
        self.lastw = {}
        self.readers = {}

    @contextlib.contextmanager
    def phase(self):
        with contextlib.ExitStack() as ph:
            old = self.st
            self.st = ph
            try:
                yield
            finally:
                self.barrier()
                self.st = old


def build(cfg=None):
    cfg = cfg or {}
    subs = cfg.get('subs', [(i, w) for i in range(DEPTH) for w in (0, 1)])
    nc = bass.Bass("TRN2", target_bir_lowering=False)

    def din(name, shape, dt=F32):
        return nc.dram_tensor(name, list(shape), dt, kind="ExternalInput").ap()

    def dout(name, shape, dt=F32):
        return nc.dram_tensor(name, list(shape), dt, kind="ExternalOutput").ap()

    def dscr(name, shape, dt=F32):
        return nc.dram_tensor(name, list(shape), dt, kind="Internal").ap()

    xin = din("xin", [NTOK, D])
    cond_d = din("cond", [2, D])
    st_in = din("st_in", [2, 2, 16, 64, 64])
    ck_d = din("ck", [2, 512, 256])
    cv_d = din("cv", [2, 512, 256])
    ada_w = din("ada_w", [4, D, 6 * D])
    ada_b = din("ada_b", [4, 6 * D])
    ln_g = din("ln_g", [4, 2, D])
    ln_b = din("ln_b", [4, 2, D])
    muT_d = din("muT", [2, 128, 6, 8])
    wrkv_d = din("wrkv", [2, 3, D, D])
    rwo_d = din("rwo", [2, D, D])
    w0_d = din("w0", [2, 2, D])
    w1c_d = din("w1c", [2, D, 128])
    w2c_d = din("w2c", [2, 128, D])
    a0_d = din("a0", [2, 2, D])
    a1c_d = din("a1c", [2, D, 128])
    a2c_d = din("a2c", [2, 128, D])
    g1_d = din("g1", [2, D, 128])
    g2_d = din("g2", [2, 128, D])
    rkk_d = din("rkk", [2, D])
    rka_d = din("rka", [2, D])
    rrk_d = din("rrk", [2, D])
    lnxg_d = din("lnxg", [2, D])
    lnxb_d = din("lnxb", [2, D])
    wqkv_d = din("wqkv", [2, D, 1536])
    awo_d = din("awo", [2, D, D])
    qn_d = din("qn", [2, 64])
    kn_d = din("kn", [2, 64])
    pwq_d = din("pwq", [4, D, 2048])
    pkT_d = din("pkT", [4, 2, 128, 128])
    pu_d = [din(f"pu{i}", [16384, D]) for i in range(4)]
    pv_d = [din(f"pv{i}", [16384, D]) for i in range(4)]
    cst_d = din("cst", [128, NCST])
    sel2_d = din("sel2", [2, 256])
    rope_d = din("rope", [1024, 64])

    y_d = dout("y", [NTOK, D])
    nst_d = dout("nst", [2, 2, 2, 16, 64, 64])
    nk_d = dout("nk", [2, 2, 256, 256])
    nv_d = dout("nv", [2, 2, 256, 256])

    mods_d = dscr("mods", [4, 2, 6 * D])
    r_s = dscr("r_s", [NTOK, D])
    k_s = dscr("k_s", [NTOK, D])
    v_s = dscr("v_s", [NTOK, D])
    g_s = dscr("g_s", [NTOK, D])
    lw_s = dscr("lw_s", [2, NTOK, D])
    a_s = dscr("a_s", [2, NTOK, D])
    yf_s = dscr("yf_s", [NTOK, D])
    z_s = dscr("z_s", [NTOK, D])

    with contextlib.ExitStack() as top:
        kb = KB(nc, top)

        def V(fn, r=(), w=()):
            return kb.op('dve', fn, r, w)

        def A(fn, r=(), w=()):
            return kb.op('act', fn, r, w)

        def G(fn, r=(), w=()):
            return kb.op('pool', fn, r, w)

        def T(fn, r=(), w=()):
            return kb.op('pe', fn, r, w)

        x_res = kb.sb('x_res', [128, NT, D])
        cst = kb.sb('cst', [128, CR0])
        identb = kb.sb('identb', [128, 128], BF16)
        sel2 = kb.sb('sel2', [2, 256])
        siluT = kb.sb('siluT', [128, 16], BF16)
        psb = [kb.ps(f"psb{i}", [128, 512]) for i in range(8)]
        PS = [f'ps{i}' for i in range(8)]
        ident = cst[:, C_ID:C_ID + 128]

        def psbf(i):
            return psb[i][:].bitcast(BF16)

        nc.all_engine_barrier()
        for sm in list(kb.sem.values()) + list(kb.dsem.values()) + kb.bgsems + [kb.bs_arrive, kb.bs_go]:
            nc.sync.sem_clear(sm)
        nc.all_engine_barrier()
        kb.dma('sp', cst[:], cst_d[:, 0:CR0], writes=['cst'])
        kb.dma('sp', sel2[:], sel2_d, writes=['sel2'])
        if cfg.get('load_x', True):
            for tt in range(NT):
                kb.dma('sp', x_res[:, tt, :], xin[tt * 128:(tt + 1) * 128, :], writes=[f'x{tt}'])
        V(lambda e: e.tensor_copy(out=identb[:], in_=ident), ['cst'], ['identb'])

        with kb.phase():
            cnd = kb.sb('cnd', [2, D])
            sil = kb.sb('sil', [2, D])
            kb.dma('sp', cnd[:], cond_d, writes=['cnd'])
            A(lambda e: e.activation(out=sil[:], in_=cnd[:], func=AF.Silu), ['cnd'], ['sil'])
            for c in range(8):
                T(lambda e: e.transpose(out=psb[0][:, c * 2:(c + 1) * 2], in_=sil[0:2, c * 128:(c + 1) * 128],
                                        identity=cst[0:2, C_ID:C_ID + 2]), ['sil', 'cst'], [PS[0]])
            V(lambda e: e.tensor_copy(out=siluT[:], in_=psb[0][:, 0:16]), [PS[0]], ['siluT'])

        def load_cast(dst, src, r=(), w=()):
            kb.dma('pool', dst, src, reads=r, writes=w)

        def adaln(i):
            with kb.phase():
                brow = kb.sb('brow', [2, 3072])
                mrow = kb.sb('mrow', [2, 3072])
                awb = [kb.sb(f'awb{b}', [128, 3072], BF16) for b in range(2)]
                for half in range(2):
                    c0 = half * 3072
                    kb.dma('sp', brow[:], ada_b[i:i + 1, c0:c0 + 3072].partition_broadcast(2), writes=['brow'])
                    for k in range(8):
                        b = k % 2
                        for q in range(2):
                            load_cast(awb[b][:, q * 1536:(q + 1) * 1536],
                                      ada_w[i, k * 128:(k + 1) * 128, c0 + q * 1536:c0 + (q + 1) * 1536],
                                      w=[f'awb{b}'])
                        for cb in range(6):
                            T(lambda e: e.matmul(psb[cb][0:2, :], lhsT=siluT[:, k * 2:(k + 1) * 2],
                                                 rhs=awb[b][:, cb * 512:(cb + 1) * 512], start=(k == 0), stop=(k == 7)),
                              ['siluT', f'awb{b}'], [PS[cb]])
                    for cb in range(6):
                        V(lambda e: e.tensor_tensor(out=mrow[:, cb * 512:(cb + 1) * 512], in0=psb[cb][0:2, :],
                                                    in1=brow[:, cb * 512:(cb + 1) * 512], op=ALU.add),
                          [PS[cb], 'brow'], ['mrow'])
                    kb.dma('sp', mods_d[i, :, c0:c0 + 3072], mrow[:], reads=['mrow'], writes=['mods'])

        def setup_mod(i, which, want=(0, 1, 2), ln=True):
            modb = kb.sb('modb', [128, 6, D])
            lnb = kb.sb('lnb', [128, 2, D]) if ln else None
            with kb.phase():
                mrow3 = kb.sb('mrow3', [2, 3072])
                kb.dma('sp', mrow3[:], mods_d[i, :, which * 3072:(which + 1) * 3072], reads=['mods'], writes=['mrow3'])
                n = 0
                for v in want:
                    for cond in range(2):
                        for half in range(2):
                            b = n % 4
                            n += 1
                            T(lambda e: e.matmul(psb[b][:], lhsT=sel2[:, cond * 128:(cond + 1) * 128],
                                                 rhs=mrow3[:, v * 1024 + half * 512:v * 1024 + (half + 1) * 512],
                                                 start=True, stop=True), ['sel2', 'mrow3'], [PS[b]])
                            dst = modb[:, v * 2 + cond, half * 512:(half + 1) * 512]
                            if v == 1:
                                V(lambda e: e.tensor_scalar(out=dst, in0=psb[b][:], scalar1=1.0, scalar2=None,
                                                            op0=ALU.add), [PS[b]], ['modb'])
                            else:
                                A(lambda e: e.copy(out=dst, in_=psb[b][:]), [PS[b]], ['modb'])
                if ln:
                    kb.dma('sp', lnb[:, 0, :], ln_g[i, which:which + 1, :].partition_broadcast(128), writes=['lnb'])
                    kb.dma('sp', lnb[:, 1, :], ln_b[i, which:which + 1, :].partition_broadcast(128), writes=['lnb'])
            return modb, lnb

        def make_h(tt, modb, htok, hkey):
            cond = 0 if tt < 4 else 1
            V(lambda e: e.tensor_tensor(out=htok, in0=x_res[:, tt, :], in1=modb[:, 2 + cond, :], op=ALU.mult),
              [f'x{tt}', 'modb'], [hkey])
            V(lambda e: e.tensor_tensor(out=htok, in0=htok, in1=modb[:, 0 + cond, :], op=ALU.add),
              [hkey, 'modb'], [hkey])

        def transpose8(src_bf, skey, dst3, dkey, bank):
            pv = psbf(bank)
            for c in range(8):
                T(lambda e: e.transpose(out=pv[:, c * 128:(c + 1) * 128], in_=src_bf[:, c * 128:(c + 1) * 128],
                                        identity=identb[:]), [skey, 'identb'], [PS[bank]])
            A(lambda e: e.copy(out=dst3, in_=pv.rearrange("p (c t) -> p c t", c=8)), [PS[bank]], [dkey])

        def post_sublayer(tt, banks, modb, lnb, wk):
            cond = 0 if tt < 4 else 1
            z, stats, mv, rs = wk
            xk = f'x{tt}'
            for half in range(2):
                sl = slice(half * 512, (half + 1) * 512)
                V(lambda e: e.tensor_tensor(out=z[:, sl], in0=psb[banks[half]][:], in1=modb[:, 4 + cond, sl], op=ALU.mult),
                  [PS[banks[half]], 'modb'], ['z'])
                V(lambda e: e.scalar_tensor_tensor(out=z[:, sl], in0=x_res[:, tt, sl], scalar=ALPHA, in1=z[:, sl],
                                                   op0=ALU.mult, op1=ALU.add), [xk, 'z'], ['z'])
                V(lambda e: e.bn_stats(out=stats[:, half, :], in_=z[:, sl]), ['z'], ['stats'])
            V(lambda e: e.bn_aggr(out=mv[:], in_=stats[:].rearrange("p a b -> p (a b)")), ['stats'], ['mv'])
            V(lambda e: e.tensor_scalar(out=rs[:, 0:1], in0=mv[:, 1:2], scalar1=LN_EPS, scalar2=None, op0=ALU.add),
              ['mv'], ['rs'])
            A(lambda e: e.activation(out=rs[:, 1:2], in_=rs[:, 0:1], func=AF.Sqrt), ['rs'], ['rs'])
            V(lambda e: e.reciprocal(out=rs[:, 2:3], in_=rs[:, 1:2]), ['rs'], ['rs'])
            V(lambda e: e.tensor_scalar(out=z[:], in0=z[:], scalar1=mv[:, 0:1], scalar2=rs[:, 2:3],
                                        op0=ALU.subtract, op1=ALU.mult), ['z', 'mv', 'rs'], ['z'])
            V(lambda e: e.tensor_tensor(out=z[:], in0=z[:], in1=lnb[:, 0, :], op=ALU.mult), ['z', 'lnb'], ['z'])
            V(lambda e: e.tensor_tensor(out=x_res[:, tt, :], in0=z[:], in1=lnb[:, 1, :], op=ALU.add),
              ['z', 'lnb'], [xk])

        def post_work():
            return (kb.sb('z', [128, D]), kb.sb('stats', [128, 2, 6]), kb.sb('mv', [128, 2]), kb.sb('rs', [128, 4]))

        ubv_d = dscr("ubv", [16384, 2048], BF16)

        def peer_convert(i):
            for c in range(16):
                rs_ = slice(c * 1024, (c + 1) * 1024)
                kb.bg_dma(ubv_d[rs_, 0:1024], pu_d[i][rs_, :])
                kb.bg_dma(ubv_d[rs_, 1024:2048], pv_d[i][rs_, :])

        def peer_sublayer(i):
            with kb.phase():
                modb, lnb = setup_mod(i, 1)
                kb.bg_wait('pool')
                wk = post_work()
                wq = kb.sb('wq', [128, 8, 2048], BF16)
                for c in range(8):
                    load_cast(wq[:, c, :], pwq_d[i, c * 128:(c + 1) * 128, :], w=['wq'])
                keyT = kb.sb('keyT', [128, 2, 128], BF16)
                load_cast(keyT[:], pkT_d[i].rearrange("z d k -> d z k"), w=['keyT'])
                htok1 = kb.sb('htok', [128, D])
                htok = [htok1, htok1]
                hbs = [kb.sb(f'hb{b}', [128, D], BF16) for b in range(2)]
                hTt = kb.sb('hTt', [128, 8, 128], BF16)
                qT = kb.sb('qT', [128, 16, 128], BF16)
                s_sb = kb.sb('s_sb', [128, 16, 128])
                sv = kb.sb('sv', [128, 16, 16])
                si = kb.sb('si', [128, 16, 16], U32)
                si_f = kb.sb('si_f', [128, 16, 16])
                cand = kb.sb('cand', [128, 8, 256])
                cv = kb.sb('cv', [128, 8, 16])
                ci = kb.sb('ci', [128, 128], U32)
                ab_i = kb.sb('ab_i', [128, 2, 128], U32)
                ab_f = kb.sb('ab_f', [128, 2, 128])
                oh = cand
                candk = [f'cand{h}' for h in range(8)]
                i12 = kb.sb('i12', [128, 2, 128])
                idx_i = [kb.sb(f'idx_i{b}', [128, 128], I32) for b in range(2)]
                gs = kb.sb('gs', [128, 8])
                gate = [kb.sb(f'gate{b}', [128, 128]) for b in range(2)]
                pre = kb.sb('pre', [128, 128])
                ga = kb.sb('ga', [128, 128])
                GK, NBUF = 2, 6
                uv = [kb.sb(f'uv{b}', [128, GK, 2048], BF16) for b in range(NBUF)]
                Dk = [kb.sb(f'Dk{b}', [128, 128], BF16) for b in range(2)]
                iota16 = cst[:, C_IOTA:C_IOTA + 16]

                def topk_stage(tt, sl):
                    hk, ik, gk = 'htok', f'idx_i{sl}', f'gate{sl}'
                    hb = hbs[sl]
                    make_h(tt, modb, htok[sl][:], hk)
                    A(lambda e: e.copy(out=hb[:], in_=htok[sl][:]), [hk], [f'hb{sl}'])
                    yield
                    transpose8(hb, f'hb{sl}', hTt[:], 'hTt', 6)
                    yield
                    for rnd in range(2):
                        for hz in range(rnd * 8, rnd * 8 + 8):
                            bk = 2 + (hz % 8) // 4
                            for c in range(8):
                                T(lambda e: e.matmul(psb[bk][:, (hz % 4) * 128:(hz % 4 + 1) * 128],
                                                     lhsT=wq[:, c, hz * 128:(hz + 1) * 128], rhs=hTt[:, c, :],
                                                     start=(c == 0), stop=(c == 7)), ['wq', 'hTt'], [PS[bk]])
                            if hz % 2 == 1:
                                yield
                        for b2 in range(2):
                            A(lambda e: e.copy(out=qT[:, rnd * 8 + b2 * 4:rnd * 8 + b2 * 4 + 4, :],
                                               in_=psb[2 + b2][:].rearrange("p (a t) -> p a t", a=4)), [PS[2 + b2]], ['qT'])
                        yield
                    for rnd in range(2):
                        for hz in range(rnd * 8, rnd * 8 + 8):
                            bk = 4 + (hz % 8) // 4
                            T(lambda e: e.matmul(psb[bk][:, (hz % 4) * 128:(hz % 4 + 1) * 128],
                                                 lhsT=qT[:, hz, :], rhs=keyT[:, hz % 2, :], start=True, stop=True),
                              ['qT', 'keyT'], [PS[bk]])
                        for b2 in range(2):
                            h0 = rnd * 8 + b2 * 4
                            A(lambda e: e.copy(out=s_sb[:, h0:h0 + 4, :],
                                               in_=psb[4 + b2][:].rearrange("p (a t) -> p a t", a=4)),
                              [PS[4 + b2]], [f's_sb{h0 + a}' for a in range(4)])
                        yield
                    for hz in range(16):
                        ks, kv_, ki = f's_sb{hz}', f'sv{hz}', f'si{hz}'
                        V(lambda e: e.max(out=sv[:, hz, 0:8], in_=s_sb[:, hz, :]), [ks], [kv_])
                        V(lambda e: e.max_index(out=si[:, hz, 0:8], in_max=sv[:, hz, 0:8], in_values=s_sb[:, hz, :]),
                          [ks, kv_], [ki])
                        V(lambda e: e.match_replace(out=s_sb[:, hz, :], in_to_replace=sv[:, hz, 0:8],
                                                    in_values=s_sb[:, hz, :], imm_value=-1e30), [ks, kv_], [ks])
                        yield
                        V(lambda e: e.max(out=sv[:, hz, 8:16], in_=s_sb[:, hz, :]), [ks], [kv_])
                        V(lambda e: e.max_index(out=si[:, hz, 8:16], in_max=sv[:, hz, 8:16], in_values=s_sb[:, hz, :]),
                          [ks, kv_], [ki])
                        yield
                    svk = [f'sv{hz}' for hz in range(16)]
                    sik = [f'si{hz}' for hz in range(16)]
                    V(lambda e: e.tensor_copy(out=si_f[:], in_=si[:]), sik, ['si_f'])
                    sv4 = sv[:].rearrange("p (h z) k -> p h z k", z=2)
                    sif4 = si_f[:].rearrange("p (h z) k -> p h z k", z=2)
                    V(lambda e: e.tensor_tensor(out=cand[:].rearrange("p h (a b) -> p h a b", a=16),
                                                in0=sv4[:, :, 0, :].unsqueeze(3).to_broadcast([128, 8, 16, 16]),
                                                in1=sv4[:, :, 1, :].unsqueeze(2).to_broadcast([128, 8, 16, 16]),
                                                op=ALU.add), svk, [f'cand{h}' for h in range(8)])
                    yield
                    for h in range(8):
                        kc, kcv, kci = f'cand{h}', f'cv{h}', f'ci{h}'
                        V(lambda e: e.max(out=cv[:, h, 0:8], in_=cand[:, h, :]), [kc], [kcv])
                        V(lambda e: e.max_index(out=ci[:, h * 16:h * 16 + 8], in_max=cv[:, h, 0:8], in_values=cand[:, h, :]),
                          [kc, kcv], [kci])
                        V(lambda e: e.match_replace(out=cand[:, h, :], in_to_replace=cv[:, h, 0:8],
                                                    in_values=cand[:, h, :], imm_value=-1e30), [kc, kcv], [kc])
                        yield
                        V(lambda e: e.max(out=cv[:, h, 8:16], in_=cand[:, h, :]), [kc], [kcv])
                        V(lambda e: e.max_index(out=ci[:, h * 16 + 8:h * 16 + 16], in_max=cv[:, h, 8:16],
                                                in_values=cand[:, h, :]), [kc, kcv], [kci])
                        yield
                    cvk = [f'cv{h}' for h in range(8)]
                    cik = [f'ci{h}' for h in range(8)]
                    V(lambda e: e.tensor_scalar(out=ab_i[:, 0, :], in0=ci[:], scalar1=4, scalar2=None,
                                                op0=ALU.logical_shift_right), cik, ['ab_i'])
                    V(lambda e: e.tensor_scalar(out=ab_i[:, 1, :], in0=ci[:], scalar1=15, scalar2=None,
                                                op0=ALU.bitwise_and), cik, ['ab_i'])
                    V(lambda e: e.tensor_copy(out=ab_f[:], in_=ab_i[:]), ['ab_i'], ['ab_f'])
                    yield
                    for zz in range(2):
                        V(lambda e: e.tensor_tensor(out=oh[:].rearrange("p h (k a) -> p (h k) a", k=16), in0=ab_f[:, zz, :].unsqueeze(2).to_broadcast([128, 128, 16]),
                                                    in1=iota16.unsqueeze(1).to_broadcast([128, 128, 16]),
                                                    op=ALU.is_equal), ['ab_f', 'cst'], candk)
                        yield
                        V(lambda e: e.tensor_tensor(out=oh[:].rearrange("p h (k a) -> p h k a", k=16),
                                                    in0=oh[:].rearrange("p h (k a) -> p h k a", k=16),
                                                    in1=sif4[:, :, zz, :].unsqueeze(2).to_broadcast([128, 8, 16, 16]),
                                                    op=ALU.mult), candk + ['si_f'], candk)
                        yield
                        V(lambda e: e.tensor_reduce(out=i12[:, zz, :], in_=oh[:].rearrange("p h (k a) -> p (h k) a", k=16), axis=AX.X, op=ALU.add), candk, ['i12'])
                        yield
                    V(lambda e: e.scalar_tensor_tensor(out=i12[:, 0, :], in0=i12[:, 0, :], scalar=128.0, in1=i12[:, 1, :],
                                                       op0=ALU.mult, op1=ALU.add), ['i12'], ['i12'])
                    V(lambda e: e.tensor_scalar(out=i12[:, 0, :], in0=i12[:, 0, :], scalar1=0.0, scalar2=16383.0, op0=ALU.max, op1=ALU.min),
                      ['i12'], ['i12'])
                    V(lambda e: e.tensor_copy(out=idx_i[sl][:], in_=i12[:, 0, :]), ['i12'], [ik])
                    g3 = gate[sl][:].rearrange("p (h k) -> p h k", h=8)
                    V(lambda e: e.tensor_tensor(out=g3, in0=cv[:], in1=cv[:, :, 0:1].to_broadcast([128, 8, 16]),
                                                op=ALU.subtract), cvk, [gk])
                    A(lambda e: e.activation(out=g3, in_=g3, func=AF.Exp), [gk], [gk])
                    V(lambda e: e.tensor_reduce(out=gs[:], in_=g3, axis=AX.X, op=ALU.add), [gk], ['gs'])
                    V(lambda e: e.reciprocal(out=gs[:], in_=gs[:]), ['gs'], ['gs'])
                    V(lambda e: e.tensor_tensor(out=g3, in0=g3, in1=gs[:].unsqueeze(2).to_broadcast([128, 8, 16]), op=ALU.mult),
                      [gk, 'gs'], [gk])
                    yield

                def drain(gen, n=None):
                    if gen is None:
                        return None
                    try:
                        if n is None:
                            while True:
                                next(gen)
                        for _ in range(n):
                            next(gen)
                    except StopIteration:
                        return None
                    return gen

                def expert_stage(tt, sl, nxt):
                    hk, ik, gk = f'htok{sl}', f'idx_i{sl}', f'gate{sl}'
                    V(lambda e: e.memset(pre[:], 0.0), [], ['pre'])
                    ngrp = 128 // GK

                    def S0(g):
                        gb = g % NBUF
                        for jj in range(GK):
                            k = g * GK + jj
                            kb.gather(uv[gb][:, jj, :], ubv_d, idx_i[sl][:, k:k + 1], reads=[ik, 'ubv'], writes=[f'uv{gb}_{jj}'])

                    def S1(g):
                        gb = g % NBUF
                        for jj in range(GK):
                            k = g * GK + jj
                            V(lambda e: e.scalar_tensor_tensor(out=uv[gb][:, jj, 0:1024], in0=uv[gb][:, jj, 0:1024], scalar=1.0, in1=hbs[sl][:],
                                                               op0=ALU.mult, op1=ALU.mult, accum_out=pre[:, k:k + 1]),
                              [f'uv{gb}_{jj}', f'hb{sl}', 'pre'], [f'pre{g}', f'uv{gb}_{jj}'])

                    def S2(g):
                        k0 = g * GK
                        A(lambda e: e.activation(out=ga[:, k0:k0 + GK], in_=pre[:, k0:k0 + GK], func=AF.Gelu), [f'pre{g}', 'pre'], [f'ga{g}'])
                        V(lambda e: e.tensor_tensor(out=ga[:, k0:k0 + GK], in0=ga[:, k0:k0 + GK], in1=gate[sl][:, k0:k0 + GK],
                                                    op=ALU.mult), [f'ga{g}', gk], [f'ga{g}'])

                    def S3(g):
                        gb = g % NBUF
                        for jj in range(GK):
                            k = g * GK + jj
                            A(lambda e: e.activation(out=Dk[k % 2][:], in_=identb[:], func=AF.Copy, scale=ga[:, k:k + 1]),
                              ['identb', f'ga{g}'], [f'Dk{k % 2}'])
                            for half in range(2):
                                T(lambda e: e.matmul(psb[half][:], lhsT=Dk[k % 2][:],
                                                     rhs=uv[gb][:, jj, 1024 + half * 512:1024 + (half + 1) * 512],
                                                     start=(k == 0), stop=(k == 127)), [f'Dk{k % 2}', f'uv{gb}_{jj}'], [PS[half]])

                    for it in range(ngrp + 3):
                        if it < ngrp:
                            S0(it)
                        if 0 <= it - 1 < ngrp:
                            S1(it - 1)
                        if 0 <= it - 2 < ngrp:
                            S2(it - 2)
                        if 0 <= it - 3 < ngrp:
                            S3(it - 3)
                        nxt = drain(nxt, 2)
                    drain(nxt)
                    post_sublayer(tt, (0, 1), modb, lnb, wk)

                drain(topk_stage(0, 0))
                for tt in range(NT):
                    sl = tt % 2
                    nxt = topk_stage(tt + 1, 1 - sl) if tt + 1 < NT else None
                    expert_stage(tt, sl, nxt)

        def rwkv_sublayer(i):
            j = i // 2
            one1 = cst[0:1, C_ONES:C_ONES + 128]
            onec = cst[:, C_ONES:C_ONES + 1]
            with contextlib.ExitStack() as sub_st:
                old_st = kb.st
                kb.st = sub_st
                bon = kb.sb('bon', [128, NT, 16])
                with kb.phase():
                    modb, _ = setup_mod(i, 0, want=(0, 1), ln=False)
                    hT = kb.sb('hT', [128, 8, HPAD], BF16)
                    G(lambda e: e.memset(hT[:], 0.0), [], ['hT'])
                    htok = kb.sb('htok', [128, D])
                    hb = kb.sb('hb', [128, D], BF16)
                    for tt in range(NT):
                        pc = padcol(tt)
                        make_h(tt, modb, htok[:], 'htok')
                        A(lambda e: e.copy(out=hb[:], in_=htok[:]), ['htok'], ['hb'])
                        transpose8(hb, 'hb', hT[:, :, pc:pc + 128], 'hT', 7)
                    muT = kb.sb('muT', [128, 6, 8])
                    kb.dma('sp', muT[:], muT_d[j], writes=['muT'])
                    w0row = kb.sb('w0row', [1, 2, D])
                    a0row = kb.sb('a0row', [1, 2, D])
                    kb.dma('sp', w0row[:], w0_d[j:j + 1], writes=['w0row'])
                    kb.dma('sp', a0row[:], a0_d[j:j + 1], writes=['a0row'])
                    xxs = [kb.sb(f'xx{b}', [128, 8, 128]) for b in range(2)]
                    xms = [kb.sb(f'xm{b}', [128, 8, 128], BF16) for b in range(2)]
                    W = kb.sb('W', [128, 8, D], BF16)
                    l1 = kb.sb('l1', [128, 8, 128], BF16)
                    l2 = kb.sb('l2', [128, D], BF16)
                    hid = kb.sb('hid', [128, 128], BF16)
                    ot = [kb.sb(f'ot{b}', [128, D]) for b in range(2)]
                    nout = [0]

                    def xm_tile(tt, m, bi):
                        pc = padcol(tt)
                        xx, xm, kx, km = xxs[bi], xms[bi], f'xx{bi}', f'xm{bi}'
                        V(lambda e: e.tensor_tensor(out=xx[:], in0=hT[:, :, pc - 1:pc + 127], in1=hT[:, :, pc + 1:pc + 129],
                                                    op=ALU.add), ['hT'], [kx])
                        V(lambda e: e.scalar_tensor_tensor(out=xx[:], in0=xx[:], scalar=0.5, in1=hT[:, :, pc:pc + 128],
                                                           op0=ALU.mult, op1=ALU.subtract), [kx, 'hT'], [kx])
                        V(lambda e: e.tensor_tensor(out=xx[:], in0=xx[:], in1=muT[:, m, :].unsqueeze(2).to_broadcast([128, 8, 128]),
                                                    op=ALU.mult), [kx, 'muT'], [kx])
                        V(lambda e: e.tensor_tensor(out=xm[:], in0=xx[:], in1=hT[:, :, pc:pc + 128], op=ALU.add),
                          [kx, 'hT'], [km])

                    def xm_iter(m):
                        xm_tile(0, m, 0)
                        for tt in range(NT):
                            if tt + 1 < NT:
                                xm_tile(tt + 1, m, (tt + 1) % 2)
                            yield tt, xms[tt % 2], f'xm{tt % 2}'

                    def store(dst_rows, func=None, post_scale=None):
                        b = nout[0] % 2
                        nout[0] += 1
                        for half in range(2):
                            sl = slice(half * 512, (half + 1) * 512)
                            if func is None:
                                A(lambda e: e.copy(out=ot[b][:, sl], in_=psb[half][:]), [PS[half]], [f'ot{b}'])
                            else:
                                A(lambda e: e.activation(out=ot[b][:, sl], in_=psb[half][:], func=func), [PS[half]], [f'ot{b}'])
                        if post_scale is not None:
                            V(lambda e: e.tensor_scalar(out=ot[b][:], in0=ot[b][:], scalar1=post_scale, scalar2=None,
                                                        op0=ALU.mult), [f'ot{b}'], [f'ot{b}'])
                        kb.dma('sp', dst_rows, ot[b][:], reads=[f'ot{b}'], writes=['scr'])

                    for (m, widx, dst_s) in ((0, 0, r_s), (2, 1, k_s), (3, 2, v_s)):
                        for c in range(8):
                            load_cast(W[:, c, :], wrkv_d[j, widx, c * 128:(c + 1) * 128, :], w=['W'])
                        for tt, xm, km in xm_iter(m):
                            for half in range(2):
                                for c in range(8):
                                    T(lambda e: e.matmul(psb[half][:], lhsT=xm[:, c, :], rhs=W[:, c, half * 512:(half + 1) * 512],
                                                         start=(c == 0), stop=(c == 7)), [km, 'W'], [PS[half]])
                            store(dst_s[tt * 128:(tt + 1) * 128, :])
                    for (m, l1_d, l2_d, brow, hfunc, dst2, ofunc, oscale) in (
                            (1, w1c_d, w2c_d, w0row, AF.Tanh, lw_s, AF.Sigmoid, NEG_EXP_HALF),
                            (4, a1c_d, a2c_d, a0row, None, a_s, AF.Sigmoid, None)):
                        load_cast(l1[:], l1_d[j].rearrange("(c p) l -> p c l", p=128), w=['l1'])
                        load_cast(l2[:], l2_d[j], w=['l2'])
                        for tt, xm, km in xm_iter(m):
                            for c in range(8):
                                T(lambda e: e.matmul(psb[2][:, 0:128], lhsT=l1[:, c, :], rhs=xm[:, c, :],
                                                     start=(c == 0), stop=(c == 7)), ['l1', km], [PS[2]])
                            if hfunc is None:
                                A(lambda e: e.copy(out=hid[:], in_=psb[2][:, 0:128]), [PS[2]], ['hid'])
                            else:
                                A(lambda e: e.activation(out=hid[:], in_=psb[2][:, 0:128], func=hfunc), [PS[2]], ['hid'])
                            for z in range(2):
                                for half in range(2):
                                    sl = slice(half * 512, (half + 1) * 512)
                                    T(lambda e: e.matmul(psb[half][:], lhsT=hid[z * 64:(z + 1) * 64, :],
                                                         rhs=l2[z * 64:(z + 1) * 64, sl], start=True, stop=False),
                                      ['hid', 'l2'], [PS[half]])
                                    T(lambda e: e.matmul(psb[half][:], lhsT=one1, rhs=brow[0:1, z, sl], start=False, stop=True),
                                      ['cst', 'w0row', 'a0row'], [PS[half]])
                                store(dst2[z, tt * 128:(tt + 1) * 128, :], func=ofunc, post_scale=oscale)
                    load_cast(l1[:], g1_d[j].rearrange("(c p) l -> p c l", p=128), w=['l1'])
                    load_cast(l2[:], g2_d[j], w=['l2'])
                    for tt, xm, km in xm_iter(5):
                        for c in range(8):
                            T(lambda e: e.matmul(psb[2][:, 0:128], lhsT=l1[:, c, :], rhs=xm[:, c, :],
                                                 start=(c == 0), stop=(c == 7)), ['l1', km], [PS[2]])
                        A(lambda e: e.activation(out=hid[:], in_=psb[2][:, 0:128], func=AF.Sigmoid), [PS[2]], ['hid'])
                        for half in range(2):
                            T(lambda e: e.matmul(psb[half][:], lhsT=hid[:], rhs=l2[:, half * 512:(half + 1) * 512],
                                                 start=True, stop=True), ['hid', 'l2'], [PS[half]])
                        store(g_s[tt * 128:(tt + 1) * 128, :])

                with kb.phase():
                    kkb = kb.sb('kkb', [128, D])
                    kab = kb.sb('kab', [128, D])
                    rkb = kb.sb('rkb', [128, D])
                    kb.dma('sp', kkb[:], rkk_d[j:j + 1, :].partition_broadcast(128), writes=['kkb'])
                    kb.dma('sp', kab[:], rka_d[j:j + 1, :].partition_broadcast(128), writes=['kab'])
                    kb.dma('sp', rkb[:], rrk_d[j:j + 1, :].partition_broadcast(128), writes=['rkb'])
                    r_t = kb.sb('r_t', [128, D])
                    k_t = kb.sb('k_t', [128, D])
                    v_t = kb.sb('v_t', [128, D])
                    lw_t = kb.sb('lw_t', [128, D])
                    a_t = kb.sb('a_t', [128, D])
                    f1 = kb.sb('f1', [128, D])
                    f2 = kb.sb('f2', [128, D])
                    f3 = kb.sb('f3', [128, D])
                    fP = kb.sb('fP', [128, D])
                    fPi = kb.sb('fPi', [128, D])
                    n16 = kb.sb('n16', [128, 4, 16])
                    at_tok = kb.sb('at_tok', [128, D], BF16)
                    rt_tok = kb.sb('rt_tok', [128, D], BF16)
                    bt_tok = kb.sb('bt_tok', [128, D], BF16)
                    kt_tok = kb.sb('kt_tok', [128, D], BF16)
                    vb = kb.sb('vb', [128, D], BF16)
                    arT = kb.sb('arT', [64, 16, 2, 128], BF16)
                    btT = kb.sb('btT', [64, 16, 128], BF16)
                    ktT = kb.sb('ktT', [64, 16, 128], BF16)
                    Am = kb.sb('Am', [128, 16, 512], BF16)
                    A4 = [kb.sb(f'A4{s}', [128, 4, 128], BF16) for s in range(8)]
                    AT4 = [kb.sb(f'AT4{s}', [128, 4, 128], BF16) for s in range(8)]
                    NB = [kb.sb(f'NB{s}', [128, 4, 128], BF16) for s in range(8)]
                    Tb = [kb.sb(f'Tb{s}', [128, 2, 128], BF16) for s in range(8)]
                    N_all = kb.sb('N_all', [128, 16, 128], BF16)
                    Xb = kb.sb('Xb', [128, D], BF16)
                    Ub = kb.sb('Ub', [128, D], BF16)
                    S_T = kb.sb('S_T', [64, 16, 64])
                    Sb = kb.sb('Sb', [64, 16, 64], BF16)
                    PC = kb.sb('PC', [64, 16])
                    stl = kb.sb('stl', [64, 16, 64])
                    yt = kb.sb('yt', [128, D])
                    cstR = kb.sb('cstR', [128, NCST - CR0])
                    kb.dma('sp', cstR[:], cst_d[:, CR0:NCST], writes=['cst'])
                    for (sq_i, tiles) in SEQS:
                        sample = (sq_i == 2)
                        for dr in range(2):
                            mask4 = cstR[:, C_M4F - CR0:C_M4F - CR0 + 512] if dr == 0 else cstR[:, C_M4B - CR0:C_M4B - CR0 + 512]
                            if sample:
                                kb.dma('sp', stl[:], st_in[j, dr].rearrange("h i j -> i h j"), writes=['stl'])
                                for h in range(16):
                                    T(lambda e: e.transpose(out=psb[h // 8][0:64, (h % 8) * 64:(h % 8 + 1) * 64], in_=stl[:, h, :],
                                                            identity=cst[0:64, C_ID:C_ID + 64]), ['stl', 'cst'], [PS[h // 8]])
                                for hq in range(2):
                                    V(lambda e: e.tensor_copy(out=S_T[:, hq * 8:(hq + 1) * 8, :],
                                                              in_=psb[hq][0:64, :].rearrange("p (h i) -> p h i", h=8)),
                                      [PS[hq]], ['S_T'])
                            else:
                                V(lambda e: e.memset(S_T[:], 0.0), [], ['S_T'])
                            A(lambda e: e.copy(out=Sb[:], in_=S_T[:]), ['S_T'], ['Sb'])
                            order = tiles if dr == 0 else tiles[::-1]
                            for tt in order:
                                rows = slice(tt * 128, (tt + 1) * 128)
                                kb.dma('sp', r_t[:], r_s[rows, :], reads=['scr'], writes=['r_t'])
                                kb.dma('sp', k_t[:], k_s[rows, :], reads=['scr'], writes=['k_t'])
                                kb.dma('sp', v_t[:], v_s[rows, :], reads=['scr'], writes=['v_t'])
                                kb.dma('sp', lw_t[:], lw_s[dr, rows, :], reads=['scr'], writes=['lw_t'])
                                kb.dma('sp', a_t[:], a_s[dr, rows, :], reads=['scr'], writes=['a_t'])
                                tri = cstR[:, C_TRIF - CR0:C_TRIF - CR0 + 128] if dr == 0 else cstR[:, C_TRIB - CR0:C_TRIB - CR0 + 128]
                                for half in range(2):
                                    T(lambda e: e.matmul(psb[half][:], lhsT=tri, rhs=lw_t[:, half * 512:(half + 1) * 512],
                                                         start=True, stop=True), ['cst', 'lw_t'], [PS[half]])
                                for h in range(16):
                                    T(lambda e: e.matmul(psb[2][0:64, h:h + 1], lhsT=lw_t[:, h * 64:(h + 1) * 64], rhs=onec,
                                                         start=True, stop=True), ['lw_t', 'cst'], [PS[2]])
                                for half in range(2):
                                    sl = slice(half * 512, (half + 1) * 512)
                                    A(lambda e: e.activation(out=fP[:, sl], in_=psb[half][:], func=AF.Exp), [PS[half]], ['fP'])
                                for half in range(2):
                                    sl = slice(half * 512, (half + 1) * 512)
                                    A(lambda e: e.activation(out=fPi[:, sl], in_=psb[half][:], func=AF.Exp, scale=-1.0),
                                      [PS[half]], ['fPi'])
                                for half in range(2):
                                    sl = slice(half * 512, (half + 1) * 512)
                                    V(lambda e: e.tensor_tensor(out=f3[:, sl], in0=psb[half][:], in1=lw_t[:, sl], op=ALU.subtract),
                                      [PS[half], 'lw_t', 'fPi'], ['f3'])
                                A(lambda e: e.activation(out=f3[:], in_=f3[:], func=AF.Exp), ['f3'], ['f3'])
                                A(lambda e: e.copy(out=vb[:], in_=v_t[:]), ['v_t'], ['vb'])
                                A(lambda e: e.activation(out=PC[:], in_=psb[2][0:64, 0:16], func=AF.Exp), [PS[2]], ['PC'])
                                V(lambda e: e.tensor_tensor(out=rt_tok[:], in0=r_t[:], in1=fP[:], op=ALU.mult), ['r_t', 'fP'], ['rt_tok'])
                                V(lambda e: e.scalar_tensor_tensor(out=f2[:], in0=a_t[:], scalar=-1.0, in1=kab[:],
                                                                   op0=ALU.add, op1=ALU.mult), ['a_t', 'kab'], ['f2'])
                                V(lambda e: e.scalar_tensor_tensor(out=f2[:], in0=f2[:], scalar=1.0, in1=k_t[:],
                                                                   op0=ALU.add, op1=ALU.mult), ['f2', 'k_t'], ['f2'])
                                V(lambda e: e.tensor_tensor(out=kt_tok[:], in0=f2[:], in1=fPi[:], op=ALU.mult), ['f2', 'fPi'], ['kt_tok'])
                                V(lambda e: e.tensor_tensor(out=fP[:], in0=r_t[:], in1=f2[:], op=ALU.mult), ['r_t', 'f2'], ['fP'])
                                V(lambda e: e.tensor_tensor(out=fP[:], in0=fP[:], in1=rkb[:], op=ALU.mult), ['fP', 'rkb'], ['fP'])
                                if dr == 0:
                                    V(lambda e: e.tensor_reduce(out=bon[:, tt, :], in_=fP[:].rearrange("p (h d) -> p h d", h=16),
                                                                axis=AX.X, op=ALU.add), ['fP'], ['bon'])
                                else:
                                    V(lambda e: e.tensor_reduce(out=n16[:, 3, :], in_=fP[:].rearrange("p (h d) -> p h d", h=16),
                                                                axis=AX.X, op=ALU.add), ['fP'], ['n16b'])
                                    V(lambda e: e.tensor_tensor(out=bon[:, tt, :], in0=bon[:, tt, :], in1=n16[:, 3, :], op=ALU.add),
                                      ['bon', 'n16b'], ['bon'])
                                V(lambda e: e.tensor_tensor(out=f1[:], in0=k_t[:], in1=kkb[:], op=ALU.mult), ['k_t', 'kkb'], ['f1'])
                                A(lambda e: e.activation(out=f2[:], in_=f1[:], func=AF.Square), ['f1'], ['f2'])
                                V(lambda e: e.tensor_reduce(out=n16[:, 0, :], in_=f2[:].rearrange("p (h d) -> p h d", h=16),
                                                            axis=AX.X, op=ALU.add), ['f2'], ['n16'])
                                A(lambda e: e.activation(out=n16[:, 1, :], in_=n16[:, 0, :], func=AF.Sqrt), ['n16'], ['n16'])
                                V(lambda e: e.tensor_scalar(out=n16[:, 1, :], in0=n16[:, 1, :], scalar1=1e-12, scalar2=None,
                                                            op0=ALU.max), ['n16'], ['n16'])
                                V(lambda e: e.reciprocal(out=n16[:, 2, :], in_=n16[:, 1, :]), ['n16'], ['n16'])
                                V(lambda e: e.tensor_tensor(out=f1[:].rearrange("p (h d) -> p h d", h=16),
                                                            in0=f1[:].rearrange("p (h d) -> p h d", h=16),
                                                            in1=n16[:, 2, :].unsqueeze(2).to_broadcast([128, 16, 64]),
                                                            op=ALU.mult), ['f1', 'n16'], ['f1'])
                                V(lambda e: e.tensor_tensor(out=f2[:], in0=f1[:], in1=a_t[:], op=ALU.mult), ['f1', 'a_t'], ['f2'])
                                V(lambda e: e.tensor_tensor(out=bt_tok[:], in0=f2[:], in1=fPi[:], op=ALU.mult), ['f2', 'fPi'], ['bt_tok'])
                                V(lambda e: e.scalar_tensor_tensor(out=at_tok[:], in0=f1[:], scalar=-1.0, in1=f3[:],
                                                                   op0=ALU.mult, op1=ALU.mult), ['f1', 'f3'], ['at_tok'])
                                nb = 0
                                for (src, skey, dstf, dkey) in ((rt_tok, 'rt_tok', lambda hq: arT[:, hq * 8:(hq + 1) * 8, 1, :], 'arT'),
                                                                (kt_tok, 'kt_tok', lambda hq: ktT[:, hq * 8:(hq + 1) * 8, :], 'ktT'),
                                                                (bt_tok, 'bt_tok', lambda hq: btT[:, hq * 8:(hq + 1) * 8, :], 'btT'),
                                                                (at_tok, 'at_tok', lambda hq: arT[:, hq * 8:(hq + 1) * 8, 0, :], 'arT')):
                                    for hq in range(2):
                                        bank = 3 + (nb % 2)
                                        nb += 1
                                        pv = psbf(bank)
                                        for h8 in range(8):
                                            h = hq * 8 + h8
                                            T(lambda e: e.transpose(out=pv[0:64, h8 * 128:(h8 + 1) * 128], in_=src[:, h * 64:(h + 1) * 64],
                                                                    identity=identb[:]), [skey, 'identb'], [PS[bank]])
                                        A(lambda e: e.copy(out=dstf(hq), in_=pv[0:64, :].rearrange("p (a t) -> p a t", a=8)),
                                          [PS[bank]], [dkey])
                                cMU = cstR[:, C_MU - CR0:C_MU - CR0 + 512]
                                cML = cstR[:, C_ML - CR0:C_ML - CR0 + 512]
                                mA = (cMU if dr == 0 else cML).rearrange("p (a t) -> p a t", a=4)
                                mAT = (cML if dr == 0 else cMU).rearrange("p (a t) -> p a t", a=4)
                                def head_chain(h, s_):
                                    bA = s_
                                    ka4, kat4, knb, ktb = f'A4{s_}', f'AT4{s_}', f'NB{s_}', f'Tb{s_}'
                                    buf = NB[s_]
                                    T(lambda e: e.matmul(psb[bA][:, 0:256], lhsT=btT[:, h, :], rhs=arT[:, h, :, :], start=True, stop=True),
                                      ['btT', 'arT'], [PS[bA]])
                                    T(lambda e: e.matmul(psb[bA][:, 256:512], lhsT=ktT[:, h, :], rhs=arT[:, h, :, :], start=True, stop=True),
                                      ['ktT', 'arT'], [PS[bA]])
                                    V(lambda e: e.tensor_tensor(out=Am[:, h, :], in0=psb[bA][:], in1=mask4, op=ALU.mult),
                                      [PS[bA], 'cst'], [f'Am{h}'])
                                    V(lambda e: e.tensor_tensor(out=A4[s_][:], in0=psb[bA][:, 0:128].unsqueeze(1).to_broadcast([128, 4, 128]),
                                                                in1=mA, op=ALU.mult), [PS[bA], 'cst'], [ka4])
                                    yield
                                    T(lambda e: e.matmul(psb[s_][:, 0:128], lhsT=arT[:, h, 0, :], rhs=btT[:, h, :],
                                                         start=True, stop=True), ['arT', 'btT'], [PS[s_]])
                                    yield
                                    V(lambda e: e.tensor_tensor(out=AT4[s_][:], in0=psb[s_][:, 0:128].unsqueeze(1).to_broadcast([128, 4, 128]),
                                                                in1=mAT, op=ALU.mult), [PS[s_], 'cst'], [kat4])
                                    G(lambda e: e.tensor_copy(out=buf[:, 1:3, :], in_=identb[:].unsqueeze(1).to_broadcast([128, 2, 128])),
                                      ['identb'], [knb])
                                    G(lambda e: e.tensor_copy(out=buf[:, 0, :], in_=A4[s_][:, 0, :]), [ka4], [knb])
                                    G(lambda e: e.tensor_copy(out=buf[:, 3, :], in_=AT4[s_][:, 0, :]), [kat4], [knb])
                                    yield
                                    for it in range(4):
                                        T(lambda e: e.matmul(psb[s_][:, 0:256], lhsT=buf[:, 3, :], rhs=buf[:, 0:2, :], start=True, stop=True),
                                          [knb], [PS[s_]])
                                        T(lambda e: e.matmul(psb[s_][:, 256:512], lhsT=buf[:, 0, :], rhs=buf[:, 2:4, :], start=True, stop=True),
                                          [knb], [PS[s_]])
                                        yield
                                        V(lambda e: e.tensor_tensor(out=buf[:, 1:3, :], in0=buf[:, 1:3, :],
                                                                    in1=psb[s_][:, 128:384].rearrange("p (a t) -> p a t", a=2), op=ALU.add),
                                          [knb, PS[s_]], [knb])
                                        if it < 3:
                                            A(lambda e: e.copy(out=buf[:, 0, :], in_=psb[s_][:, 0:128]), [PS[s_]], [knb])
                                            A(lambda e: e.copy(out=buf[:, 3, :], in_=psb[s_][:, 384:512]), [PS[s_]], [knb])
                                        yield
                                    for lv in range(1, 4):
                                        lastk = (lv == 3)
                                        T(lambda e: e.matmul(psb[s_][:, 0:128], lhsT=AT4[s_][:, lv, :], rhs=buf[:, 1, :], start=True, stop=True),
                                          [kat4, knb], [PS[s_]])
                                        if not lastk:
                                            T(lambda e: e.matmul(psb[s_][:, 128:256], lhsT=A4[s_][:, lv, :], rhs=buf[:, 2, :], start=True, stop=True),
                                              [ka4, knb], [PS[s_]])
                                        yield
                                        if not lastk:
                                            A(lambda e: e.copy(out=Tb[s_][:], in_=psb[s_][:, 0:256].rearrange("p (a t) -> p a t", a=2)),
                                              [PS[s_]], [ktb])
                                        else:
                                            A(lambda e: e.copy(out=Tb[s_][:, 0, :], in_=psb[s_][:, 0:128]), [PS[s_]], [ktb])
                                        yield
                                        T(lambda e: e.matmul(psb[s_][:, 256:384], lhsT=buf[:, 2, :], rhs=Tb[s_][:, 0, :], start=True, stop=True),
                                          [knb, ktb], [PS[s_]])
                                        if not lastk:
                                            T(lambda e: e.matmul(psb[s_][:, 384:512], lhsT=buf[:, 1, :], rhs=Tb[s_][:, 1, :], start=True, stop=True),
                                              [knb, ktb], [PS[s_]])
                                        yield
                                        if not lastk:
                                            V(lambda e: e.tensor_tensor(out=buf[:, 1:3, :], in0=buf[:, 1:3, :],
                                                                        in1=psb[s_][:, 256:512].rearrange("p (a t) -> p a t", a=2), op=ALU.add),
                                              [knb, PS[s_]], [knb])
                                        else:
                                            V(lambda e: e.tensor_tensor(out=N_all[:, h, :], in0=buf[:, 1, :], in1=psb[s_][:, 256:384], op=ALU.add),
                                              [knb, PS[s_]], [f'N{h}'])
                                for grp in range(2):
                                    gens = [head_chain(grp * 8 + q8, q8) for q8 in range(8)]
                                    while gens:
                                        alive = []
                                        for gch in gens:
                                            try:
                                                next(gch)
                                                alive.append(gch)
                                            except StopIteration:
                                                pass
                                        gens = alive
                                amk = [f'Am{h}' for h in range(16)]
                                for h in range(16):
                                    o = psb[h // 8][:, (h % 8) * 64:(h % 8 + 1) * 64]
                                    T(lambda e: e.matmul(o, lhsT=arT[:, h, 0, :], rhs=Sb[:, h, :], start=True, stop=False),
                                      ['arT', 'Sb'], [PS[h // 8]])
                                    T(lambda e: e.matmul(o, lhsT=Am[:, h, 256:384], rhs=vb[:, h * 64:(h + 1) * 64], start=False, stop=True),
                                      [f'Am{h}', 'vb'], [PS[h // 8]])
                                for hq in range(2):
                                    A(lambda e: e.copy(out=Xb[:, hq * 512:(hq + 1) * 512], in_=psb[hq][:]), [PS[hq]], ['Xb'])
                                for h in range(16):
                                    o = psb[2 + h // 8][:, (h % 8) * 64:(h % 8 + 1) * 64]
                                    T(lambda e: e.matmul(o, lhsT=N_all[:, h, :], rhs=Xb[:, h * 64:(h + 1) * 64], start=True, stop=True),
                                      [f'N{h}', 'Xb'], [PS[2 + h // 8]])
                                for hq in range(2):
                                    V(lambda e: e.tensor_copy(out=Ub[:, hq * 512:(hq + 1) * 512], in_=psb[2 + hq][:]), [PS[2 + hq]], ['Ub'])
                                for h in range(16):
                                    o = psb[4 + h // 8][:, (h % 8) * 64:(h % 8 + 1) * 64]
                                    hs = slice(h * 64, (h + 1) * 64)
                                    T(lambda e: e.matmul(o, lhsT=arT[:, h, 1, :], rhs=Sb[:, h, :], start=True, stop=False),
                                      ['arT', 'Sb'], [PS[4 + h // 8]])
                                    T(lambda e: e.matmul(o, lhsT=Am[:, h, 128:256], rhs=Ub[:, hs], start=False, stop=False),
                                      [f'Am{h}', 'Ub'], [PS[4 + h // 8]])
                                    T(lambda e: e.matmul(o, lhsT=Am[:, h, 384:512], rhs=vb[:, hs], start=False, stop=True),
                                      [f'Am{h}', 'vb'], [PS[4 + h // 8]])
                                for h in range(16):
                                    o = psb[6 + h // 8][0:64, (h % 8) * 64:(h % 8 + 1) * 64]
                                    hs = slice(h * 64, (h + 1) * 64)
                                    T(lambda e: e.matmul(o, lhsT=bt_tok[:, hs], rhs=Ub[:, hs], start=True, stop=False),
                                      ['bt_tok', 'Ub'], [PS[6 + h // 8]])
                                    T(lambda e: e.matmul(o, lhsT=kt_tok[:, hs], rhs=vb[:, hs], start=False, stop=True),
                                      ['kt_tok', 'vb'], [PS[6 + h // 8]])
                                for hq in range(2):
                                    V(lambda e: e.tensor_tensor(out=S_T[:, hq * 8:(hq + 1) * 8, :], in0=S_T[:, hq * 8:(hq + 1) * 8, :],
                                                                in1=psb[6 + hq][0:64, :].rearrange("p (h i) -> p h i", h=8), op=ALU.add),
                                      ['S_T', PS[6 + hq]], ['S_T'])
                                V(lambda e: e.tensor_tensor(out=S_T[:], in0=S_T[:], in1=PC[:].unsqueeze(2).to_broadcast([64, 16, 64]),
                                                            op=ALU.mult), ['S_T', 'PC'], ['S_T'])
                                A(lambda e: e.copy(out=Sb[:], in_=S_T[:]), ['S_T'], ['Sb'])
                                if dr == 0:
                                    for hq in range(2):
                                        A(lambda e: e.copy(out=yt[:, hq * 512:(hq + 1) * 512], in_=psb[4 + hq][:]), [PS[4 + hq]], ['yt'])
                                else:
                                    kb.dma('sp', yt[:], yf_s[rows, :], reads=['yfs'], writes=['yt'])
                                    for hq in range(2):
                                        V(lambda e: e.tensor_tensor(out=yt[:, hq * 512:(hq + 1) * 512], in0=yt[:, hq * 512:(hq + 1) * 512],
                                                                    in1=psb[4 + hq][:], op=ALU.add), ['yt', PS[4 + hq]], ['yt'])
                                kb.dma('sp', yf_s[rows, :], yt[:], reads=['yt'], writes=['yfs'])
                            if not sample:
                                for h in range(16):
                                    T(lambda e: e.transpose(out=psb[h // 8][0:64, (h % 8) * 64:(h % 8 + 1) * 64], in_=S_T[:, h, :],
                                                            identity=cst[0:64, C_ID:C_ID + 64]), ['S_T', 'cst'], [PS[h // 8]])
                                for hq in range(2):
                                    V(lambda e: e.tensor_copy(out=stl[:, hq * 8:(hq + 1) * 8, :],
                                                              in_=psb[hq][0:64, :].rearrange("p (h i) -> p h i", h=8)),
                                      [PS[hq]], ['stl'])
                                kb.dma('sp', nst_d[sq_i, j, dr].rearrange("h i j -> i h j"), stl[:], reads=['stl'], writes=['nst'])

                with kb.phase():
                    modb, lnb = setup_mod(i, 0, want=(2,), ln=True)
                    wk = post_work()
                    wo = kb.sb('wo', [128, 8, D], BF16)
                    for c in range(8):
                        load_cast(wo[:, c, :], rwo_d[j, c * 128:(c + 1) * 128, :], w=['wo'])
                    lxg = kb.sb('lxg', [128, D])
                    lxb = kb.sb('lxb', [128, D])
                    kb.dma('sp', lxg[:], lnxg_d[j:j + 1, :].partition_broadcast(128), writes=['lxg'])
                    kb.dma('sp', lxb[:], lnxb_d[j:j + 1, :].partition_broadcast(128), writes=['lxb'])
                    y_t = kb.sb('y_t', [128, D])
                    v_t = kb.sb('v_t', [128, D])
                    g_t = kb.sb('g_t', [128, D])
                    f1 = kb.sb('f1', [128, D])
                    m16 = kb.sb('m16', [128, 6, 16])
                    zb = kb.sb('zb', [128, D], BF16)
                    zT = kb.sb('zT', [128, 8, 128], BF16)
                    for tt in range(NT):
                        rows = slice(tt * 128, (tt + 1) * 128)
                        kb.dma('sp', y_t[:], yf_s[rows, :], writes=['y_t'])
                        kb.dma('sp', v_t[:], v_s[rows, :], writes=['v_t'])
                        kb.dma('sp', g_t[:], g_s[rows, :], writes=['g_t'])
                        y3 = y_t[:].rearrange("p (h d) -> p h d", h=16)
                        V(lambda e: e.tensor_reduce(out=m16[:, 0, :], in_=y3, axis=AX.X, op=ALU.add), ['y_t'], ['m16'])
                        V(lambda e: e.tensor_scalar(out=m16[:, 0, :], in0=m16[:, 0, :], scalar1=1.0 / 64, scalar2=None, op0=ALU.mult),
                          ['m16'], ['m16'])
                        V(lambda e: e.tensor_tensor(out=y3, in0=y3, in1=m16[:, 0, :].unsqueeze(2).to_broadcast([128, 16, 64]),
                                                    op=ALU.subtract), ['y_t', 'm16'], ['y_t'])
                        A(lambda e: e.activation(out=f1[:], in_=y_t[:], func=AF.Square), ['y_t'], ['f1'])
                        V(lambda e: e.tensor_reduce(out=m16[:, 1, :], in_=f1[:].rearrange("p (h d) -> p h d", h=16), axis=AX.X,
                                                    op=ALU.add), ['f1'], ['m16'])
                        V(lambda e: e.tensor_scalar(out=m16[:, 1, :], in0=m16[:, 1, :], scalar1=1.0 / 64, scalar2=GN_EPS,
                                                    op0=ALU.mult, op1=ALU.add), ['m16'], ['m16'])
                        A(lambda e: e.activation(out=m16[:, 2, :], in_=m16[:, 1, :], func=AF.Sqrt), ['m16'], ['m16'])
                        V(lambda e: e.reciprocal(out=m16[:, 3, :], in_=m16[:, 2, :]), ['m16'], ['m16'])
                        V(lambda e: e.tensor_tensor(out=y3, in0=y3, in1=m16[:, 3, :].unsqueeze(2).to_broadcast([128, 16, 64]),
                                                    op=ALU.mult), ['y_t', 'm16'], ['y_t'])
                        V(lambda e: e.tensor_tensor(out=y_t[:], in0=y_t[:], in1=lxg[:], op=ALU.mult), ['y_t', 'lxg'], ['y_t'])
                        V(lambda e: e.tensor_tensor(out=y_t[:], in0=y_t[:], in1=lxb[:], op=ALU.add), ['y_t', 'lxb'], ['y_t'])
                        V(lambda e: e.tensor_tensor(out=f1[:].rearrange("p (h d) -> p h d", h=16),
                                                    in0=v_t[:].rearrange("p (h d) -> p h d", h=16),
                                                    in1=bon[:, tt, :].unsqueeze(2).to_broadcast([128, 16, 64]), op=ALU.mult),
                          ['v_t', 'bon'], ['f1'])
                        V(lambda e: e.tensor_tensor(out=y_t[:], in0=y_t[:], in1=f1[:], op=ALU.add), ['y_t', 'f1'], ['y_t'])
                        V(lambda e: e.tensor_tensor(out=zb[:], in0=y_t[:], in1=g_t[:], op=ALU.mult), ['y_t', 'g_t'], ['zb'])
                        transpose8(zb, 'zb', zT[:], 'zT', 7)
                        for half in range(2):
                            for c in range(8):
                                T(lambda e: e.matmul(psb[half][:], lhsT=zT[:, c, :], rhs=wo[:, c, half * 512:(half + 1) * 512],
                                                     start=(c == 0), stop=(c == 7)), ['zT', 'wo'], [PS[half]])
                        post_sublayer(tt, (0, 1), modb, lnb, wk)
                kb.st = old_st

        def rope_apply(src, skey, H, dst, dkey, ropet, tmp):
            xv = src.rearrange("p (h a s f) -> p h a s f", h=H, a=2, s=2)
            dv = dst.rearrange("p (h a s f) -> p h a s f", h=H, a=2, s=2)
            x1, x2 = xv[:, :, :, 0, :], xv[:, :, :, 1, :]
            d1, d2 = dv[:, :, :, 0, :], dv[:, :, :, 1, :]
            cosb = ropet[:, 0:32].rearrange("p (a f) -> p a f", a=2).unsqueeze(1).to_broadcast([128, H, 2, 16])
            sinb = ropet[:, 32:64].rearrange("p (a f) -> p a f", a=2).unsqueeze(1).to_broadcast([128, H, 2, 16])
            t1 = tmp[0][:, 0:H * 32].rearrange("p (h a f) -> p h a f", h=H, a=2)
            t2 = tmp[1][:, 0:H * 32].rearrange("p (h a f) -> p h a f", h=H, a=2)
            V(lambda e: e.tensor_tensor(out=t1, in0=x1, in1=cosb, op=ALU.mult), [skey, 'ropet'], ['rt1'])
            V(lambda e: e.tensor_tensor(out=t2, in0=x2, in1=sinb, op=ALU.mult), [skey, 'ropet'], ['rt2'])
            V(lambda e: e.tensor_tensor(out=d1, in0=t1, in1=t2, op=ALU.subtract), ['rt1', 'rt2'], [dkey])
            V(lambda e: e.tensor_tensor(out=t1, in0=x2, in1=cosb, op=ALU.mult), [skey, 'ropet'], ['rt1'])
            V(lambda e: e.tensor_tensor(out=t2, in0=x1, in1=sinb, op=ALU.mult), [skey, 'ropet'], ['rt2'])
            V(lambda e: e.tensor_tensor(out=d2, in0=t1, in1=t2, op=ALU.add), ['rt1', 'rt2'], [dkey])

        def attn_sublayer(i):
            j = i // 2
            with kb.phase():
                modb, lnb = setup_mod(i, 0)
                wk = post_work()
                wqkv = kb.sb('wqkv', [128, 8, 1536], BF16)
                for c in range(8):
                    load_cast(wqkv[:, c, :], wqkv_d[j, c * 128:(c + 1) * 128, :], w=['wqkv'])
                wo = kb.sb('wo', [128, 8, D], BF16)
                for c in range(8):
                    load_cast(wo[:, c, :], awo_d[j, c * 128:(c + 1) * 128, :], w=['wo'])
                qnb = kb.sb('qnb', [128, 64])
                knb = kb.sb('knb', [128, 64])
                kb.dma('sp', qnb[:], qn_d[j:j + 1, :].partition_broadcast(128), writes=['qnb'])
                kb.dma('sp', knb[:], kn_d[j:j + 1, :].partition_broadcast(128), writes=['knb'])
                hT = kb.sb('hT', [128, 8, NTOK], BF16)
                kT = kb.sb('kT', [64, 4, 1536], BF16)
                Vx = kb.sb('Vx', [128, 12, 4, 65], BF16)
                htok = kb.sb('htok', [128, D])
                hb = kb.sb('hb', [128, D], BF16)
                ckt = kb.sb('ckt', [128, 256], BF16)
                sq = htok
                ss = kb.sb('ss', [128, 3, 16])
                qf = kb.sb('qf', [128, D])
                qb = kb.sb('qb', [128, D], BF16)
                kf = kb.sb('kf', [128, 256])
                vf = kb.sb('vf', [128, 256])
                kbb = kb.sb('kbb', [128, 256], BF16)
                ropet = kb.sb('ropet', [128, 64])
                rtmp = [kb.sb(f'rtmp{b}', [128, 256]) for b in range(2)]
                qT = kb.sb('qT', [64, 16, 128], BF16)
                PT = [kb.sb(f'PT{b}', [128, 512], BF16) for b in range(2)]
                rc4 = kb.sb('rc4', [128, 4])
                ob = kb.sb('ob', [128, D], BF16)
                oT = kb.sb('oT', [128, 8, 128], BF16)
                V(lambda e: e.memset(Vx[:, :, :, 64:65], 1.0), [], ['Vx'])

                def rms(src_ap_list, keys, H, gb, out_f, okey):
                    off = 0
                    for ap, kk_ in zip(src_ap_list, keys):
                        w_ = ap.shape[-1]
                        A(lambda e: e.activation(out=sq[:, off:off + w_], in_=ap, func=AF.Square), [kk_], ['htok'])
                        off += w_
                    V(lambda e: e.tensor_reduce(out=ss[:, 0, 0:H], in_=sq[:, 0:H * 64].rearrange("p (h d) -> p h d", h=H),
                                                axis=AX.X, op=ALU.add), ['htok'], ['ss'])
                    V(lambda e: e.tensor_scalar(out=ss[:, 0, 0:H], in0=ss[:, 0, 0:H], scalar1=1.0 / 64, scalar2=RMS_EPS,
                                                op0=ALU.mult, op1=ALU.add), ['ss'], ['ss'])
                    A(lambda e: e.activation(out=ss[:, 1, 0:H], in_=ss[:, 0, 0:H], func=AF.Sqrt), ['ss'], ['ss'])
                    V(lambda e: e.reciprocal(out=ss[:, 2, 0:H], in_=ss[:, 1, 0:H]), ['ss'], ['ss'])
                    off = 0
                    for ap, kk_ in zip(src_ap_list, keys):
                        w_ = ap.shape[-1]
                        hh = w_ // 64
                        h0 = off // 64
                        V(lambda e: e.tensor_tensor(out=out_f[:, off:off + w_].rearrange("p (h d) -> p h d", h=hh),
                                                    in0=ap.rearrange("p (h d) -> p h d", h=hh),
                                                    in1=ss[:, 2, h0:h0 + hh].unsqueeze(2).to_broadcast([128, hh, 64]),
                                                    op=ALU.mult), [kk_, 'ss'], [okey])
                        off += w_
                    V(lambda e: e.tensor_tensor(out=out_f[:, 0:H * 64].rearrange("p (h d) -> p h d", h=H),
                                                in0=out_f[:, 0:H * 64].rearrange("p (h d) -> p h d", h=H),
                                                in1=gb[:].unsqueeze(1).to_broadcast([128, H, 64]), op=ALU.mult),
                      [okey, 'qnb', 'knb'], [okey])

                for (sq_i, tiles) in SEQS:
                    sample = (sq_i == 2)
                    kc0 = 4 if sample else 0
                    nch = kc0 + len(tiles)
                    if sample:
                        for kc in range(4):
                            load_cast(ckt[:], ck_d[j, kc * 128:(kc + 1) * 128, :], w=['ckt'])
                            pv = psbf(7)
                            for kv in range(4):
                                T(lambda e: e.transpose(out=pv[0:64, kv * 128:(kv + 1) * 128], in_=ckt[:, kv * 64:(kv + 1) * 64],
                                                        identity=identb[:]), ['ckt', 'identb'], [PS[7]])
                            A(lambda e: e.copy(out=kT[:, :, kc * 128:(kc + 1) * 128],
                                               in_=pv[0:64, 0:512].rearrange("p (a t) -> p a t", a=4)), [PS[7]], ['kT'])
                            load_cast(Vx[:, kc, :, 0:64], cv_d[j, kc * 128:(kc + 1) * 128, :].rearrange("p (a d) -> p a d", a=4),
                                      w=['Vx'])
                    for lt, tt in enumerate(tiles):
                        kc = kc0 + lt
                        make_h(tt, modb, htok[:], 'htok')
                        A(lambda e: e.copy(out=hb[:], in_=htok[:]), ['htok'], ['hb'])
                        transpose8(hb, 'hb', hT[:, :, tt * 128:(tt + 1) * 128], 'hT', 7)
                        for c in range(8):
                            T(lambda e: e.matmul(psb[6][:], lhsT=hT[:, c, tt * 128:(tt + 1) * 128], rhs=wqkv[:, c, 1024:1536],
                                                 start=(c == 0), stop=(c == 7)), ['hT', 'wqkv'], [PS[6]])
                        rms([psb[6][:, 0:256]], [PS[6]], 4, knb, kf, 'kf')
                        if sample:
                            kb.dma('sp', ropet[:], rope_d[(tt - 4) * 128:(tt - 3) * 128, :], writes=['ropet'])
                            rope_apply(kf[:], 'kf', 4, kbb[:], 'kbb', ropet, rtmp)
                        else:
                            kb.dma('sp', nk_d[sq_i, j, lt * 128:(lt + 1) * 128, :], kf[:], reads=['kf'], writes=['nk'])
                            A(lambda e: e.copy(out=vf[:], in_=psb[6][:, 256:512]), [PS[6]], ['vf'])
                            kb.dma('sp', nv_d[sq_i, j, lt * 128:(lt + 1) * 128, :], vf[:], reads=['vf'], writes=['nv'])
                            V(lambda e: e.tensor_copy(out=kbb[:], in_=kf[:]), ['kf'], ['kbb'])
                        A(lambda e: e.copy(out=Vx[:, kc, :, 0:64], in_=psb[6][:, 256:512].rearrange("p (a d) -> p a d", a=4)),
                          [PS[6]], ['Vx'])
                        pv = psbf(7)
                        for kv in range(4):
                            T(lambda e: e.transpose(out=pv[0:64, kv * 128:(kv + 1) * 128], in_=kbb[:, kv * 64:(kv + 1) * 64],
                                                    identity=identb[:]), ['kbb', 'identb'], [PS[7]])
                        A(lambda e: e.copy(out=kT[:, :, kc * 128:(kc + 1) * 128],
                                           in_=pv[0:64, 0:512].rearrange("p (a t) -> p a t", a=4)), [PS[7]], ['kT'])
                    nsc = 0
                    for lt, tt in enumerate(tiles):
                        for half in range(2):
                            for c in range(8):
                                T(lambda e: e.matmul(psb[half][:], lhsT=hT[:, c, tt * 128:(tt + 1) * 128],
                                                     rhs=wqkv[:, c, half * 512:(half + 1) * 512],
                                                     start=(c == 0), stop=(c == 7)), ['hT', 'wqkv'], [PS[half]])
                        rms([psb[0][:], psb[1][:]], [PS[0], PS[1]], 16, qnb, qf, 'qf')
                        if sample:
                            kb.dma('sp', ropet[:], rope_d[(tt - 4) * 128:(tt - 3) * 128, :], writes=['ropet'])
                            for hh in range(2):
                                rope_apply(qf[:, hh * 512:(hh + 1) * 512], 'qf', 8, qb[:, hh * 512:(hh + 1) * 512], 'qb',
                                           ropet, rtmp)
                        else:
                            V(lambda e: e.tensor_copy(out=qb[:], in_=qf[:]), ['qf'], ['qb'])
                        for hh in range(2):
                            pv = psbf(2 + hh)
                            for h8 in range(8):
                                h = hh * 8 + h8
                                T(lambda e: e.transpose(out=pv[0:64, h8 * 128:(h8 + 1) * 128], in_=qb[:, h * 64:(h + 1) * 64],
                                                        identity=identb[:]), ['qb', 'identb'], [PS[2 + hh]])
                            A(lambda e: e.copy(out=qT[:, hh * 8:(hh + 1) * 8, :],
                                               in_=pv[0:64, :].rearrange("p (a t) -> p a t", a=8)), [PS[2 + hh]], ['qT'])
                        work = [(kv, kc) for kv in range(4) for kc in range(nch)]

                        def emit_score(n):
                            kv, kc = work[n]
                            sbk = 2 + (n % 2)
                            T(lambda e: e.matmul(psb[sbk][:], lhsT=kT[:, kv, kc * 128:(kc + 1) * 128],
                                                 rhs=qT[:, kv * 4:(kv + 1) * 4, :], start=True, stop=True),
                              ['kT', 'qT'], [PS[sbk]])

                        emit_score(0)
                        for n, (kv, kc) in enumerate(work):
                            if n + 1 < len(work):
                                emit_score(n + 1)
                            sbk = 2 + (n % 2)
                            pt = PT[n % 2]
                            ptk = f'PT{n % 2}'
                            A(lambda e: e.activation(out=pt[:], in_=psb[sbk][:], func=AF.Exp, scale=ATTN_SCALE),
                              [PS[sbk]], [ptk])
                            for g in range(4):
                                T(lambda e: e.matmul(psb[4 + kv][:, g * 65:(g + 1) * 65], lhsT=pt[:, g * 128:(g + 1) * 128],
                                                     rhs=Vx[:, kc, kv, :], start=(kc == 0), stop=(kc == nch - 1)),
                                  [ptk, 'Vx'], [PS[4 + kv]])
                            if kc == nch - 1:
                                po = psb[4 + kv][:, 0:260].rearrange("p (g d) -> p g d", g=4)
                                V(lambda e: e.reciprocal(out=rc4[:], in_=po[:, :, 64]), [PS[4 + kv]], ['rc4'])
                                V(lambda e: e.tensor_tensor(out=ob[:, kv * 256:(kv + 1) * 256].rearrange("p (g d) -> p g d", g=4),
                                                            in0=po[:, :, 0:64], in1=rc4[:].unsqueeze(2).to_broadcast([128, 4, 64]),
                                                            op=ALU.mult), [PS[4 + kv], 'rc4'], ['ob'])
                        transpose8(ob, 'ob', oT[:], 'oT', 2)
                        for half in range(2):
                            for c in range(8):
                                T(lambda e: e.matmul(psb[half][:], lhsT=oT[:, c, :], rhs=wo[:, c, half * 512:(half + 1) * 512],
                                                     start=(c == 0), stop=(c == 7)), ['oT', 'wo'], [PS[half]])
                        post_sublayer(tt, (0, 1), modb, lnb, wk)

        done_ada = set()
        for (i, which) in subs:
            if i not in done_ada:
                if (i, 1) in subs:
                    peer_convert(i)
                adaln(i)
                done_ada.add(i)
            if which == 1:
                peer_sublayer(i)
            else:
                (rwkv_sublayer if i % 2 == 0 else attn_sublayer)(i)

        for tt in range(NT):
            kb.dma('sp', y_d[tt * 128:(tt + 1) * 128, :], x_res[:, tt, :], reads=[f'x{tt}'], writes=['y'])
        kb.barrier()
        print("ninst", kb.ninst, flush=True)
        _LAST_KB['kb'] = kb
    return nc


_LAST_KB = {}


def _consts():
    c = np.zeros((128, NCST), np.float32)
    s = np.arange(128)[:, None]
    t = np.arange(128)[None, :]
    c[:, C_ID:C_ID + 128] = np.eye(128)
    c[:, C_TRIF:C_TRIF + 128] = (s <= t)
    c[:, C_TRIB:C_TRIB + 128] = (s >= t)
    strictF, inclF = (s < t), (s <= t)
    strictB, inclB = (s > t), (s >= t)
    c[:, C_M4F:C_M4F + 512] = np.concatenate([strictF, inclF, strictF, inclF], 1)
    c[:, C_M4B:C_M4B + 512] = np.concatenate([strictB, inclB, strictB, inclB], 1)
    idx = np.arange(128)
    def bm(b):
        return (idx[:, None] // b == idx[None, :] // b)
    blocks = [bm(16), bm(32) & ~bm(16), bm(64) & ~bm(32), ~bm(64)]
    c[:, C_MU:C_MU + 512] = np.concatenate([strictF & b_ for b_ in blocks], 1)
    c[:, C_ML:C_ML + 512] = np.concatenate([strictB & b_ for b_ in blocks], 1)
    c[:, C_IOTA:C_IOTA + 16] = np.arange(16)[None, :]
    c[:, C_ONES:C_ONES + 128] = 1.0
    sel2 = np.zeros((2, 256), np.float32)
    sel2[0, :128] = 1.0
    sel2[1, 128:] = 1.0
    n = np.arange(1024)
    row = (n // 64).astype(np.float32)
    col = (n % 64).astype(np.float32)
    freqs = (10000.0 ** (-np.arange(16, dtype=np.float32) / 16)).astype(np.float32)
    ang = np.stack([row[:, None] * freqs, col[:, None] * freqs], axis=1).astype(np.float32)
    rope = np.concatenate([np.cos(ang).reshape(1024, 32), np.sin(ang).reshape(1024, 32)], 1).astype(np.float32)
    return c, sel2, rope


def prep_inputs(inputs, x_override=None):
    f = lambda a: np.ascontiguousarray(np.asarray(a, dtype=np.float32))
    I = {k: np.asarray(v) for k, v in inputs.items()}
    cst, sel2, rope = _consts()
    shared = {
        "ada_w": f(I['ada_w']), "ada_b": f(I['ada_b']), "ln_g": f(I['ln_g']), "ln_b": f(I['ln_b']),
        "muT": f(I['rwkv_mu'].reshape(2, 6, 8, 128).transpose(0, 3, 1, 2)),
        "wrkv": f(I['rwkv_wrkv']), "rwo": f(I['rwkv_wo']), "w0": f(I['rwkv_w0']),
        "w1c": f(I['rwkv_w1'].transpose(0, 2, 1, 3).reshape(2, 1024, 128)),
        "w2c": f(I['rwkv_w2'].reshape(2, 128, 1024)),
        "a0": f(I['rwkv_a0']),
        "a1c": f(I['rwkv_a1'].transpose(0, 2, 1, 3).reshape(2, 1024, 128)),
        "a2c": f(I['rwkv_a2'].reshape(2, 128, 1024)),
        "g1": f(I['rwkv_g1']), "g2": f(I['rwkv_g2']),
        "rkk": f(I['rwkv_kk']), "rka": f(I['rwkv_ka']), "rrk": f(I['rwkv_rk'].reshape(2, 1024)),
        "lnxg": f(I['rwkv_lnx_g']), "lnxb": f(I['rwkv_lnx_b']),
        "wqkv": f(I['attn_wqkv']), "awo": f(I['attn_wo']), "qn": f(I['attn_qn']), "kn": f(I['attn_kn']),
        "pwq": f(I['peer_wq']), "pkT": f(I['peer_keys'].transpose(0, 1, 3, 2)),
        "cst": cst, "sel2": sel2, "rope": rope,
    }
    for i in range(4):
        shared[f"pu{i}"] = f(I['peer_u'][i])
        shared[f"pv{i}"] = f(I['peer_v'][i])
    in_maps = []
    for c in range(8):
        m = dict(shared)
        if x_override is not None:
            xp, xs = x_override
        else:
            xp, xs = I['x_prompt'], I['x_sample']
        m["xin"] = f(np.concatenate([xp[2 * c], xp[2 * c + 1], xs[c]], 0))
        m["cond"] = f(np.stack([I['c_ctx'], I['c'][c]], 0))
        m["st_in"] = f(I['state_rwkv'][c])
        m["ck"] = f(I['cache_k'][c].reshape(2, 512, 256))
        m["cv"] = f(I['cache_v'][c].reshape(2, 512, 256))
        in_maps.append(m)
    return in_maps


def assemble(results):
    yp = np.zeros((16, 256, 1024), np.float32)
    ys = np.zeros((8, 1024, 1024), np.float32)
    nst = np.zeros((16, 2, 2, 16, 64, 64), np.float32)
    nk = np.zeros((16, 2, 256, 4, 64), np.float32)
    nv = np.zeros((16, 2, 256, 4, 64), np.float32)
    for c, r in enumerate(results):
        y = r["y"]
        yp[2 * c] = y[0:256]
        yp[2 * c + 1] = y[256:512]
        ys[c] = y[512:]
        nst[2 * c:2 * c + 2] = r["nst"]
        nk[2 * c:2 * c + 2] = r["nk"].reshape(2, 2, 256, 4, 64)
        nv[2 * c:2 * c + 2] = r["nv"].reshape(2, 2, 256, 4, 64)
    return yp, ys, nst, nk, nv


_NC_CACHE = {}


def kernel(**inputs):
    if 'nc' not in _NC_CACHE:
        _NC_CACHE['nc'] = build()
    nc = _NC_CACHE['nc']
    in_maps = prep_inputs(inputs)
    res = run_bass_kernel_spmd(nc, in_maps, core_ids=list(range(8)))
    return assemble(res.results)
```

```python
import contextlib
import numpy as np
import concourse.bass as bass
import concourse.mybir as mybir
from concourse.bass_utils import run_bass_kernel_spmd

F32 = mybir.dt.float32
BF16 = mybir.dt.bfloat16
I32 = mybir.dt.int32
U32 = mybir.dt.uint32
AF = mybir.ActivationFunctionType
ALU = mybir.AluOpType
AX = mybir.AxisListType

D = 1024
NT = 12
NTOK = 1536
DEPTH = 4
ALPHA = float((2 * DEPTH) ** 0.25)
LN_EPS = 1e-5
GN_EPS = 64 * 1e-5
RMS_EPS = 1e-6
ATTN_SCALE = 0.125
NEG_EXP_HALF = -float(np.exp(-0.5))
SEQS = [(0, [0, 1]), (1, [2, 3]), (2, [4, 5, 6, 7, 8, 9, 10, 11])]
HPAD = 1540

C_ID, C_IOTA, C_ONES = 0, 128, 144
CR0 = 272
C_TRIF, C_TRIB, C_M4F, C_M4B, C_MU, C_ML = 272, 400, 528, 1040, 1552, 2064
NCST = 2576


def padcol(tt):
    return tt * 128 + 1 + (1 if tt >= 2 else 0) + (1 if tt >= 4 else 0)


class KB:
    def __init__(self, nc, stack, n_dma_sems=4):
        self.nc = nc
        self.st = stack
        self.eng = {'pe': nc.tensor, 'dve': nc.vector, 'act': nc.scalar, 'pool': nc.gpsimd, 'sp': nc.sync}
        self.sem = {}
        self.cnt = {}
        for e in self.eng:
            self.sem[e] = stack.enter_context(nc.semaphore("s_" + e))
            self.cnt[e] = 0
        self.n_dma_sems = n_dma_sems
        self.dsem = {}
        self.dcnt = {}
        self.drr = {}
        self.nds = {'sp': 8, 'act': 1, 'pool': 16}
        for q in ('sp', 'act', 'pool'):
            self.drr[q] = 0
            for k in range(self.nds[q]):
                self.dsem[(q, k)] = stack.enter_context(nc.semaphore(f"d_{q}{k}"))
                self.dcnt[(q, k)] = 0
        self.bgsems = [stack.enter_context(nc.semaphore(f"bg{i}")) for i in range(32)]
        self.bgcnt = [0] * 32
        self.bgrr = 0
        self.bs_arrive = stack.enter_context(nc.semaphore("bs_arrive"))
        self.bs_go = stack.enter_context(nc.semaphore("bs_go"))
        self.nbar = 0
        self.seen = {e: {} for e in self.eng}
        self.lastw = {}
        self.readers = {}
        self.ninst = 0
        self.uid = 0
        self.rec = {e: [] for e in self.eng}

    def sb(self, name, shape, dt=F32):
        self.uid += 1
        return self.st.enter_context(self.nc.sbuf_tensor(f"{name}_{self.uid}", list(shape), dt))

    def ps(self, name, shape, dt=F32):
        return self.st.enter_context(self.nc.psum_tensor(name, list(shape), dt))

    EXCL = frozenset(f'ps{i}' for i in range(8))

    def _deps(self, reads, writes, e=None):
        deps = []
        for k in reads:
            if k in self.lastw:
                deps.append(self.lastw[k])
            if k in self.EXCL:
                deps.extend((sk, v) for sk, v in self.readers.get(k, {}).items() if sk != e)
        for k in writes:
            if k in self.lastw:
                deps.append(self.lastw[k])
            deps.extend(self.readers.get(k, {}).items())
        return deps

    def _wait(self, e, deps):
        best = {}
        for (sk, v) in deps:
            if v > best.get(sk, 0):
                best[sk] = v
        for sk, v in best.items():
            if self.seen[e].get(sk, 0) >= v:
                continue
            sem = self.sem[sk] if isinstance(sk, str) else self.dsem[sk]
            self.eng[e].wait_ge(sem, v)
            self.rec[e].append(('w', sk, v))
            self.seen[e][sk] = v

    def _record(self, tok, reads, writes):
        for k in reads:
            d = self.readers.setdefault(k, {})
            if tok[1] > d.get(tok[0], 0):
                d[tok[0]] = tok[1]
        for k in writes:
            self.lastw[k] = tok
            self.readers[k] = {}

    def op(self, e, fn, reads=(), writes=()):
        self._wait(e, self._deps(reads, writes, e))
        ins = fn(self.eng[e])
        self.cnt[e] += 1
        ins.then_inc(self.sem[e], 1)
        self.rec[e].append(('i', e, 1))
        self._record((e, self.cnt[e]), reads, writes)
        self.ninst += 1
        return ins

    def _dma_pre(self, q):
        k = self.drr[q]
        if self.dcnt[(q, k)] > 0:
            self._wait(q, [((q, k), self.dcnt[(q, k)])])

    def _dma_fin(self, q, ins, reads, writes):
        k = self.drr[q]
        self.drr[q] = (k + 1) % self.nds[q]
        self.dcnt[(q, k)] += 16
        ins.then_inc(self.dsem[(q, k)], 16)
        self.rec[q].append(('i', (q, k), 16))
        self._record(((q, k), self.dcnt[(q, k)]), reads, writes)
        self.ninst += 1

    def dma(self, q, out, in_, reads=(), writes=(), **kw):
        self._wait(q, self._deps(reads, writes))
        self._dma_pre(q)
        ins = self.eng[q].dma_start(out=out, in_=in_, **kw)
        self._dma_fin(q, ins, reads, writes)
        return ins

    def gather(self, out, in_, idx_ap, reads=(), writes=()):
        q = 'pool'
        self._wait(q, self._deps(reads, writes))
        self._dma_pre(q)
        ins = self.nc.gpsimd.indirect_dma_start(
            out=out, out_offset=None, in_=in_,
            in_offset=bass.IndirectOffsetOnAxis(ap=idx_ap, axis=0))
        self._dma_fin(q, ins, reads, writes)
        return ins

    def bg_dma(self, out, in_):
        k = self.bgrr
        self.bgrr = (k + 1) % 32
        if self.bgcnt[k]:
            self.eng['pool'].wait_ge(self.bgsems[k], self.bgcnt[k])
        ins = self.eng['pool'].dma_start(out=out, in_=in_)
        self.bgcnt[k] += 16
        ins.then_inc(self.bgsems[k], 16)
        self.ninst += 1

    def bg_wait(self, e):
        for k in range(32):
            if self.bgcnt[k]:
                self.eng[e].wait_ge(self.bgsems[k], self.bgcnt[k])

    def all_tokens(self):
        deps = [(e, c) for e, c in self.cnt.items() if c > 0]
        deps += [(k, c) for k, c in self.dcnt.items() if c > 0]
        return deps

    RESET = True

    def barrier(self, reset=True):
        reset = reset and KB.RESET
        deps = self.all_tokens()
        for e in self.eng:
            self._wait(e, deps)
        if reset:
            self.nbar += 1
            for e in self.eng:
                self.rec[e].append(('b', self.nbar, 0))
            for e in self.eng:
                self.eng[e].sem_inc(self.bs_arrive, 1)
            m = self.eng['sp']
            m.wait_ge(self.bs_arrive, len(self.eng) * self.nbar)
            for sm in list(self.sem.values()) + [v for k, v in self.dsem.items() if k[0] != 'pool']:
                m.sem_clear(sm)
            m.sem_inc(self.bs_go, 1)
            for e in self.eng:
                self.eng[e].wait_ge(self.bs_go, self.nbar)
            for e in self.cnt:
                self.cnt[e] = 0
            for k in self.dcnt:
                if k[0] != 'pool':
                    self.dcnt[k] = 0
            self.seen = {e: {} for e in self.eng}
        self.lastw = {}
        self.readers = {}

    @contextlib.contextmanager
    def phase(self):
        with contextlib.ExitStack() as ph:
            old = self.st
            self.st = ph
            try:
                yield
            finally:
                self.barrier()
                self.st = old


def build(cfg=None):
    cfg = cfg or {}
    subs = cfg.get('subs', [(i, w) for i in range(DEPTH) for w in (0, 1)])
    nc = bass.Bass("TRN2", target_bir_lowering=False)

    def din(name, shape, dt=F32):
        return nc.dram_tensor(name, list(shape), dt, kind="ExternalInput").ap()

    def dout(name, shape, dt=F32):
        return nc.dram_tensor(name, list(shape), dt, kind="ExternalOutput").ap()

    def dscr(name, shape, dt=F32):
        return nc.dram_tensor(name, list(shape), dt, kind="Internal").ap()

    xin = din("xin", [NTOK, D])
    cond_d = din("cond", [2, D])
    st_in = din("st_in", [2, 2, 16, 64, 64])
    ck_d = din("ck", [2, 512, 256])
    cv_d = din("cv", [2, 512, 256])
    ada_w = din("ada_w", [4, D, 6 * D])
    ada_b = din("ada_b", [4, 6 * D])
    ln_g = din("ln_g", [4, 2, D])
    ln_b = din("ln_b", [4, 2, D])
    muT_d = din("muT", [2, 128, 6, 8])
    wrkv_d = din("wrkv", [2, 3, D, D])
    rwo_d = din("rwo", [2, D, D])
    w0_d = din("w0", [2, 2, D])
    w1c_d = din("w1c", [2, D, 128])
    w2c_d = din("w2c", [2, 128, D])
    a0_d = din("a0", [2, 2, D])
    a1c_d = din("a1c", [2, D, 128])
    a2c_d = din("a2c", [2, 128, D])
    g1_d = din("g1", [2, D, 128])
    g2_d = din("g2", [2, 128, D])
    rkk_d = din("rkk", [2, D])
    rka_d = din("rka", [2, D])
    rrk_d = din("rrk", [2, D])
    lnxg_d = din("lnxg", [2, D])
    lnxb_d = din("lnxb", [2, D])
    wqkv_d = din("wqkv", [2, D, 1536])
    awo_d = din("awo", [2, D, D])
    qn_d = din("qn", [2, 64])
    kn_d = din("kn", [2, 64])
    pwq_d = din("pwq", [4, D, 2048])
    pkT_d = din("pkT", [4, 2, 128, 128])
    pu_d = [din(f"pu{i}", [16384, D]) for i in range(4)]
    pv_d = [din(f"pv{i}", [16384, D]) for i in range(4)]
    cst_d = din("cst", [128, NCST])
    sel2_d = din("sel2", [2, 256])
    rope_d = din("rope", [1024, 64])

    y_d = dout("y", [NTOK, D])
    nst_d = dout("nst", [2, 2, 2, 16, 64, 64])
    nk_d = dout("nk", [2, 2, 256, 256])
    nv_d = dout("nv", [2, 2, 256, 256])

    mods_d = dscr("mods", [4, 2, 6 * D])
    r_s = dscr("r_s", [NTOK, D])
    k_s = dscr("k_s", [NTOK, D])
    v_s = dscr("v_s", [NTOK, D])
    g_s = dscr("g_s", [NTOK, D])
    lw_s = dscr("lw_s", [2, NTOK, D])
    a_s = dscr("a_s", [2, NTOK, D])
    yf_s = dscr("yf_s", [NTOK, D])
    z_s = dscr("z_s", [NTOK, D])

    with contextlib.ExitStack() as top:
        kb = KB(nc, top)

        def V(fn, r=(), w=()):
            return kb.op('dve', fn, r, w)

        def A(fn, r=(), w=()):
            return kb.op('act', fn, r, w)

        def G(fn, r=(), w=()):
            return kb.op('pool', fn, r, w)

        def T(fn, r=(), w=()):
            return kb.op('pe', fn, r, w)

        x_res = kb.sb('x_res', [128, NT, D])
        cst = kb.sb('cst', [128, CR0])
        identb = kb.sb('identb', [128, 128], BF16)
        sel2 = kb.sb('sel2', [2, 256])
        siluT = kb.sb('siluT', [128, 16], BF16)
        psb = [kb.ps(f"psb{i}", [128, 512]) for i in range(8)]
        PS = [f'ps{i}' for i in range(8)]
        ident = cst[:, C_ID:C_ID + 128]

        def psbf(i):
            return psb[i][:].bitcast(BF16)

        nc.all_engine_barrier()
        for sm in list(kb.sem.values()) + list(kb.dsem.values()) + kb.bgsems + [kb.bs_arrive, kb.bs_go]:
            nc.sync.sem_clear(sm)
        nc.all_engine_barrier()
        kb.dma('sp', cst[:], cst_d[:, 0:CR0], writes=['cst'])
        kb.dma('sp', sel2[:], sel2_d, writes=['sel2'])
        if cfg.get('load_x', True):
            for tt in range(NT):
                kb.dma('sp', x_res[:, tt, :], xin[tt * 128:(tt + 1) * 128, :], writes=[f'x{tt}'])
        V(lambda e: e.tensor_copy(out=identb[:], in_=ident), ['cst'], ['identb'])

        with kb.phase():
            cnd = kb.sb('cnd', [2, D])
            sil = kb.sb('sil', [2, D])
            kb.dma('sp', cnd[:], cond_d, writes=['cnd'])
            A(lambda e: e.activation(out=sil[:], in_=cnd[:], func=AF.Silu), ['cnd'], ['sil'])
            for c in range(8):
                T(lambda e: e.transpose(out=psb[0][:, c * 2:(c + 1) * 2], in_=sil[0:2, c * 128:(c + 1) * 128],
                                        identity=cst[0:2, C_ID:C_ID + 2]), ['sil', 'cst'], [PS[0]])
            V(lambda e: e.tensor_copy(out=siluT[:], in_=psb[0][:, 0:16]), [PS[0]], ['siluT'])

        def load_cast(dst, src, r=(), w=()):
            kb.dma('pool', dst, src, reads=r, writes=w)

        def adaln(i):
            with kb.phase():
                brow = kb.sb('brow', [2, 3072])
                mrow = kb.sb('mrow', [2, 3072])
                awb = [kb.sb(f'awb{b}', [128, 3072], BF16) for b in range(2)]
                for half in range(2):
                    c0 = half * 3072
                    kb.dma('sp', brow[:], ada_b[i:i + 1, c0:c0 + 3072].partition_broadcast(2), writes=['brow'])
                    for k in range(8):
                        b = k % 2
                        for q in range(2):
                            load_cast(awb[b][:, q * 1536:(q + 1) * 1536],
                                      ada_w[i, k * 128:(k + 1) * 128, c0 + q * 1536:c0 + (q + 1) * 1536],
                                      w=[f'awb{b}'])
                        for cb in range(6):
                            T(lambda e: e.matmul(psb[cb][0:2, :], lhsT=siluT[:, k * 2:(k + 1) * 2],
                                                 rhs=awb[b][:, cb * 512:(cb + 1) * 512], start=(k == 0), stop=(k == 7)),
                              ['siluT', f'awb{b}'], [PS[cb]])
                    for cb in range(6):
                        V(lambda e: e.tensor_tensor(out=mrow[:, cb * 512:(cb + 1) * 512], in0=psb[cb][0:2, :],
                                                    in1=brow[:, cb * 512:(cb + 1) * 512], op=ALU.add),
                          [PS[cb], 'brow'], ['mrow'])
                    kb.dma('sp', mods_d[i, :, c0:c0 + 3072], mrow[:], reads=['mrow'], writes=['mods'])

        def setup_mod(i, which, want=(0, 1, 2), ln=True):
            modb = kb.sb('modb', [128, 6, D])
            lnb = kb.sb('lnb', [128, 2, D]) if ln else None
            with kb.phase():
                mrow3 = kb.sb('mrow3', [2, 3072])
                kb.dma('sp', mrow3[:], mods_d[i, :, which * 3072:(which + 1) * 3072], reads=['mods'], writes=['mrow3'])
                n = 0
                for v in want:
                    for cond in range(2):
                        for half in range(2):
                            b = n % 4
                            n += 1
                            T(lambda e: e.matmul(psb[b][:], lhsT=sel2[:, cond * 128:(cond + 1) * 128],
                                                 rhs=mrow3[:, v * 1024 + half * 512:v * 1024 + (half + 1) * 512],
                                                 start=True, stop=True), ['sel2', 'mrow3'], [PS[b]])
                            dst = modb[:, v * 2 + cond, half * 512:(half + 1) * 512]
                            if v == 1:
                                V(lambda e: e.tensor_scalar(out=dst, in0=psb[b][:], scalar1=1.0, scalar2=None,
                                                            op0=ALU.add), [PS[b]], ['modb'])
                            else:
                                A(lambda e: e.copy(out=dst, in_=psb[b][:]), [PS[b]], ['modb'])
                if ln:
                    kb.dma('sp', lnb[:, 0, :], ln_g[i, which:which + 1, :].partition_broadcast(128), writes=['lnb'])
                    kb.dma('sp', lnb[:, 1, :], ln_b[i, which:which + 1, :].partition_broadcast(128), writes=['lnb'])
            return modb, lnb

        def make_h(tt, modb, htok, hkey):
            cond = 0 if tt < 4 else 1
            V(lambda e: e.tensor_tensor(out=htok, in0=x_res[:, tt, :], in1=modb[:, 2 + cond, :], op=ALU.mult),
              [f'x{tt}', 'modb'], [hkey])
            V(lambda e: e.tensor_tensor(out=htok, in0=htok, in1=modb[:, 0 + cond, :], op=ALU.add),
              [hkey, 'modb'], [hkey])

        def transpose8(src_bf, skey, dst3, dkey, bank):
            pv = psbf(bank)
            for c in range(8):
                T(lambda e: e.transpose(out=pv[:, c * 128:(c + 1) * 128], in_=src_bf[:, c * 128:(c + 1) * 128],
                                        identity=identb[:]), [skey, 'identb'], [PS[bank]])
            A(lambda e: e.copy(out=dst3, in_=pv.rearrange("p (c t) -> p c t", c=8)), [PS[bank]], [dkey])

        def post_sublayer(tt, banks, modb, lnb, wk):
            cond = 0 if tt < 4 else 1
            z, stats, mv, rs = wk
            xk = f'x{tt}'
            for half in range(2):
                sl = slice(half * 512, (half + 1) * 512)
                V(lambda e: e.tensor_tensor(out=z[:, sl], in0=psb[banks[half]][:], in1=modb[:, 4 + cond, sl], op=ALU.mult),
                  [PS[banks[half]], 'modb'], ['z'])
                V(lambda e: e.scalar_tensor_tensor(out=z[:, sl], in0=x_res[:, tt, sl], scalar=ALPHA, in1=z[:, sl],
                                                   op0=ALU.mult, op1=ALU.add), [xk, 'z'], ['z'])
                V(lambda e: e.bn_stats(out=stats[:, half, :], in_=z[:, sl]), ['z'], ['stats'])
            V(lambda e: e.bn_aggr(out=mv[:], in_=stats[:].rearrange("p a b -> p (a b)")), ['stats'], ['mv'])
            V(lambda e: e.tensor_scalar(out=rs[:, 0:1], in0=mv[:, 1:2], scalar1=LN_EPS, scalar2=None, op0=ALU.add),
              ['mv'], ['rs'])
            A(lambda e: e.activation(out=rs[:, 1:2], in_=rs[:, 0:1], func=AF.Sqrt), ['rs'], ['rs'])
            V(lambda e: e.reciprocal(out=rs[:, 2:3], in_=rs[:, 1:2]), ['rs'], ['rs'])
            V(lambda e: e.tensor_scalar(out=z[:], in0=z[:], scalar1=mv[:, 0:1], scalar2=rs[:, 2:3],
                                        op0=ALU.subtract, op1=ALU.mult), ['z', 'mv', 'rs'], ['z'])
            V(lambda e: e.tensor_tensor(out=z[:], in0=z[:], in1=lnb[:, 0, :], op=ALU.mult), ['z', 'lnb'], ['z'])
            V(lambda e: e.tensor_tensor(out=x_res[:, tt, :], in0=z[:], in1=lnb[:, 1, :], op=ALU.add),
              ['z', 'lnb'], [xk])

        def post_work():
            return (kb.sb('z', [128, D]), kb.sb('stats', [128, 2, 6]), kb.sb('mv', [128, 2]), kb.sb('rs', [128, 4]))

        ubv_d = dscr("ubv", [16384, 2048], BF16)

        def peer_convert(i):
            for c in range(16):
                rs_ = slice(c * 1024, (c + 1) * 1024)
                kb.bg_dma(ubv_d[rs_, 0:1024], pu_d[i][rs_, :])
                kb.bg_dma(ubv_d[rs_, 1024:2048], pv_d[i][rs_, :])

        def peer_sublayer(i):
            with kb.phase():
                modb, lnb = setup_mod(i, 1)
                kb.bg_wait('pool')
                wk = post_work()
                wq = kb.sb('wq', [128, 8, 2048], BF16)
                for c in range(8):
                    load_cast(wq[:, c, :], pwq_d[i, c * 128:(c + 1) * 128, :], w=['wq'])
                keyT = kb.sb('keyT', [128, 2, 128], BF16)
                load_cast(keyT[:], pkT_d[i].rearrange("z d k -> d z k"), w=['keyT'])
                htok1 = kb.sb('htok', [128, D])
                htok = [htok1, htok1]
                hbs = [kb.sb(f'hb{b}', [128, D], BF16) for b in range(2)]
                hTt = kb.sb('hTt', [128, 8, 128], BF16)
                qT = kb.sb('qT', [128, 16, 128], BF16)
                s_sb = kb.sb('s_sb', [128, 16, 128])
                sv = kb.sb('sv', [128, 16, 16])
                si = kb.sb('si', [128, 16, 16], U32)
                si_f = kb.sb('si_f', [128, 16, 16])
                cand = kb.sb('cand', [128, 8, 256])
                cv = kb.sb('cv', [128, 8, 16])
                ci = kb.sb('ci', [128, 128], U32)
                ab_i = kb.sb('ab_i', [128, 2, 128], U32)
                ab_f = kb.sb('ab_f', [128, 2, 128])
                oh = cand
                candk = [f'cand{h}' for h in range(8)]
                i12 = kb.sb('i12', [128, 2, 128])
                idx_i = [kb.sb(f'idx_i{b}', [128, 128], I32) for b in range(2)]
                gs = kb.sb('gs', [128, 8])
                gate = [kb.sb(f'gate{b}', [128, 128]) for b in range(2)]
                pre = kb.sb('pre', [128, 128])
                ga = kb.sb('ga', [128, 128])
                GK, NBUF = 2, 6
                uv = [kb.sb(f'uv{b}', [128, GK, 2048], BF16) for b in range(NBUF)]
                Dk = [kb.sb(f'Dk{b}', [128, 128], BF16) for b in range(2)]
                iota16 = cst[:, C_IOTA:C_IOTA + 16]

                def topk_stage(tt, sl):
                    hk, ik, gk = 'htok', f'idx_i{sl}', f'gate{sl}'
                    hb = hbs[sl]
                    make_h(tt, modb, htok[sl][:], hk)
                    A(lambda e: e.copy(out=hb[:], in_=htok[sl][:]), [hk], [f'hb{sl}'])
                    yield
                    transpose8(hb, f'hb{sl}', hTt[:], 'hTt', 6)
                    yield
                    for rnd in range(2):
                        for hz in range(rnd * 8, rnd * 8 + 8):
                            bk = 2 + (hz % 8) // 4
                            for c in range(8):
                                T(lambda e: e.matmul(psb[bk][:, (hz % 4) * 128:(hz % 4 + 1) * 128],
                                                     lhsT=wq[:, c, hz * 128:(hz + 1) * 128], rhs=hTt[:, c, :],
                                                     start=(c == 0), stop=(c == 7)), ['wq', 'hTt'], [PS[bk]])
                            if hz % 2 == 1:
                                yield
                        for b2 in range(2):
                            A(lambda e: e.copy(out=qT[:, rnd * 8 + b2 * 4:rnd * 8 + b2 * 4 + 4, :],
                                               in_=psb[2 + b2][:].rearrange("p (a t) -> p a t", a=4)), [PS[2 + b2]], ['qT'])
                        yield
                    for rnd in range(2):
                        for hz in range(rnd * 8, rnd * 8 + 8):
                            bk = 4 + (hz % 8) // 4
                            T(lambda e: e.matmul(psb[bk][:, (hz % 4) * 128:(hz % 4 + 1) * 128],
                                                 lhsT=qT[:, hz, :], rhs=keyT[:, hz % 2, :], start=True, stop=True),
                              ['qT', 'keyT'], [PS[bk]])
                        for b2 in range(2):
                            h0 = rnd * 8 + b2 * 4
                            A(lambda e: e.copy(out=s_sb[:, h0:h0 + 4, :],
                                               in_=psb[4 + b2][:].rearrange("p (a t) -> p a t", a=4)),
                              [PS[4 + b2]], [f's_sb{h0 + a}' for a in range(4)])
                        yield
                    for hz in range(16):
                        ks, kv_, ki = f's_sb{hz}', f'sv{hz}', f'si{hz}'
                        V(lambda e: e.max(out=sv[:, hz, 0:8], in_=s_sb[:, hz, :]), [ks], [kv_])
                        yield
                        V(lambda e: e.max_index(out=si[:, hz, 0:8], in_max=sv[:, hz, 0:8], in_values=s_sb[:, hz, :]),
                          [ks, kv_], [ki])
                        yield
                        V(lambda e: e.match_replace(out=s_sb[:, hz, :], in_to_replace=sv[:, hz, 0:8],
                                                    in_values=s_sb[:, hz, :], imm_value=-1e30), [ks, kv_], [ks])
                        yield
                        V(lambda e: e.max(out=sv[:, hz, 8:16], in_=s_sb[:, hz, :]), [ks], [kv_])
                        yield
                        V(lambda e: e.max_index(out=si[:, hz, 8:16], in_max=sv[:, hz, 8:16], in_values=s_sb[:, hz, :]),
                          [ks, kv_], [ki])
                        yield
                    svk = [f'sv{hz}' for hz in range(16)]
                    sik = [f'si{hz}' for hz in range(16)]
                    V(lambda e: e.tensor_copy(out=si_f[:], in_=si[:]), sik, ['si_f'])
                    sv4 = sv[:].rearrange("p (h z) k -> p h z k", z=2)
                    sif4 = si_f[:].rearrange("p (h z) k -> p h z k", z=2)
                    V(lambda e: e.tensor_tensor(out=cand[:].rearrange("p h (a b) -> p h a b", a=16),
                                                in0=sv4[:, :, 0, :].unsqueeze(3).to_broadcast([128, 8, 16, 16]),
                                                in1=sv4[:, :, 1, :].unsqueeze(2).to_broadcast([128, 8, 16, 16]),
                                                op=ALU.add), svk, [f'cand{h}' for h in range(8)])
                    yield
                    for h in range(8):
                        kc, kcv, kci = f'cand{h}', f'cv{h}', f'ci{h}'
                        V(lambda e: e.max(out=cv[:, h, 0:8], in_=cand[:, h, :]), [kc], [kcv])
                        yield
                        V(lambda e: e.max_index(out=ci[:, h * 16:h * 16 + 8], in_max=cv[:, h, 0:8], in_values=cand[:, h, :]),
                          [kc, kcv], [kci])
                        yield
                        V(lambda e: e.match_replace(out=cand[:, h, :], in_to_replace=cv[:, h, 0:8],
                                                    in_values=cand[:, h, :], imm_value=-1e30), [kc, kcv], [kc])
                        yield
                        V(lambda e: e.max(out=cv[:, h, 8:16], in_=cand[:, h, :]), [kc], [kcv])
                        yield
                        V(lambda e: e.max_index(out=ci[:, h * 16 + 8:h * 16 + 16], in_max=cv[:, h, 8:16],
                                                in_values=cand[:, h, :]), [kc, kcv], [kci])
                        yield
                    cvk = [f'cv{h}' for h in range(8)]
                    cik = [f'ci{h}' for h in range(8)]
                    V(lambda e: e.tensor_scalar(out=ab_i[:, 0, :], in0=ci[:], scalar1=4, scalar2=None,
                                                op0=ALU.logical_shift_right), cik, ['ab_i'])
                    V(lambda e: e.tensor_scalar(out=ab_i[:, 1, :], in0=ci[:], scalar1=15, scalar2=None,
                                                op0=ALU.bitwise_and), cik, ['ab_i'])
                    V(lambda e: e.tensor_copy(out=ab_f[:], in_=ab_i[:]), ['ab_i'], ['ab_f'])
                    yield
                    for zz in range(2):
                        V(lambda e: e.tensor_tensor(out=oh[:].rearrange("p h (k a) -> p (h k) a", k=16), in0=ab_f[:, zz, :].unsqueeze(2).to_broadcast([128, 128, 16]),
                                                    in1=iota16.unsqueeze(1).to_broadcast([128, 128, 16]),
                                                    op=ALU.is_equal), ['ab_f', 'cst'], candk)
                        yield
                        V(lambda e: e.tensor_tensor(out=oh[:].rearrange("p h (k a) -> p h k a", k=16),
                                                    in0=oh[:].rearrange("p h (k a) -> p h k a", k=16),
                                                    in1=sif4[:, :, zz, :].unsqueeze(2).to_broadcast([128, 8, 16, 16]),
                                                    op=ALU.mult), candk + ['si_f'], candk)
                        yield
                        V(lambda e: e.tensor_reduce(out=i12[:, zz, :], in_=oh[:].rearrange("p h (k a) -> p (h k) a", k=16), axis=AX.X, op=ALU.add), candk, ['i12'])
                        yield
                    V(lambda e: e.scalar_tensor_tensor(out=i12[:, 0, :], in0=i12[:, 0, :], scalar=128.0, in1=i12[:, 1, :],
                                                       op0=ALU.mult, op1=ALU.add), ['i12'], ['i12'])
                    V(lambda e: e.tensor_scalar(out=i12[:, 0, :], in0=i12[:, 0, :], scalar1=0.0, scalar2=16383.0, op0=ALU.max, op1=ALU.min),
                      ['i12'], ['i12'])
                    V(lambda e: e.tensor_copy(out=idx_i[sl][:], in_=i12[:, 0, :]), ['i12'], [ik])
                    g3 = gate[sl][:].rearrange("p (h k) -> p h k", h=8)
                    V(lambda e: e.tensor_tensor(out=g3, in0=cv[:], in1=cv[:, :, 0:1].to_broadcast([128, 8, 16]),
                                                op=ALU.subtract), cvk, [gk])
                    A(lambda e: e.activation(out=g3, in_=g3, func=AF.Exp), [gk], [gk])
                    V(lambda e: e.tensor_reduce(out=gs[:], in_=g3, axis=AX.X, op=ALU.add), [gk], ['gs'])
                    V(lambda e: e.reciprocal(out=gs[:], in_=gs[:]), ['gs'], ['gs'])
                    V(lambda e: e.tensor_tensor(out=g3, in0=g3, in1=gs[:].unsqueeze(2).to_broadcast([128, 8, 16]), op=ALU.mult),
                      [gk, 'gs'], [gk])
                    yield

                def drain(gen, n=None):
                    if gen is None:
                        return None
                    try:
                        if n is None:
                            while True:
                                next(gen)
                        for _ in range(n):
                            next(gen)
                    except StopIteration:
                        return None
                    return gen

                def expert_stage(tt, sl, nxt):
                    hk, ik, gk = f'htok{sl}', f'idx_i{sl}', f'gate{sl}'
                    V(lambda e: e.memset(pre[:], 0.0), [], ['pre'])
                    ngrp = 128 // GK

                    def S0(g):
                        gb = g % NBUF
                        for jj in range(GK):
                            k = g * GK + jj
                            kb.gather(uv[gb][:, jj, :], ubv_d, idx_i[sl][:, k:k + 1], reads=[ik, 'ubv'], writes=[f'uv{gb}_{jj}'])

                    def S1(g):
                        gb = g % NBUF
                        for jj in range(GK):
                            k = g * GK + jj
                            V(lambda e: e.scalar_tensor_tensor(out=uv[gb][:, jj, 0:1024], in0=uv[gb][:, jj, 0:1024], scalar=1.0, in1=hbs[sl][:],
                                                               op0=ALU.mult, op1=ALU.mult, accum_out=pre[:, k:k + 1]),
                              [f'uv{gb}_{jj}', f'hb{sl}', 'pre'], [f'pre{k}', f'uv{gb}_{jj}'])

                    def S2(g):
                        k0 = g * GK
                        A(lambda e: e.activation(out=ga[:, k0:k0 + GK], in_=pre[:, k0:k0 + GK], func=AF.Gelu), [f'pre{k0 + q_}' for q_ in range(GK)] + ['pre'], [f'ga{g}'])
                        V(lambda e: e.tensor_tensor(out=ga[:, k0:k0 + GK], in0=ga[:, k0:k0 + GK], in1=gate[sl][:, k0:k0 + GK],
                                                    op=ALU.mult), [f'ga{g}', gk], [f'ga{g}'])

                    def S3(g):
                        gb = g % NBUF
                        for jj in range(GK):
                            k = g * GK + jj
                            A(lambda e: e.activation(out=Dk[k % 2][:], in_=identb[:], func=AF.Copy, scale=ga[:, k:k + 1]),
                              ['identb', f'ga{g}'], [f'Dk{k % 2}'])
                            for half in range(2):
                                T(lambda e: e.matmul(psb[half][:], lhsT=Dk[k % 2][:],
                                                     rhs=uv[gb][:, jj, 1024 + half * 512:1024 + (half + 1) * 512],
                                                     start=(k == 0), stop=(k == 127)), [f'Dk{k % 2}', f'uv{gb}_{jj}'], [PS[half]])

                    for it in range(ngrp + 3):
                        if it < ngrp:
                            S0(it)
                        if 0 <= it - 1 < ngrp:
                            S1(it - 1)
                        nxt = drain(nxt, 2)
                        if 0 <= it - 2 < ngrp:
                            S2(it - 2)
                        nxt = drain(nxt, 1)
                        if 0 <= it - 3 < ngrp:
                            S3(it - 3)
                        nxt = drain(nxt, 1)
                    drain(nxt)
                    post_sublayer(tt, (0, 1), modb, lnb, wk)

                drain(topk_stage(0, 0))
                for tt in range(NT):
                    sl = tt % 2
                    nxt = topk_stage(tt + 1, 1 - sl) if tt + 1 < NT else None
                    expert_stage(tt, sl, nxt)

        def rwkv_sublayer(i):
            j = i // 2
            one1 = cst[0:1, C_ONES:C_ONES + 128]
            onec = cst[:, C_ONES:C_ONES + 1]
            with contextlib.ExitStack() as sub_st:
                old_st = kb.st
                kb.st = sub_st
                bon = kb.sb('bon', [128, NT, 16])
                with kb.phase():
                    modb, _ = setup_mod(i, 0, want=(0, 1), ln=False)
                    hT = kb.sb('hT', [128, 8, HPAD], BF16)
                    G(lambda e: e.memset(hT[:], 0.0), [], ['hT'])
                    htok = kb.sb('htok', [128, D])
                    hb = kb.sb('hb', [128, D], BF16)
                    for tt in range(NT):
                        pc = padcol(tt)
                        make_h(tt, modb, htok[:], 'htok')
                        A(lambda e: e.copy(out=hb[:], in_=htok[:]), ['htok'], ['hb'])
                        transpose8(hb, 'hb', hT[:, :, pc:pc + 128], 'hT', 7)
                    muT = kb.sb('muT', [128, 6, 8])
                    kb.dma('sp', muT[:], muT_d[j], writes=['muT'])
                    w0row = kb.sb('w0row', [1, 2, D])
                    a0row = kb.sb('a0row', [1, 2, D])
                    kb.dma('sp', w0row[:], w0_d[j:j + 1], writes=['w0row'])
                    kb.dma('sp', a0row[:], a0_d[j:j + 1], writes=['a0row'])
                    xxs = [kb.sb(f'xx{b}', [128, 8, 128]) for b in range(2)]
                    xms = [kb.sb(f'xm{b}', [128, 8, 128], BF16) for b in range(2)]
                    W = kb.sb('W', [128, 8, D], BF16)
                    l1 = kb.sb('l1', [128, 8, 128], BF16)
                    l2 = kb.sb('l2', [128, D], BF16)
                    hid = kb.sb('hid', [128, 128], BF16)
                    ot = [kb.sb(f'ot{b}', [128, D]) for b in range(2)]
                    nout = [0]

                    def xm_tile(tt, m, bi):
                        pc = padcol(tt)
                        xx, xm, kx, km = xxs[bi], xms[bi], f'xx{bi}', f'xm{bi}'
                        V(lambda e: e.tensor_tensor(out=xx[:], in0=hT[:, :, pc - 1:pc + 127], in1=hT[:, :, pc + 1:pc + 129],
                                                    op=ALU.add), ['hT'], [kx])
                        V(lambda e: e.scalar_tensor_tensor(out=xx[:], in0=xx[:], scalar=0.5, in1=hT[:, :, pc:pc + 128],
                                                           op0=ALU.mult, op1=ALU.subtract), [kx, 'hT'], [kx])
                        V(lambda e: e.tensor_tensor(out=xx[:], in0=xx[:], in1=muT[:, m, :].unsqueeze(2).to_broadcast([128, 8, 128]),
                                                    op=ALU.mult), [kx, 'muT'], [kx])
                        V(lambda e: e.tensor_tensor(out=xm[:], in0=xx[:], in1=hT[:, :, pc:pc + 128], op=ALU.add),
                          [kx, 'hT'], [km])

                    def xm_iter(m):
                        xm_tile(0, m, 0)
                        for tt in range(NT):
                            if tt + 1 < NT:
                                xm_tile(tt + 1, m, (tt + 1) % 2)
                            yield tt, xms[tt % 2], f'xm{tt % 2}'

                    def store(dst_rows, func=None, post_scale=None):
                        b = nout[0] % 2
                        nout[0] += 1
                        for half in range(2):
                            sl = slice(half * 512, (half + 1) * 512)
                            if func is None:
                                A(lambda e: e.copy(out=ot[b][:, sl], in_=psb[half][:]), [PS[half]], [f'ot{b}'])
                            else:
                                A(lambda e: e.activation(out=ot[b][:, sl], in_=psb[half][:], func=func), [PS[half]], [f'ot{b}'])
                        if post_scale is not None:
                            V(lambda e: e.tensor_scalar(out=ot[b][:], in0=ot[b][:], scalar1=post_scale, scalar2=None,
                                                        op0=ALU.mult), [f'ot{b}'], [f'ot{b}'])
                        kb.dma('sp', dst_rows, ot[b][:], reads=[f'ot{b}'], writes=['scr'])

                    for (m, widx, dst_s) in ((0, 0, r_s), (2, 1, k_s), (3, 2, v_s)):
                        for c in range(8):
                            load_cast(W[:, c, :], wrkv_d[j, widx, c * 128:(c + 1) * 128, :], w=['W'])
                        for tt, xm, km in xm_iter(m):
                            for half in range(2):
                                for c in range(8):
                                    T(lambda e: e.matmul(psb[half][:], lhsT=xm[:, c, :], rhs=W[:, c, half * 512:(half + 1) * 512],
                                                         start=(c == 0), stop=(c == 7)), [km, 'W'], [PS[half]])
                            store(dst_s[tt * 128:(tt + 1) * 128, :])
                    for (m, l1_d, l2_d, brow, hfunc, dst2, ofunc, oscale) in (
                            (1, w1c_d, w2c_d, w0row, AF.Tanh, lw_s, AF.Sigmoid, NEG_EXP_HALF),
                            (4, a1c_d, a2c_d, a0row, None, a_s, AF.Sigmoid, None)):
                        load_cast(l1[:], l1_d[j].rearrange("(c p) l -> p c l", p=128), w=['l1'])
                        load_cast(l2[:], l2_d[j], w=['l2'])
                        for tt, xm, km in xm_iter(m):
                            for c in range(8):
                                T(lambda e: e.matmul(psb[2][:, 0:128], lhsT=l1[:, c, :], rhs=xm[:, c, :],
                                                     start=(c == 0), stop=(c == 7)), ['l1', km], [PS[2]])
                            if hfunc is None:
                                A(lambda e: e.copy(out=hid[:], in_=psb[2][:, 0:128]), [PS[2]], ['hid'])
                            else:
                                A(lambda e: e.activation(out=hid[:], in_=psb[2][:, 0:128], func=hfunc), [PS[2]], ['hid'])
                            for z in range(2):
                                for half in range(2):
                                    sl = slice(half * 512, (half + 1) * 512)
                                    T(lambda e: e.matmul(psb[half][:], lhsT=hid[z * 64:(z + 1) * 64, :],
                                                         rhs=l2[z * 64:(z + 1) * 64, sl], start=True, stop=False),
                                      ['hid', 'l2'], [PS[half]])
                                    T(lambda e: e.matmul(psb[half][:], lhsT=one1, rhs=brow[0:1, z, sl], start=False, stop=True),
                                      ['cst', 'w0row', 'a0row'], [PS[half]])
                                store(dst2[z, tt * 128:(tt + 1) * 128, :], func=ofunc, post_scale=oscale)
                    load_cast(l1[:], g1_d[j].rearrange("(c p) l -> p c l", p=128), w=['l1'])
                    load_cast(l2[:], g2_d[j], w=['l2'])
                    for tt, xm, km in xm_iter(5):
                        for c in range(8):
                            T(lambda e: e.matmul(psb[2][:, 0:128], lhsT=l1[:, c, :], rhs=xm[:, c, :],
                                                 start=(c == 0), stop=(c == 7)), ['l1', km], [PS[2]])
                        A(lambda e: e.activation(out=hid[:], in_=psb[2][:, 0:128], func=AF.Sigmoid), [PS[2]], ['hid'])
                        for half in range(2):
                            T(lambda e: e.matmul(psb[half][:], lhsT=hid[:], rhs=l2[:, half * 512:(half + 1) * 512],
                                                 start=True, stop=True), ['hid', 'l2'], [PS[half]])
                        store(g_s[tt * 128:(tt + 1) * 128, :])

                with kb.phase():
                    kkb = kb.sb('kkb', [128, D])
                    kab = kb.sb('kab', [128, D])
                    rkb = kb.sb('rkb', [128, D])
                    kb.dma('sp', kkb[:], rkk_d[j:j + 1, :].partition_broadcast(128), writes=['kkb'])
                    kb.dma('sp', kab[:], rka_d[j:j + 1, :].partition_broadcast(128), writes=['kab'])
                    kb.dma('sp', rkb[:], rrk_d[j:j + 1, :].partition_broadcast(128), writes=['rkb'])
                    r_t = kb.sb('r_t', [128, D])
                    k_t = kb.sb('k_t', [128, D])
                    v_t = kb.sb('v_t', [128, D])
                    lw_t = kb.sb('lw_t', [128, D])
                    a_t = kb.sb('a_t', [128, D])
                    f1 = kb.sb('f1', [128, D])
                    f2 = kb.sb('f2', [128, D])
                    f3 = kb.sb('f3', [128, D])
                    fP = kb.sb('fP', [128, D])
                    fPi = kb.sb('fPi', [128, D])
                    n16 = kb.sb('n16', [128, 4, 16])
                    at_tok = kb.sb('at_tok', [128, D], BF16)
                    rt_tok = kb.sb('rt_tok', [128, D], BF16)
                    bt_tok = kb.sb('bt_tok', [128, D], BF16)
                    kt_tok = kb.sb('kt_tok', [128, D], BF16)
                    vb = kb.sb('vb', [128, D], BF16)
                    arT = kb.sb('arT', [64, 16, 2, 128], BF16)
                    btT = kb.sb('btT', [64, 16, 128], BF16)
                    ktT = kb.sb('ktT', [64, 16, 128], BF16)
                    Am = kb.sb('Am', [128, 16, 512], BF16)
                    A4 = [kb.sb(f'A4{s}', [128, 4, 128], BF16) for s in range(8)]
                    AT4 = [kb.sb(f'AT4{s}', [128, 4, 128], BF16) for s in range(8)]
                    NB = [kb.sb(f'NB{s}', [128, 4, 128], BF16) for s in range(8)]
                    Tb = [kb.sb(f'Tb{s}', [128, 2, 128], BF16) for s in range(8)]
                    N_all = kb.sb('N_all', [128, 16, 128], BF16)
                    Xb = kb.sb('Xb', [128, D], BF16)
                    Ub = kb.sb('Ub', [128, D], BF16)
                    S_T = kb.sb('S_T', [64, 16, 64])
                    Sb = kb.sb('Sb', [64, 16, 64], BF16)
                    PC = kb.sb('PC', [64, 16])
                    stl = kb.sb('stl', [64, 16, 64])
                    yt = kb.sb('yt', [128, D])
                    cstR = kb.sb('cstR', [128, NCST - CR0])
                    kb.dma('sp', cstR[:], cst_d[:, CR0:NCST], writes=['cst'])
                    for (sq_i, tiles) in SEQS:
                        sample = (sq_i == 2)
                        for dr in range(2):
                            mask4 = cstR[:, C_M4F - CR0:C_M4F - CR0 + 512] if dr == 0 else cstR[:, C_M4B - CR0:C_M4B - CR0 + 512]
                            if sample:
                                kb.dma('sp', stl[:], st_in[j, dr].rearrange("h i j -> i h j"), writes=['stl'])
                                for h in range(16):
                                    T(lambda e: e.transpose(out=psb[h // 8][0:64, (h % 8) * 64:(h % 8 + 1) * 64], in_=stl[:, h, :],
                                                            identity=cst[0:64, C_ID:C_ID + 64]), ['stl', 'cst'], [PS[h // 8]])
                                for hq in range(2):
                                    V(lambda e: e.tensor_copy(out=S_T[:, hq * 8:(hq + 1) * 8, :],
                                                              in_=psb[hq][0:64, :].rearrange("p (h i) -> p h i", h=8)),
                                      [PS[hq]], ['S_T'])
                            else:
                                V(lambda e: e.memset(S_T[:], 0.0), [], ['S_T'])
                            A(lambda e: e.copy(out=Sb[:], in_=S_T[:]), ['S_T'], ['Sb'])
                            order = tiles if dr == 0 else tiles[::-1]
                            for tt in order:
                                rows = slice(tt * 128, (tt + 1) * 128)
                                kb.dma('sp', r_t[:], r_s[rows, :], reads=['scr'], writes=['r_t'])
                                kb.dma('sp', k_t[:], k_s[rows, :], reads=['scr'], writes=['k_t'])
                                kb.dma('sp', v_t[:], v_s[rows, :], reads=['scr'], writes=['v_t'])
                                kb.dma('sp', lw_t[:], lw_s[dr, rows, :], reads=['scr'], writes=['lw_t'])
                                kb.dma('sp', a_t[:], a_s[dr, rows, :], reads=['scr'], writes=['a_t'])
                                tri = cstR[:, C_TRIF - CR0:C_TRIF - CR0 + 128] if dr == 0 else cstR[:, C_TRIB - CR0:C_TRIB - CR0 + 128]
                                for half in range(2):
                                    T(lambda e: e.matmul(psb[half][:], lhsT=tri, rhs=lw_t[:, half * 512:(half + 1) * 512],
                                                         start=True, stop=True), ['cst', 'lw_t'], [PS[half]])
                                for h in range(16):
                                    T(lambda e: e.matmul(psb[2][0:64, h:h + 1], lhsT=lw_t[:, h * 64:(h + 1) * 64], rhs=onec,
                                                         start=True, stop=True), ['lw_t', 'cst'], [PS[2]])
                                for half in range(2):
                                    sl = slice(half * 512, (half + 1) * 512)
                                    A(lambda e: e.activation(out=fP[:, sl], in_=psb[half][:], func=AF.Exp), [PS[half]], ['fP'])
                                for half in range(2):
                                    sl = slice(half * 512, (half + 1) * 512)
                                    A(lambda e: e.activation(out=fPi[:, sl], in_=psb[half][:], func=AF.Exp, scale=-1.0),
                                      [PS[half]], ['fPi'])
                                for half in range(2):
                                    sl = slice(half * 512, (half + 1) * 512)
                                    V(lambda e: e.tensor_tensor(out=f3[:, sl], in0=psb[half][:], in1=lw_t[:, sl], op=ALU.subtract),
                                      [PS[half], 'lw_t', 'fPi'], ['f3'])
                                A(lambda e: e.activation(out=f3[:], in_=f3[:], func=AF.Exp), ['f3'], ['f3'])
                                A(lambda e: e.copy(out=vb[:], in_=v_t[:]), ['v_t'], ['vb'])
                                A(lambda e: e.activation(out=PC[:], in_=psb[2][0:64, 0:16], func=AF.Exp), [PS[2]], ['PC'])
                                V(lambda e: e.tensor_tensor(out=rt_tok[:], in0=r_t[:], in1=fP[:], op=ALU.mult), ['r_t', 'fP'], ['rt_tok'])
                                V(lambda e: e.scalar_tensor_tensor(out=f2[:], in0=a_t[:], scalar=-1.0, in1=kab[:],
                                                                   op0=ALU.add, op1=ALU.mult), ['a_t', 'kab'], ['f2'])
                                V(lambda e: e.scalar_tensor_tensor(out=f2[:], in0=f2[:], scalar=1.0, in1=k_t[:],
                                                                   op0=ALU.add, op1=ALU.mult), ['f2', 'k_t'], ['f2'])
                                V(lambda e: e.tensor_tensor(out=kt_tok[:], in0=f2[:], in1=fPi[:], op=ALU.mult), ['f2', 'fPi'], ['kt_tok'])
                                V(lambda e: e.tensor_tensor(out=fP[:], in0=r_t[:], in1=f2[:], op=ALU.mult), ['r_t', 'f2'], ['fP'])
                                V(lambda e: e.tensor_tensor(out=fP[:], in0=fP[:], in1=rkb[:], op=ALU.mult), ['fP', 'rkb'], ['fP'])
                                if dr == 0:
                                    V(lambda e: e.tensor_reduce(out=bon[:, tt, :], in_=fP[:].rearrange("p (h d) -> p h d", h=16),
                                                                axis=AX.X, op=ALU.add), ['fP'], ['bon'])
                                else:
                                    V(lambda e: e.tensor_reduce(out=n16[:, 3, :], in_=fP[:].rearrange("p (h d) -> p h d", h=16),
                                                                axis=AX.X, op=ALU.add), ['fP'], ['n16b'])
                                    V(lambda e: e.tensor_tensor(out=bon[:, tt, :], in0=bon[:, tt, :], in1=n16[:, 3, :], op=ALU.add),
                                      ['bon', 'n16b'], ['bon'])
                                V(lambda e: e.tensor_tensor(out=f1[:], in0=k_t[:], in1=kkb[:], op=ALU.mult), ['k_t', 'kkb'], ['f1'])
                                A(lambda e: e.activation(out=f2[:], in_=f1[:], func=AF.Square), ['f1'], ['f2'])
                                V(lambda e: e.tensor_reduce(out=n16[:, 0, :], in_=f2[:].rearrange("p (h d) -> p h d", h=16),
                                                            axis=AX.X, op=ALU.add), ['f2'], ['n16'])
                                A(lambda e: e.activation(out=n16[:, 1, :], in_=n16[:, 0, :], func=AF.Sqrt), ['n16'], ['n16'])
                                V(lambda e: e.tensor_scalar(out=n16[:, 1, :], in0=n16[:, 1, :], scalar1=1e-12, scalar2=None,
                                                            op0=ALU.max), ['n16'], ['n16'])
                                V(lambda e: e.reciprocal(out=n16[:, 2, :], in_=n16[:, 1, :]), ['n16'], ['n16'])
                                V(lambda e: e.tensor_tensor(out=f1[:].rearrange("p (h d) -> p h d", h=16),
                                                            in0=f1[:].rearrange("p (h d) -> p h d", h=16),
                                                            in1=n16[:, 2, :].unsqueeze(2).to_broadcast([128, 16, 64]),
                                                            op=ALU.mult), ['f1', 'n16'], ['f1'])
                                V(lambda e: e.tensor_tensor(out=f2[:], in0=f1[:], in1=a_t[:], op=ALU.mult), ['f1', 'a_t'], ['f2'])
                                V(lambda e: e.tensor_tensor(out=bt_tok[:], in0=f2[:], in1=fPi[:], op=ALU.mult), ['f2', 'fPi'], ['bt_tok'])
                                V(lambda e: e.scalar_tensor_tensor(out=at_tok[:], in0=f1[:], scalar=-1.0, in1=f3[:],
                                                                   op0=ALU.mult, op1=ALU.mult), ['f1', 'f3'], ['at_tok'])
                                nb = 0
                                for (src, skey, dstf, dkey) in ((rt_tok, 'rt_tok', lambda hq: arT[:, hq * 8:(hq + 1) * 8, 1, :], 'arT'),
                                                                (kt_tok, 'kt_tok', lambda hq: ktT[:, hq * 8:(hq + 1) * 8, :], 'ktT'),
                                                                (bt_tok, 'bt_tok', lambda hq: btT[:, hq * 8:(hq + 1) * 8, :], 'btT'),
                                                                (at_tok, 'at_tok', lambda hq: arT[:, hq * 8:(hq + 1) * 8, 0, :], 'arT')):
                                    for hq in range(2):
                                        bank = 3 + (nb % 2)
                                        nb += 1
                                        pv = psbf(bank)
                                        for h8 in range(8):
                                            h = hq * 8 + h8
                                            T(lambda e: e.transpose(out=pv[0:64, h8 * 128:(h8 + 1) * 128], in_=src[:, h * 64:(h + 1) * 64],
                                                                    identity=identb[:]), [skey, 'identb'], [PS[bank]])
                                        A(lambda e: e.copy(out=dstf(hq), in_=pv[0:64, :].rearrange("p (a t) -> p a t", a=8)),
                                          [PS[bank]], [dkey])
                                cMU = cstR[:, C_MU - CR0:C_MU - CR0 + 512]
                                cML = cstR[:, C_ML - CR0:C_ML - CR0 + 512]
                                mA = (cMU if dr == 0 else cML).rearrange("p (a t) -> p a t", a=4)
                                mAT = (cML if dr == 0 else cMU).rearrange("p (a t) -> p a t", a=4)
                                def head_chain(h, s_):
                                    bA = s_
                                    ka4, kat4, knb, ktb = f'A4{s_}', f'AT4{s_}', f'NB{s_}', f'Tb{s_}'
                                    buf = NB[s_]
                                    T(lambda e: e.matmul(psb[bA][:, 0:256], lhsT=btT[:, h, :], rhs=arT[:, h, :, :], start=True, stop=True),
                                      ['btT', 'arT'], [PS[bA]])
                                    T(lambda e: e.matmul(psb[bA][:, 256:512], lhsT=ktT[:, h, :], rhs=arT[:, h, :, :], start=True, stop=True),
                                      ['ktT', 'arT'], [PS[bA]])
                                    V(lambda e: e.tensor_tensor(out=Am[:, h, :], in0=psb[bA][:], in1=mask4, op=ALU.mult),
                                      [PS[bA], 'cst'], [f'Am{h}'])
                                    V(lambda e: e.tensor_tensor(out=A4[s_][:], in0=psb[bA][:, 0:128].unsqueeze(1).to_broadcast([128, 4, 128]),
                                                                in1=mA, op=ALU.mult), [PS[bA], 'cst'], [ka4])
                                    yield
                                    T(lambda e: e.matmul(psb[s_][:, 0:128], lhsT=arT[:, h, 0, :], rhs=btT[:, h, :],
                                                         start=True, stop=True), ['arT', 'btT'], [PS[s_]])
                                    yield
                                    V(lambda e: e.tensor_tensor(out=AT4[s_][:], in0=psb[s_][:, 0:128].unsqueeze(1).to_broadcast([128, 4, 128]),
                                                                in1=mAT, op=ALU.mult), [PS[s_], 'cst'], [kat4])
                                    G(lambda e: e.tensor_copy(out=buf[:, 1:3, :], in_=identb[:].unsqueeze(1).to_broadcast([128, 2, 128])),
                                      ['identb'], [knb])
                                    G(lambda e: e.tensor_copy(out=buf[:, 0, :], in_=A4[s_][:, 0, :]), [ka4], [knb])
                                    G(lambda e: e.tensor_copy(out=buf[:, 3, :], in_=AT4[s_][:, 0, :]), [kat4], [knb])
                                    yield
                                    for it in range(4):
                                        if it < 3:
                                            T(lambda e: e.matmul(psb[s_][:, 0:256], lhsT=buf[:, 3, :], rhs=buf[:, 0:2, :], start=True, stop=True),
                                              [knb], [PS[s_]])
                                            T(lambda e: e.matmul(psb[s_][:, 256:512], lhsT=buf[:, 0, :], rhs=buf[:, 2:4, :], start=True, stop=True),
                                              [knb], [PS[s_]])
                                        else:
                                            T(lambda e: e.matmul(psb[s_][:, 128:256], lhsT=buf[:, 3, :], rhs=buf[:, 1, :], start=True, stop=True),
                                              [knb], [PS[s_]])
                                            T(lambda e: e.matmul(psb[s_][:, 256:384], lhsT=buf[:, 0, :], rhs=buf[:, 2, :], start=True, stop=True),
                                              [knb], [PS[s_]])
                                        yield
                                        V(lambda e: e.tensor_tensor(out=buf[:, 1:3, :], in0=buf[:, 1:3, :],
                                                                    in1=psb[s_][:, 128:384].rearrange("p (a t) -> p a t", a=2), op=ALU.add),
                                          [knb, PS[s_]], [knb])
                                        if it < 3:
                                            A(lambda e: e.copy(out=buf[:, 0, :], in_=psb[s_][:, 0:128]), [PS[s_]], [knb])
                                            A(lambda e: e.copy(out=buf[:, 3, :], in_=psb[s_][:, 384:512]), [PS[s_]], [knb])
                                        yield
                                    for lv in range(1, 4):
                                        lastk = (lv == 3)
                                        T(lambda e: e.matmul(psb[s_][:, 0:128], lhsT=AT4[s_][:, lv, :], rhs=buf[:, 1, :], start=True, stop=True),
                                          [kat4, knb], [PS[s_]])
                                        if not lastk:
                                            T(lambda e: e.matmul(psb[s_][:, 128:256], lhsT=A4[s_][:, lv, :], rhs=buf[:, 2, :], start=True, stop=True),
                                              [ka4, knb], [PS[s_]])
                                        yield
                                        if not lastk:
                                            A(lambda e: e.copy(out=Tb[s_][:], in_=psb[s_][:, 0:256].rearrange("p (a t) -> p a t", a=2)),
                                              [PS[s_]], [ktb])
                                        else:
                                            A(lambda e: e.copy(out=Tb[s_][:, 0, :], in_=psb[s_][:, 0:128]), [PS[s_]], [ktb])
                                        yield
                                        T(lambda e: e.matmul(psb[s_][:, 256:384], lhsT=buf[:, 2, :], rhs=Tb[s_][:, 0, :], start=True, stop=True),
                                          [knb, ktb], [PS[s_]])
                                        if not lastk:
                                            T(lambda e: e.matmul(psb[s_][:, 384:512], lhsT=buf[:, 1, :], rhs=Tb[s_][:, 1, :], start=True, stop=True),
                                              [knb, ktb], [PS[s_]])
                                        yield
                                        if not lastk:
                                            V(lambda e: e.tensor_tensor(out=buf[:, 1:3, :], in0=buf[:, 1:3, :],
                                                                        in1=psb[s_][:, 256:512].rearrange("p (a t) -> p a t", a=2), op=ALU.add),
                                              [knb, PS[s_]], [knb])
                                        else:
                                            V(lambda e: e.tensor_tensor(out=N_all[:, h, :], in0=buf[:, 1, :], in1=psb[s_][:, 256:384], op=ALU.add),
                                              [knb, PS[s_]], [f'N{h}'])
                                for grp in range(2):
                                    gens = [head_chain(grp * 8 + q8, q8) for q8 in range(8)]
                                    while gens:
                                        alive = []
                                        for gch in gens:
                                            try:
                                                next(gch)
                                                alive.append(gch)
                                            except StopIteration:
                                                pass
                                        gens = alive
                                amk = [f'Am{h}' for h in range(16)]
                                for h in range(16):
                                    o = psb[h // 8][:, (h % 8) * 64:(h % 8 + 1) * 64]
                                    T(lambda e: e.matmul(o, lhsT=arT[:, h, 0, :], rhs=Sb[:, h, :], start=True, stop=False),
                                      ['arT', 'Sb'], [PS[h // 8]])
                                    T(lambda e: e.matmul(o, lhsT=Am[:, h, 256:384], rhs=vb[:, h * 64:(h + 1) * 64], start=False, stop=True),
                                      [f'Am{h}', 'vb'], [PS[h // 8]])
                                for hq in range(2):
                                    A(lambda e: e.copy(out=Xb[:, hq * 512:(hq + 1) * 512], in_=psb[hq][:]), [PS[hq]], ['Xb'])
                                for h in range(16):
                                    o = psb[2 + h // 8][:, (h % 8) * 64:(h % 8 + 1) * 64]
                                    T(lambda e: e.matmul(o, lhsT=N_all[:, h, :], rhs=Xb[:, h * 64:(h + 1) * 64], start=True, stop=True),
                                      [f'N{h}', 'Xb'], [PS[2 + h // 8]])
                                for hq in range(2):
                                    V(lambda e: e.tensor_copy(out=Ub[:, hq * 512:(hq + 1) * 512], in_=psb[2 + hq][:]), [PS[2 + hq]], ['Ub'])
                                for h in range(16):
                                    o = psb[4 + h // 8][:, (h % 8) * 64:(h % 8 + 1) * 64]
                                    hs = slice(h * 64, (h + 1) * 64)
                                    T(lambda e: e.matmul(o, lhsT=arT[:, h, 1, :], rhs=Sb[:, h, :], start=True, stop=False),
                                      ['arT', 'Sb'], [PS[4 + h // 8]])
                                    T(lambda e: e.matmul(o, lhsT=Am[:, h, 128:256], rhs=Ub[:, hs], start=False, stop=False),
                                      [f'Am{h}', 'Ub'], [PS[4 + h // 8]])
                                    T(lambda e: e.matmul(o, lhsT=Am[:, h, 384:512], rhs=vb[:, hs], start=False, stop=True),
                                      [f'Am{h}', 'vb'], [PS[4 + h // 8]])
                                for h in range(16):
                                    o = psb[6 + h // 8][0:64, (h % 8) * 64:(h % 8 + 1) * 64]
                                    hs = slice(h * 64, (h + 1) * 64)
                                    T(lambda e: e.matmul(o, lhsT=bt_tok[:, hs], rhs=Ub[:, hs], start=True, stop=False),
                                      ['bt_tok', 'Ub'], [PS[6 + h // 8]])
                                    T(lambda e: e.matmul(o, lhsT=kt_tok[:, hs], rhs=vb[:, hs], start=False, stop=True),
                                      ['kt_tok', 'vb'], [PS[6 + h // 8]])
                                for hq in range(2):
                                    V(lambda e: e.tensor_tensor(out=S_T[:, hq * 8:(hq + 1) * 8, :], in0=S_T[:, hq * 8:(hq + 1) * 8, :],
                                                                in1=psb[6 + hq][0:64, :].rearrange("p (h i) -> p h i", h=8), op=ALU.add),
                                      ['S_T', PS[6 + hq]], ['S_T'])
                                V(lambda e: e.tensor_tensor(out=S_T[:], in0=S_T[:], in1=PC[:].unsqueeze(2).to_broadcast([64, 16, 64]),
                                                            op=ALU.mult), ['S_T', 'PC'], ['S_T'])
                                A(lambda e: e.copy(out=Sb[:], in_=S_T[:]), ['S_T'], ['Sb'])
                                if dr == 0:
                                    for hq in range(2):
                                        A(lambda e: e.copy(out=yt[:, hq * 512:(hq + 1) * 512], in_=psb[4 + hq][:]), [PS[4 + hq]], ['yt'])
                                else:
                                    kb.dma('sp', yt[:], yf_s[rows, :], reads=['yfs'], writes=['yt'])
                                    for hq in range(2):
                                        V(lambda e: e.tensor_tensor(out=yt[:, hq * 512:(hq + 1) * 512], in0=yt[:, hq * 512:(hq + 1) * 512],
                                                                    in1=psb[4 + hq][:], op=ALU.add), ['yt', PS[4 + hq]], ['yt'])
                                kb.dma('sp', yf_s[rows, :], yt[:], reads=['yt'], writes=['yfs'])
                            if not sample:
                                for h in range(16):
                                    T(lambda e: e.transpose(out=psb[h // 8][0:64, (h % 8) * 64:(h % 8 + 1) * 64], in_=S_T[:, h, :],
                                                            identity=cst[0:64, C_ID:C_ID + 64]), ['S_T', 'cst'], [PS[h // 8]])
                                for hq in range(2):
                                    V(lambda e: e.tensor_copy(out=stl[:, hq * 8:(hq + 1) * 8, :],
                                                              in_=psb[hq][0:64, :].rearrange("p (h i) -> p h i", h=8)),
                                      [PS[hq]], ['stl'])
                                kb.dma('sp', nst_d[sq_i, j, dr].rearrange("h i j -> i h j"), stl[:], reads=['stl'], writes=['nst'])

                with kb.phase():
                    modb, lnb = setup_mod(i, 0, want=(2,), ln=True)
                    wk = post_work()
                    wo = kb.sb('wo', [128, 8, D], BF16)
                    for c in range(8):
                        load_cast(wo[:, c, :], rwo_d[j, c * 128:(c + 1) * 128, :], w=['wo'])
                    lxg = kb.sb('lxg', [128, D])
                    lxb = kb.sb('lxb', [128, D])
                    kb.dma('sp', lxg[:], lnxg_d[j:j + 1, :].partition_broadcast(128), writes=['lxg'])
                    kb.dma('sp', lxb[:], lnxb_d[j:j + 1, :].partition_broadcast(128), writes=['lxb'])
                    y_t = kb.sb('y_t', [128, D])
                    v_t = kb.sb('v_t', [128, D])
                    g_t = kb.sb('g_t', [128, D])
                    f1 = kb.sb('f1', [128, D])
                    m16 = kb.sb('m16', [128, 6, 16])
                    zb = kb.sb('zb', [128, D], BF16)
                    zT = kb.sb('zT', [128, 8, 128], BF16)
                    for tt in range(NT):
                        rows = slice(tt * 128, (tt + 1) * 128)
                        kb.dma('sp', y_t[:], yf_s[rows, :], writes=['y_t'])
                        kb.dma('sp', v_t[:], v_s[rows, :], writes=['v_t'])
                        kb.dma('sp', g_t[:], g_s[rows, :], writes=['g_t'])
                        y3 = y_t[:].rearrange("p (h d) -> p h d", h=16)
                        V(lambda e: e.tensor_reduce(out=m16[:, 0, :], in_=y3, axis=AX.X, op=ALU.add), ['y_t'], ['m16'])
                        V(lambda e: e.tensor_scalar(out=m16[:, 0, :], in0=m16[:, 0, :], scalar1=1.0 / 64, scalar2=None, op0=ALU.mult),
                          ['m16'], ['m16'])
                        V(lambda e: e.tensor_tensor(out=y3, in0=y3, in1=m16[:, 0, :].unsqueeze(2).to_broadcast([128, 16, 64]),
                                                    op=ALU.subtract), ['y_t', 'm16'], ['y_t'])
                        A(lambda e: e.activation(out=f1[:], in_=y_t[:], func=AF.Square), ['y_t'], ['f1'])
                        V(lambda e: e.tensor_reduce(out=m16[:, 1, :], in_=f1[:].rearrange("p (h d) -> p h d", h=16), axis=AX.X,
                                                    op=ALU.add), ['f1'], ['m16'])
                        V(lambda e: e.tensor_scalar(out=m16[:, 1, :], in0=m16[:, 1, :], scalar1=1.0 / 64, scalar2=GN_EPS,
                                                    op0=ALU.mult, op1=ALU.add), ['m16'], ['m16'])
                        A(lambda e: e.activation(out=m16[:, 2, :], in_=m16[:, 1, :], func=AF.Sqrt), ['m16'], ['m16'])
                        V(lambda e: e.reciprocal(out=m16[:, 3, :], in_=m16[:, 2, :]), ['m16'], ['m16'])
                        V(lambda e: e.tensor_tensor(out=y3, in0=y3, in1=m16[:, 3, :].unsqueeze(2).to_broadcast([128, 16, 64]),
                                                    op=ALU.mult), ['y_t', 'm16'], ['y_t'])
                        V(lambda e: e.tensor_tensor(out=y_t[:], in0=y_t[:], in1=lxg[:], op=ALU.mult), ['y_t', 'lxg'], ['y_t'])
                        V(lambda e: e.tensor_tensor(out=y_t[:], in0=y_t[:], in1=lxb[:], op=ALU.add), ['y_t', 'lxb'], ['y_t'])
                        V(lambda e: e.tensor_tensor(out=f1[:].rearrange("p (h d) -> p h d", h=16),
                                                    in0=v_t[:].rearrange("p (h d) -> p h d", h=16),
                                                    in1=bon[:, tt, :].unsqueeze(2).to_broadcast([128, 16, 64]), op=ALU.mult),
                          ['v_t', 'bon'], ['f1'])
                        V(lambda e: e.tensor_tensor(out=y_t[:], in0=y_t[:], in1=f1[:], op=ALU.add), ['y_t', 'f1'], ['y_t'])
                        V(lambda e: e.tensor_tensor(out=zb[:], in0=y_t[:], in1=g_t[:], op=ALU.mult), ['y_t', 'g_t'], ['zb'])
                        transpose8(zb, 'zb', zT[:], 'zT', 7)
                        for half in range(2):
                            for c in range(8):
                                T(lambda e: e.matmul(psb[half][:], lhsT=zT[:, c, :], rhs=wo[:, c, half * 512:(half + 1) * 512],
                                                     start=(c == 0), stop=(c == 7)), ['zT', 'wo'], [PS[half]])
                        post_sublayer(tt, (0, 1), modb, lnb, wk)
                kb.st = old_st

        def rope_apply(src, skey, H, dst, dkey, ropet, tmp):
            xv = src.rearrange("p (h a s f) -> p h a s f", h=H, a=2, s=2)
            dv = dst.rearrange("p (h a s f) -> p h a s f", h=H, a=2, s=2)
            x1, x2 = xv[:, :, :, 0, :], xv[:, :, :, 1, :]
            d1, d2 = dv[:, :, :, 0, :], dv[:, :, :, 1, :]
            cosb = ropet[:, 0:32].rearrange("p (a f) -> p a f", a=2).unsqueeze(1).to_broadcast([128, H, 2, 16])
            sinb = ropet[:, 32:64].rearrange("p (a f) -> p a f", a=2).unsqueeze(1).to_broadcast([128, H, 2, 16])
            t1 = tmp[0][:, 0:H * 32].rearrange("p (h a f) -> p h a f", h=H, a=2)
            t2 = tmp[1][:, 0:H * 32].rearrange("p (h a f) -> p h a f", h=H, a=2)
            V(lambda e: e.tensor_tensor(out=t1, in0=x1, in1=cosb, op=ALU.mult), [skey, 'ropet'], ['rt1'])
            V(lambda e: e.tensor_tensor(out=t2, in0=x2, in1=sinb, op=ALU.mult), [skey, 'ropet'], ['rt2'])
            V(lambda e: e.tensor_tensor(out=d1, in0=t1, in1=t2, op=ALU.subtract), ['rt1', 'rt2'], [dkey])
            V(lambda e: e.tensor_tensor(out=t1, in0=x2, in1=cosb, op=ALU.mult), [skey, 'ropet'], ['rt1'])
            V(lambda e: e.tensor_tensor(out=t2, in0=x1, in1=sinb, op=ALU.mult), [skey, 'ropet'], ['rt2'])
            V(lambda e: e.tensor_tensor(out=d2, in0=t1, in1=t2, op=ALU.add), ['rt1', 'rt2'], [dkey])

        def attn_sublayer(i):
            j = i // 2
            with kb.phase():
                modb, lnb = setup_mod(i, 0)
                wk = post_work()
                wqkv = kb.sb('wqkv', [128, 8, 1536], BF16)
                for c in range(8):
                    load_cast(wqkv[:, c, :], wqkv_d[j, c * 128:(c + 1) * 128, :], w=['wqkv'])
                wo = kb.sb('wo', [128, 8, D], BF16)
                for c in range(8):
                    load_cast(wo[:, c, :], awo_d[j, c * 128:(c + 1) * 128, :], w=['wo'])
                qnb = kb.sb('qnb', [128, 64])
                knb = kb.sb('knb', [128, 64])
                kb.dma('sp', qnb[:], qn_d[j:j + 1, :].partition_broadcast(128), writes=['qnb'])
                kb.dma('sp', knb[:], kn_d[j:j + 1, :].partition_broadcast(128), writes=['knb'])
                hT = kb.sb('hT', [128, 8, NTOK], BF16)
                kT = kb.sb('kT', [64, 4, 1536], BF16)
                Vx = kb.sb('Vx', [128, 12, 4, 65], BF16)
                htok = kb.sb('htok', [128, D])
                hb = kb.sb('hb', [128, D], BF16)
                ckt = kb.sb('ckt', [128, 256], BF16)
                sq = htok
                ss = kb.sb('ss', [128, 3, 16])
                qf = kb.sb('qf', [128, D])
                qb = kb.sb('qb', [128, D], BF16)
                kf = kb.sb('kf', [128, 256])
                vf = kb.sb('vf', [128, 256])
                kbb = kb.sb('kbb', [128, 256], BF16)
                ropet = kb.sb('ropet', [128, 64])
                rtmp = [kb.sb(f'rtmp{b}', [128, 256]) for b in range(2)]
                qT = kb.sb('qT', [64, 16, 128], BF16)
                PT = [kb.sb(f'PT{b}', [128, 512], BF16) for b in range(2)]
                rc4 = kb.sb('rc4', [128, 4])
                ob = kb.sb('ob', [128, D], BF16)
                oT = kb.sb('oT', [128, 8, 128], BF16)
                V(lambda e: e.memset(Vx[:, :, :, 64:65], 1.0), [], ['Vx'])

                def rms(src_ap_list, keys, H, gb, out_f, okey):
                    off = 0
                    for ap, kk_ in zip(src_ap_list, keys):
                        w_ = ap.shape[-1]
                        A(lambda e: e.activation(out=sq[:, off:off + w_], in_=ap, func=AF.Square), [kk_], ['htok'])
                        off += w_
                    V(lambda e: e.tensor_reduce(out=ss[:, 0, 0:H], in_=sq[:, 0:H * 64].rearrange("p (h d) -> p h d", h=H),
                                                axis=AX.X, op=ALU.add), ['htok'], ['ss'])
                    V(lambda e: e.tensor_scalar(out=ss[:, 0, 0:H], in0=ss[:, 0, 0:H], scalar1=1.0 / 64, scalar2=RMS_EPS,
                                                op0=ALU.mult, op1=ALU.add), ['ss'], ['ss'])
                    A(lambda e: e.activation(out=ss[:, 1, 0:H], in_=ss[:, 0, 0:H], func=AF.Sqrt), ['ss'], ['ss'])
                    V(lambda e: e.reciprocal(out=ss[:, 2, 0:H], in_=ss[:, 1, 0:H]), ['ss'], ['ss'])
                    off = 0
                    for ap, kk_ in zip(src_ap_list, keys):
                        w_ = ap.shape[-1]
                        hh = w_ // 64
                        h0 = off // 64
                        V(lambda e: e.tensor_tensor(out=out_f[:, off:off + w_].rearrange("p (h d) -> p h d", h=hh),
                                                    in0=ap.rearrange("p (h d) -> p h d", h=hh),
                                                    in1=ss[:, 2, h0:h0 + hh].unsqueeze(2).to_broadcast([128, hh, 64]),
                                                    op=ALU.mult), [kk_, 'ss'], [okey])
                        off += w_
                    V(lambda e: e.tensor_tensor(out=out_f[:, 0:H * 64].rearrange("p (h d) -> p h d", h=H),
                                                in0=out_f[:, 0:H * 64].rearrange("p (h d) -> p h d", h=H),
                                                in1=gb[:].unsqueeze(1).to_broadcast([128, H, 64]), op=ALU.mult),
                      [okey, 'qnb', 'knb'], [okey])

                for (sq_i, tiles) in SEQS:
                    sample = (sq_i == 2)
                    kc0 = 4 if sample else 0
                    nch = kc0 + len(tiles)
                    if sample:
                        for kc in range(4):
                            load_cast(ckt[:], ck_d[j, kc * 128:(kc + 1) * 128, :], w=['ckt'])
                            pv = psbf(7)
                            for kv in range(4):
                                T(lambda e: e.transpose(out=pv[0:64, kv * 128:(kv + 1) * 128], in_=ckt[:, kv * 64:(kv + 1) * 64],
                                                        identity=identb[:]), ['ckt', 'identb'], [PS[7]])
                            A(lambda e: e.copy(out=kT[:, :, kc * 128:(kc + 1) * 128],
                                               in_=pv[0:64, 0:512].rearrange("p (a t) -> p a t", a=4)), [PS[7]], ['kT'])
                            load_cast(Vx[:, kc, :, 0:64], cv_d[j, kc * 128:(kc + 1) * 128, :].rearrange("p (a d) -> p a d", a=4),
                                      w=['Vx'])
                    for lt, tt in enumerate(tiles):
                        kc = kc0 + lt
                        make_h(tt, modb, htok[:], 'htok')
                        A(lambda e: e.copy(out=hb[:], in_=htok[:]), ['htok'], ['hb'])
                        transpose8(hb, 'hb', hT[:, :, tt * 128:(tt + 1) * 128], 'hT', 7)
                        for c in range(8):
                            T(lambda e: e.matmul(psb[6][:], lhsT=hT[:, c, tt * 128:(tt + 1) * 128], rhs=wqkv[:, c, 1024:1536],
                                                 start=(c == 0), stop=(c == 7)), ['hT', 'wqkv'], [PS[6]])
                        rms([psb[6][:, 0:256]], [PS[6]], 4, knb, kf, 'kf')
                        if sample:
                            kb.dma('sp', ropet[:], rope_d[(tt - 4) * 128:(tt - 3) * 128, :], writes=['ropet'])
                            rope_apply(kf[:], 'kf', 4, kbb[:], 'kbb', ropet, rtmp)
                        else:
                            kb.dma('sp', nk_d[sq_i, j, lt * 128:(lt + 1) * 128, :], kf[:], reads=['kf'], writes=['nk'])
                            A(lambda e: e.copy(out=vf[:], in_=psb[6][:, 256:512]), [PS[6]], ['vf'])
                            kb.dma('sp', nv_d[sq_i, j, lt * 128:(lt + 1) * 128, :], vf[:], reads=['vf'], writes=['nv'])
                            V(lambda e: e.tensor_copy(out=kbb[:], in_=kf[:]), ['kf'], ['kbb'])
                        A(lambda e: e.copy(out=Vx[:, kc, :, 0:64], in_=psb[6][:, 256:512].rearrange("p (a d) -> p a d", a=4)),
                          [PS[6]], ['Vx'])
                        pv = psbf(7)
                        for kv in range(4):
                            T(lambda e: e.transpose(out=pv[0:64, kv * 128:(kv + 1) * 128], in_=kbb[:, kv * 64:(kv + 1) * 64],
                                                    identity=identb[:]), ['kbb', 'identb'], [PS[7]])
                        A(lambda e: e.copy(out=kT[:, :, kc * 128:(kc + 1) * 128],
                                           in_=pv[0:64, 0:512].rearrange("p (a t) -> p a t", a=4)), [PS[7]], ['kT'])
                    nsc = 0
                    for lt, tt in enumerate(tiles):
                        for half in range(2):
                            for c in range(8):
                                T(lambda e: e.matmul(psb[half][:], lhsT=hT[:, c, tt * 128:(tt + 1) * 128],
                                                     rhs=wqkv[:, c, half * 512:(half + 1) * 512],
                                                     start=(c == 0), stop=(c == 7)), ['hT', 'wqkv'], [PS[half]])
                        rms([psb[0][:], psb[1][:]], [PS[0], PS[1]], 16, qnb, qf, 'qf')
                        if sample:
                            kb.dma('sp', ropet[:], rope_d[(tt - 4) * 128:(tt - 3) * 128, :], writes=['ropet'])
                            for hh in range(2):
                                rope_apply(qf[:, hh * 512:(hh + 1) * 512], 'qf', 8, qb[:, hh * 512:(hh + 1) * 512], 'qb',
                                           ropet, rtmp)
                        else:
                            V(lambda e: e.tensor_copy(out=qb[:], in_=qf[:]), ['qf'], ['qb'])
                        for hh in range(2):
                            pv = psbf(2 + hh)
                            for h8 in range(8):
                                h = hh * 8 + h8
                                T(lambda e: e.transpose(out=pv[0:64, h8 * 128:(h8 + 1) * 128], in_=qb[:, h * 64:(h + 1) * 64],
                                                        identity=identb[:]), ['qb', 'identb'], [PS[2 + hh]])
                            A(lambda e: e.copy(out=qT[:, hh * 8:(hh + 1) * 8, :],
                                               in_=pv[0:64, :].rearrange("p (a t) -> p a t", a=8)), [PS[2 + hh]], ['qT'])
                        work = [(kv, kc) for kv in range(4) for kc in range(nch)]

                        def emit_score(n):
                            kv, kc = work[n]
                            sbk = 2 + (n % 2)
                            T(lambda e: e.matmul(psb[sbk][:], lhsT=kT[:, kv, kc * 128:(kc + 1) * 128],
                                                 rhs=qT[:, kv * 4:(kv + 1) * 4, :], start=True, stop=True),
                              ['kT', 'qT'], [PS[sbk]])

                        emit_score(0)
                        for n, (kv, kc) in enumerate(work):
                            if n + 1 < len(work):
                                emit_score(n + 1)
                            sbk = 2 + (n % 2)
                            pt = PT[n % 2]
                            ptk = f'PT{n % 2}'
                            A(lambda e: e.activation(out=pt[:], in_=psb[sbk][:], func=AF.Exp, scale=ATTN_SCALE),
                              [PS[sbk]], [ptk])
                            for g in range(4):
                                T(lambda e: e.matmul(psb[4 + kv][:, g * 65:(g + 1) * 65], lhsT=pt[:, g * 128:(g + 1) * 128],
                                                     rhs=Vx[:, kc, kv, :], start=(kc == 0), stop=(kc == nch - 1)),
                                  [ptk, 'Vx'], [PS[4 + kv]])
                            if kc == nch - 1:
                                po = psb[4 + kv][:, 0:260].rearrange("p (g d) -> p g d", g=4)
                                V(lambda e: e.reciprocal(out=rc4[:], in_=po[:, :, 64]), [PS[4 + kv]], ['rc4'])
                                V(lambda e: e.tensor_tensor(out=ob[:, kv * 256:(kv + 1) * 256].rearrange("p (g d) -> p g d", g=4),
                                                            in0=po[:, :, 0:64], in1=rc4[:].unsqueeze(2).to_broadcast([128, 4, 64]),
                                                            op=ALU.mult), [PS[4 + kv], 'rc4'], ['ob'])
                        transpose8(ob, 'ob', oT[:], 'oT', 2)
                        for half in range(2):
                            for c in range(8):
                                T(lambda e: e.matmul(psb[half][:], lhsT=oT[:, c, :], rhs=wo[:, c, half * 512:(half + 1) * 512],
                                                     start=(c == 0), stop=(c == 7)), ['oT', 'wo'], [PS[half]])
                        post_sublayer(tt, (0, 1), modb, lnb, wk)

        done_ada = set()
        for (i, which) in subs:
            if i not in done_ada:
                if (i, 1) in subs:
                    peer_convert(i)
                adaln(i)
                done_ada.add(i)
            if which == 1:
                peer_sublayer(i)
            else:
                (rwkv_sublayer if i % 2 == 0 else attn_sublayer)(i)

        for tt in range(NT):
            kb.dma('sp', y_d[tt * 128:(tt + 1) * 128, :], x_res[:, tt, :], reads=[f'x{tt}'], writes=['y'])
        kb.barrier()
        print("ninst", kb.ninst, flush=True)
        _LAST_KB['kb'] = kb
    return nc


_LAST_KB = {}


def _consts():
    c = np.zeros((128, NCST), np.float32)
    s = np.arange(128)[:, None]
    t = np.arange(128)[None, :]
    c[:, C_ID:C_ID + 128] = np.eye(128)
    c[:, C_TRIF:C_TRIF + 128] = (s <= t)
    c[:, C_TRIB:C_TRIB + 128] = (s >= t)
    strictF, inclF = (s < t), (s <= t)
    strictB, inclB = (s > t), (s >= t)
    c[:, C_M4F:C_M4F + 512] = np.concatenate([strictF, inclF, strictF, inclF], 1)
    c[:, C_M4B:C_M4B + 512] = np.concatenate([strictB, inclB, strictB, inclB], 1)
    idx = np.arange(128)
    def bm(b):
        return (idx[:, None] // b == idx[None, :] // b)
    blocks = [bm(16), bm(32) & ~bm(16), bm(64) & ~bm(32), ~bm(64)]
    c[:, C_MU:C_MU + 512] = np.concatenate([strictF & b_ for b_ in blocks], 1)
    c[:, C_ML:C_ML + 512] = np.concatenate([strictB & b_ for b_ in blocks], 1)
    c[:, C_IOTA:C_IOTA + 16] = np.arange(16)[None, :]
    c[:, C_ONES:C_ONES + 128] = 1.0
    sel2 = np.zeros((2, 256), np.float32)
    sel2[0, :128] = 1.0
    sel2[1, 128:] = 1.0
    n = np.arange(1024)
    row = (n // 64).astype(np.float32)
    col = (n % 64).astype(np.float32)
    freqs = (10000.0 ** (-np.arange(16, dtype=np.float32) / 16)).astype(np.float32)
    ang = np.stack([row[:, None] * freqs, col[:, None] * freqs], axis=1).astype(np.float32)
    rope = np.concatenate([np.cos(ang).reshape(1024, 32), np.sin(ang).reshape(1024, 32)], 1).astype(np.float32)
    return c, sel2, rope


def prep_inputs(inputs, x_override=None):
    f = lambda a: np.ascontiguousarray(np.asarray(a, dtype=np.float32))
    I = {k: np.asarray(v) for k, v in inputs.items()}
    cst, sel2, rope = _consts()
    shared = {
        "ada_w": f(I['ada_w']), "ada_b": f(I['ada_b']), "ln_g": f(I['ln_g']), "ln_b": f(I['ln_b']),
        "muT": f(I['rwkv_mu'].reshape(2, 6, 8, 128).transpose(0, 3, 1, 2)),
        "wrkv": f(I['rwkv_wrkv']), "rwo": f(I['rwkv_wo']), "w0": f(I['rwkv_w0']),
        "w1c": f(I['rwkv_w1'].transpose(0, 2, 1, 3).reshape(2, 1024, 128)),
        "w2c": f(I['rwkv_w2'].reshape(2, 128, 1024)),
        "a0": f(I['rwkv_a0']),
        "a1c": f(I['rwkv_a1'].transpose(0, 2, 1, 3).reshape(2, 1024, 128)),
        "a2c": f(I['rwkv_a2'].reshape(2, 128, 1024)),
        "g1": f(I['rwkv_g1']), "g2": f(I['rwkv_g2']),
        "rkk": f(I['rwkv_kk']), "rka": f(I['rwkv_ka']), "rrk": f(I['rwkv_rk'].reshape(2, 1024)),
        "lnxg": f(I['rwkv_lnx_g']), "lnxb": f(I['rwkv_lnx_b']),
        "wqkv": f(I['attn_wqkv']), "awo": f(I['attn_wo']), "qn": f(I['attn_qn']), "kn": f(I['attn_kn']),
        "pwq": f(I['peer_wq']), "pkT": f(I['peer_keys'].transpose(0, 1, 3, 2)),
        "cst": cst, "sel2": sel2, "rope": rope,
    }
    for i in range(4):
        shared[f"pu{i}"] = f(I['peer_u'][i])
        shared[f"pv{i}"] = f(I['peer_v'][i])
    in_maps = []
    for c in range(8):
        m = dict(shared)
        if x_override is not None:
            xp, xs = x_override
        else:
            xp, xs = I['x_prompt'], I['x_sample']
        m["xin"] = f(np.concatenate([xp[2 * c], xp[2 * c + 1], xs[c]], 0))
        m["cond"] = f(np.stack([I['c_ctx'], I['c'][c]], 0))
        m["st_in"] = f(I['state_rwkv'][c])
        m["ck"] = f(I['cache_k'][c].reshape(2, 512, 256))
        m["cv"] = f(I['cache_v'][c].reshape(2, 512, 256))
        in_maps.append(m)
    return in_maps


def assemble(results):
    yp = np.zeros((16, 256, 1024), np.float32)
    ys = np.zeros((8, 1024, 1024), np.float32)
    nst = np.zeros((16, 2, 2, 16, 64, 64), np.float32)
    nk = np.zeros((16, 2, 256, 4, 64), np.float32)
    nv = np.zeros((16, 2, 256, 4, 64), np.float32)
    for c, r in enumerate(results):
        y = r["y"]
        yp[2 * c] = y[0:256]
        yp[2 * c + 1] = y[256:512]
        ys[c] = y[512:]
        nst[2 * c:2 * c + 2] = r["nst"]
        nk[2 * c:2 * c + 2] = r["nk"].reshape(2, 2, 256, 4, 64)
        nv[2 * c:2 * c + 2] = r["nv"].reshape(2, 2, 256, 4, 64)
    return yp, ys, nst, nk, nv


_NC_CACHE = {}


def kernel(**inputs):
    if 'nc' not in _NC_CACHE:
        _NC_CACHE['nc'] = build()
    nc = _NC_CACHE['nc']
    in_maps = prep_inputs(inputs)
    res = run_bass_kernel_spmd(nc, in_maps, core_ids=list(range(8)))
    return assemble(res.results)
```
